# Optimizing a Trainium2 kernel written in Bass

```python
import math
import jax
import jax.numpy as jnp
from jax import lax
import numpy as np

D_MODEL = 1024
BATCH = 8
SEQ = 2048
DEPTH = 2
DEC_BATCH = 16
DEC_SEQ = 32
PAST_LEN = 1024

CHUNK = 64
Q_BLOCK = 128
EPS = 1e-6
ML_HEADS = 4
ML_DH = 3 * D_MODEL // (8 * ML_HEADS)
ML_W = ML_HEADS * ML_DH
SB_DH = 64
SB_HEADS = 3 * D_MODEL // (8 * SB_DH)
SB_W = SB_HEADS * SB_DH
SSM_W = D_MODEL // 4
SSM_CH = 16
SSM_GROUPS = SSM_W // SSM_CH
SSM_P = 64
MIX_W = ML_W + SB_W + SSM_W
SPLIT_SIZES = (ML_W, ML_W, ML_W, ML_W, ML_HEADS, ML_HEADS, SB_W, SB_W, SB_W, SSM_W)
N_IN = 4 * ML_W + 2 * ML_HEADS + 3 * SB_W + SSM_W
N_MEM = 256
X_HEADS = 4
X_DH = D_MODEL // X_HEADS
D_FF = 128 * math.ceil(8 * D_MODEL / (3 * 128))
CONV_W = 3

kernel_name = 'hybrid_mlstm_stickbreak_s5_stream_step'


def head_rmsnorm(x, g, n_heads):
    sh = x.shape
    xh = x.astype(jnp.float32).reshape(sh[:-1] + (n_heads, sh[-1] // n_heads))
    y = xh * lax.rsqrt(jnp.mean(xh * xh, axis=-1, keepdims=True) + EPS)
    return (y.reshape(sh) * g.astype(jnp.float32)).astype(x.dtype)


def rmsnorm(x, g):
    return head_rmsnorm(x, g, 1)


def split_cols(z):
    out, off = [], 0
    for w in SPLIT_SIZES:
        out.append(z[..., off:off + w])
        off += w
    return out


def mlstm(q, k, v, i_pre, f_pre, c0, n0, m0):
    f32 = jnp.float32
    bsz, t_len, nh, d = q.shape
    blk = min(CHUNK, t_len)
    nc = t_len // blk
    q = q.astype(f32)
    k = k.astype(f32) * (d ** -0.5)
    v = v.astype(f32)
    logi = i_pre.astype(f32)
    logf = jax.nn.log_sigmoid(f_pre.astype(f32))

    def to_chunks(a):
        a = a.reshape((bsz, nc, blk) + a.shape[2:])
        return jnp.swapaxes(jnp.moveaxis(a, 1, 0), 2, 3)

    xs = (to_chunks(q), to_chunks(k), to_chunks(v), to_chunks(logi), to_chunks(logf))
    causal = jnp.tril(jnp.ones((blk, blk), dtype=bool))

    def step(carry, xc):
        c, n, m = carry
        qc, kc, vc, li, lf = xc
        b = jnp.cumsum(lf, axis=-1)
        dmat = b[..., :, None] - b[..., None, :] + li[..., None, :]
        dmat = jnp.where(causal, dmat, -jnp.inf)
        inter = b + m[..., None]
        m_t = jnp.maximum(inter, jnp.max(dmat, axis=-1))
        w_intra = jnp.exp(dmat - m_t[..., None])
        w_inter = jnp.exp(inter - m_t)
        s = jnp.einsum('bhtd,bhsd->bhts', qc, kc) * w_intra
        num = jnp.einsum('bhts,bhsd->bhtd', s, vc) + w_inter[..., None] * jnp.einsum('bhvk,bhtk->bhtv', c, qc)
        den = jnp.sum(s, axis=-1) + w_inter * jnp.einsum('bhk,bhtk->bht', n, qc)
        h = num / jnp.maximum(jnp.abs(den), jnp.exp(-m_t))[..., None]
        wg = w_intra[..., -1, :]
        decay = w_inter[..., -1]
        c_new = decay[..., None, None] * c + jnp.einsum('bhsv,bhsk->bhvk', vc * wg[..., None], kc)
        n_new = decay[..., None] * n + jnp.einsum('bhs,bhsk->bhk', wg, kc)
        return (c_new, n_new, m_t[..., -1]), h

    (c, n, m), hs = lax.scan(step, (c0.astype(f32), n0.astype(f32), m0.astype(f32)), xs)
    hs = jnp.transpose(hs, (1, 0, 3, 2, 4)).reshape(bsz, t_len, nh * d)
    return hs, c, n, m


def stick_breaking_block(q, k, v, q_pos, k_pos):
    f32 = jnp.float32
    z = jnp.einsum('bthd,bshd->bhts', q.astype(f32), k.astype(f32)) * (q.shape[-1] ** -0.5)
    vis = k_pos[None, :] < q_pos[:, None]
    log_1m = jnp.where(vis, jax.nn.log_sigmoid(-z), 0.0)
    after = lax.cumsum(log_1m, axis=3, reverse=True) - log_1m
    a = jnp.where(vis, jnp.exp(jax.nn.log_sigmoid(z) + after), 0.0)
    return jnp.einsum('bhts,bshd->bthd', a, v.astype(f32))


def stick_breaking_prompt(q, k, v):
    t_len = q.shape[1]
    pos = jnp.arange(t_len)
    outs = []
    for qb in range(t_len // Q_BLOCK):
        lo, hi = qb * Q_BLOCK, (qb + 1) * Q_BLOCK
        outs.append(stick_breaking_block(q[:, lo:hi], k[:, :hi], v[:, :hi], pos[lo:hi], pos[:hi]))
    return jnp.concatenate(outs, axis=1)


def s5(u, prm, h0_re, h0_im):
    f32 = jnp.float32
    bsz, t_len, _ = u.shape
    uf = u.astype(f32)
    ug = uf.reshape(bsz, t_len, SSM_GROUPS, SSM_CH)
    a_re = prm['ssm_a_re'].astype(f32)
    a_im = prm['ssm_a_im'].astype(f32)
    dt = jnp.exp(prm['ssm_log_dt'].astype(f32))
    mag = jnp.exp(a_re * dt)
    ab_re = mag * jnp.cos(a_im * dt)
    ab_im = mag * jnp.sin(a_im * dt)
    den = a_re * a_re + a_im * a_im
    nr = ab_re - 1.0
    zr = (nr * a_re + ab_im * a_im) / den
    zi = (ab_im * a_re - nr * a_im) / den
    b_re = prm['ssm_b_re'].astype(f32)
    b_im = prm['ssm_b_im'].astype(f32)
    bb_re = zr[..., None] * b_re - zi[..., None] * b_im
    bb_im = zr[..., None] * b_im + zi[..., None] * b_re
    x_re = jnp.einsum('gpc,btgc->btgp', bb_re, ug)
    x_im = jnp.einsum('gpc,btgc->btgp', bb_im, ug)
    h0_re = h0_re.astype(f32)
    h0_im = h0_im.astype(f32)
    x_re = x_re.at[:, 0].add(ab_re * h0_re - ab_im * h0_im)
    x_im = x_im.at[:, 0].add(ab_re * h0_im + ab_im * h0_re)
    ar = jnp.broadcast_to(ab_re, x_re.shape)
    ai = jnp.broadcast_to(ab_im, x_re.shape)

    def combine(e1, e2):
        a1r, a1i, b1r, b1i = e1
        a2r, a2i, b2r, b2i = e2
        return (a2r * a1r - a2i * a1i, a2r * a1i + a2i * a1r,
                a2r * b1r - a2i * b1i + b2r, a2r * b1i + a2i * b1r + b2i)

    _, _, h_re, h_im = lax.associative_scan(combine, (ar, ai, x_re, x_im), axis=1)
    y = (jnp.einsum('gcp,btgp->btgc', prm['ssm_c_re'].astype(f32), h_re)
         - jnp.einsum('gcp,btgp->btgc', prm['ssm_c_im'].astype(f32), h_im))
    y = y.reshape(bsz, t_len, SSM_W) + prm['ssm_d'].astype(f32) * uf
    return y, h_re[:, -1], h_im[:, -1]


def cross_attn(h, mem_k, mem_v, w_q, w_o):
    bsz, t_len, _ = h.shape
    q = (h @ w_q).reshape(bsz, t_len, X_HEADS, X_DH)
    s = jnp.einsum('bthd,bmhd->bhtm', q.astype(jnp.float32), mem_k.astype(jnp.float32)) * (X_DH ** -0.5)
    p = jax.nn.softmax(s, axis=-1)
    o = jnp.einsum('bhtm,bmhd->bthd', p, mem_v.astype(jnp.float32)).reshape(bsz, t_len, D_MODEL)
    return o.astype(h.dtype) @ w_o


def conv_ffn(h, prm, prev):
    t_len = h.shape[1]
    a = h @ prm['w_ffn_a']
    ext = jnp.concatenate([prev.astype(a.dtype), a], axis=1)
    w = prm['ffn_conv_w']
    c = prm['ffn_conv_b'] + ext[:, 0:t_len] * w[0]
    for j in range(1, CONV_W):
        c = c + ext[:, j:j + t_len] * w[j]
    y = (jax.nn.silu(c) * (h @ prm['w_ffn_b'])) @ prm['w_ffn_down']
    return y, ext[:, ext.shape[1] - (CONV_W - 1):]


def layer(x, mem_k, mem_v, past_k, past_v, c0, n0, m0, s0_re, s0_im, conv_prev, prm):
    bsz, t_len, _ = x.shape
    h = rmsnorm(x, prm['ln_mix_g'])
    z = h @ prm['w_in'] + prm['b_in']
    q_a, k_a, v_a, o_a, i_a, f_a, q_b, k_b, v_b, u_c = split_cols(z)
    ml = (bsz, t_len, ML_HEADS, ML_DH)
    h_a, c_new, n_new, m_new = mlstm(q_a.reshape(ml), k_a.reshape(ml), v_a.reshape(ml), i_a, f_a, c0, n0, m0)
    y_a = head_rmsnorm(jax.nn.sigmoid(o_a) * h_a.astype(x.dtype), prm['gn_a_g'], ML_HEADS)
    sb = (bsz, t_len, SB_HEADS, SB_DH)
    q_b, k_b, v_b = q_b.reshape(sb), k_b.reshape(sb), v_b.reshape(sb)
    if past_k is None:
        o_b = stick_breaking_prompt(q_b, k_b, v_b)
    else:
        p_len = past_k.shape[1]
        k_all = jnp.concatenate([past_k.astype(k_b.dtype), k_b], axis=1)
        v_all = jnp.concatenate([past_v.astype(v_b.dtype), v_b], axis=1)
        o_b = stick_breaking_block(q_b, k_all, v_all, p_len + jnp.arange(t_len), jnp.arange(p_len + t_len))
    y_b = head_rmsnorm(o_b.reshape(bsz, t_len, SB_W), prm['gn_b_g'], SB_HEADS)
    y_s, s_re, s_im = s5(u_c, prm, s0_re, s0_im)
    g = jax.nn.gelu(y_s)
    y_c = rmsnorm(g * jax.nn.sigmoid(g @ prm['w_glu'].astype(jnp.float32) + prm['b_glu'].astype(jnp.float32)), prm['gn_c_g'])
    y_mix = jnp.concatenate([y_a, y_b.astype(x.dtype), y_c.astype(x.dtype)], axis=-1)
    x = x + y_mix @ prm['w_out']
    x = x + cross_attn(rmsnorm(x, prm['ln_x_g']), mem_k, mem_v, prm['w_xq'], prm['w_xo'])
    f, conv_new = conv_ffn(rmsnorm(x, prm['ln_ffn_g']), prm, conv_prev)
    x = x + f
    return x, k_b, v_b, c_new, n_new, m_new, s_re, s_im, conv_new


def setup_inputs(seed: int = 0) -> dict:
    key = jax.random.key(seed)
    ks = iter(jax.random.split(key, 64))
    f32 = jnp.float32

    def nrm(shape, scale=1.0):
        return scale * jax.random.normal(next(ks), shape, f32)

    def gain(shape):
        return 1.0 + 0.02 * nrm(shape)

    L = DEPTH
    f_off = 4 * ML_W + ML_HEADS
    b_in = 0.02 * nrm((L, N_IN))
    b_in = b_in.at[:, f_off:f_off + ML_HEADS].add(jnp.linspace(3.0, 6.0, ML_HEADS))
    a_im = jnp.pi * jnp.arange(SSM_P, dtype=f32)
    return {
        'x_prompt': nrm((BATCH, SEQ, D_MODEL)),
        'x_sample': nrm((DEC_BATCH, DEC_SEQ, D_MODEL)),
        'cache_sb_k': nrm((L, DEC_BATCH, PAST_LEN, SB_HEADS, SB_DH)),
        'cache_sb_v': nrm((L, DEC_BATCH, PAST_LEN, SB_HEADS, SB_DH)),
        'state_mlstm_c': nrm((L, DEC_BATCH, ML_HEADS, ML_DH, ML_DH), 0.3),
        'state_mlstm_n': nrm((L, DEC_BATCH, ML_HEADS, ML_DH), 0.3),
        'state_mlstm_m': nrm((L, DEC_BATCH, ML_HEADS)),
        'state_ssm_re': nrm((L, DEC_BATCH, SSM_GROUPS, SSM_P), 0.5),
        'state_ssm_im': nrm((L, DEC_BATCH, SSM_GROUPS, SSM_P), 0.5),
        'state_ffn_conv': nrm((L, DEC_BATCH, CONV_W - 1, D_FF)),
        'cache_mem_k': nrm((L, DEC_BATCH, N_MEM, X_HEADS, X_DH)),
        'cache_mem_v': nrm((L, DEC_BATCH, N_MEM, X_HEADS, X_DH)),
        'mem_prompt': nrm((BATCH, N_MEM, D_MODEL)),
        'ln_mix_g': gain((L, D_MODEL)),
        'w_in': nrm((L, D_MODEL, N_IN), D_MODEL ** -0.5),
        'b_in': b_in,
        'gn_a_g': gain((L, ML_W)),
        'gn_b_g': gain((L, SB_W)),
        'gn_c_g': gain((L, SSM_W)),
        'ssm_a_re': -0.5 + 0.01 * nrm((L, SSM_GROUPS, SSM_P)),
        'ssm_a_im': a_im + 0.01 * nrm((L, SSM_GROUPS, SSM_P)),
        'ssm_log_dt': jax.random.uniform(next(ks), (L, SSM_GROUPS, SSM_P), f32, minval=math.log(1e-3), maxval=math.log(1e-1)),
        'ssm_b_re': nrm((L, SSM_GROUPS, SSM_P, SSM_CH), (2 * SSM_CH) ** -0.5),
        'ssm_b_im': nrm((L, SSM_GROUPS, SSM_P, SSM_CH), (2 * SSM_CH) ** -0.5),
        'ssm_c_re': nrm((L, SSM_GROUPS, SSM_CH, SSM_P), (2 * SSM_P) ** -0.5),
        'ssm_c_im': nrm((L, SSM_GROUPS, SSM_CH, SSM_P), (2 * SSM_P) ** -0.5),
        'ssm_d': nrm((L, SSM_W)),
        'w_glu': nrm((L, SSM_W, SSM_W), SSM_W ** -0.5),
        'b_glu': 0.02 * nrm((L, SSM_W)),
        'w_out': nrm((L, MIX_W, D_MODEL), MIX_W ** -0.5),
        'ln_x_g': gain((L, D_MODEL)),
        'ln_mem_g': gain((L, D_MODEL)),
        'w_xq': nrm((L, D_MODEL, D_MODEL), D_MODEL ** -0.5),
        'w_xk': nrm((L, D_MODEL, D_MODEL), D_MODEL ** -0.5),
        'w_xv': nrm((L, D_MODEL, D_MODEL), D_MODEL ** -0.5),
        'w_xo': nrm((L, D_MODEL, D_MODEL), D_MODEL ** -0.5),
        'ln_ffn_g': gain((L, D_MODEL)),
        'w_ffn_a': nrm((L, D_MODEL, D_FF), D_MODEL ** -0.5),
        'w_ffn_b': nrm((L, D_MODEL, D_FF), D_MODEL ** -0.5),
        'ffn_conv_w': nrm((L, CONV_W, D_FF), CONV_W ** -0.5),
        'ffn_conv_b': 0.02 * nrm((L, D_FF)),
        'w_ffn_down': nrm((L, D_FF, D_MODEL), D_FF ** -0.5),
        'ln_f_g': gain((D_MODEL,)),
    }


def reference(x_prompt, x_sample, cache_sb_k, cache_sb_v, state_mlstm_c, state_mlstm_n, state_mlstm_m,
              state_ssm_re, state_ssm_im, state_ffn_conv, cache_mem_k, cache_mem_v, mem_prompt,
              ln_mix_g, w_in, b_in, gn_a_g, gn_b_g, gn_c_g, ssm_a_re, ssm_a_im, ssm_log_dt,
              ssm_b_re, ssm_b_im, ssm_c_re, ssm_c_im, ssm_d, w_glu, b_glu, w_out,
              ln_x_g, ln_mem_g, w_xq, w_xk, w_xv, w_xo,
              ln_ffn_g, w_ffn_a, w_ffn_b, ffn_conv_w, ffn_conv_b, w_ffn_down, ln_f_g):
    f32 = jnp.float32
    bp = x_prompt.shape[0]
    n_mem = mem_prompt.shape[1]
    xp, xs = x_prompt, x_sample
    p_states, s_states = [], []
    for l in range(DEPTH):
        prm = {'ln_mix_g': ln_mix_g[l], 'w_in': w_in[l], 'b_in': b_in[l],
               'gn_a_g': gn_a_g[l], 'gn_b_g': gn_b_g[l], 'gn_c_g': gn_c_g[l],
               'ssm_a_re': ssm_a_re[l], 'ssm_a_im': ssm_a_im[l], 'ssm_log_dt': ssm_log_dt[l],
               'ssm_b_re': ssm_b_re[l], 'ssm_b_im': ssm_b_im[l], 'ssm_c_re': ssm_c_re[l], 'ssm_c_im': ssm_c_im[l],
               'ssm_d': ssm_d[l], 'w_glu': w_glu[l], 'b_glu': b_glu[l], 'w_out': w_out[l],
               'ln_x_g': ln_x_g[l], 'w_xq': w_xq[l], 'w_xo': w_xo[l],
               'ln_ffn_g': ln_ffn_g[l], 'w_ffn_a': w_ffn_a[l], 'w_ffn_b': w_ffn_b[l],
               'ffn_conv_w': ffn_conv_w[l], 'ffn_conv_b': ffn_conv_b[l], 'w_ffn_down': w_ffn_down[l]}
        mn = rmsnorm(mem_prompt, ln_mem_g[l])
        mem_k = (mn @ w_xk[l]).reshape(bp, n_mem, X_HEADS, X_DH)
        mem_v = (mn @ w_xv[l]).reshape(bp, n_mem, X_HEADS, X_DH)
        xp, pk, pv, pc, pn, pm, pre, pim, pconv = layer(
            xp, mem_k, mem_v, None, None,
            jnp.zeros((bp, ML_HEADS, ML_DH, ML_DH), f32), jnp.zeros((bp, ML_HEADS, ML_DH), f32),
            jnp.zeros((bp, ML_HEADS), f32),
            jnp.zeros((bp, SSM_GROUPS, SSM_P), f32), jnp.zeros((bp, SSM_GROUPS, SSM_P), f32),
            jnp.zeros((bp, CONV_W - 1, D_FF), xp.dtype), prm)
        p_states.append((pk, pv, pc, pn, pm, pre, pim, pconv, mem_k, mem_v))
        xs, sk, sv, sc, sn, sm, sre, sim, sconv = layer(
            xs, cache_mem_k[l], cache_mem_v[l], cache_sb_k[l], cache_sb_v[l],
            state_mlstm_c[l], state_mlstm_n[l], state_mlstm_m[l],
            state_ssm_re[l], state_ssm_im[l], state_ffn_conv[l], prm)
        s_states.append((sk, sv, sc, sn, sm, sre, sim, sconv))
    (p_sb_k, p_sb_v, p_mlstm_c, p_mlstm_n, p_mlstm_m, p_ssm_re, p_ssm_im, p_ffn_conv,
     p_mem_k, p_mem_v) = [jnp.stack(a) for a in zip(*p_states)]
    (s_sb_k, s_sb_v, s_mlstm_c, s_mlstm_n, s_mlstm_m, s_ssm_re, s_ssm_im,
     s_ffn_conv) = [jnp.stack(a) for a in zip(*s_states)]
    y_prompt = rmsnorm(xp, ln_f_g)
    y_sample = rmsnorm(xs, ln_f_g)
    return (y_prompt, y_sample,
            p_sb_k, p_sb_v, p_mlstm_c, p_mlstm_n, p_mlstm_m, p_ssm_re, p_ssm_im, p_ffn_conv, p_mem_k, p_mem_v,
            s_sb_k, s_sb_v, s_mlstm_c, s_mlstm_n, s_mlstm_m, s_ssm_re, s_ssm_im, s_ffn_conv)
```

```python
import contextlib
import numpy as np
import concourse.bass as bass
import concourse.mybir as mybir
from concourse.bass_utils import run_bass_kernel_spmd

F32 = mybir.dt.float32
BF16 = mybir.dt.bfloat16
AF = mybir.ActivationFunctionType
ALU = mybir.AluOpType
AX = mybir.AxisListType

NCORES = 8
L = 2
D = 1024
TP = 2048
TSQ = 32
NT = TP + 2 * TSQ
EPS = 1e-6
N_IN = 2952
DFF = 2816
NF = 22
TB = [(0, 512), (512, 512), (1024, 512), (1536, 512), (2048, 64)]
TMB = [(128 * i, 128) for i in range(16)] + [(2048, 32), (2080, 32)]
NDS = 40
NEG = -30000.0

C_ID, C_NTI, C_SBNEG, C_SBM, C_MLNEG, C_BLK, C_MISC = 0, 128, 256, 384, 512, 640, 768
NCST = 832
NCBF = 768


def make_consts():
    c = np.zeros((128, NCST), np.float32)
    p = np.arange(128)
    c[:, C_ID:C_ID + 128] = np.eye(128)
    c[:, C_NTI:C_NTI + 128] = -(p[:, None] >= p[None, :]).astype(np.float32)
    c[:, C_SBNEG:C_SBNEG + 128] = NEG * (p[:, None] >= p[None, :])
    c[:, C_SBM:C_SBM + 128] = (p[:, None] < p[None, :]).astype(np.float32)
    c[:, C_MLNEG:C_MLNEG + 128] = NEG * (p[:, None] > p[None, :])
    c[:, C_BLK:C_BLK + 128] = (p[:, None] // 64 == p[None, :] // 64)
    m = C_MISC
    c[:, m + 0] = (p < 64)
    c[:, m + 1] = (p >= 64)
    c[:, m + 2] = (p < 4)
    c[:, m + 3] = (p >= 4) & (p < 8)
    for h in range(4):
        c[:, m + 4 + h] = (p == h) | (p == 4 + h)
    c[:, m + 8] = ((p // 16) % 2 == 0)
    c[:, m + 10] = -1.0 * ((p >= 4) & (p < 8))
    c[:, m + 16] = (p >= 96)
    for h in range(4):
        c[:, m + 12 + h] = (p == h)
    c[:, m + 9] = ((p // 16) % 2 == 1)
    return c


class KB:
    def __init__(self, nc):
        self.nc = nc
        self.es = contextlib.ExitStack()
        self.eng = {'pe': nc.tensor, 'act': nc.scalar, 'dve': nc.vector, 'pool': nc.gpsimd, 'sp': nc.sync}
        self.sem = {}
        self.cnt = {}
        for e in ('pe', 'act', 'dve', 'pool'):
            self.sem[e] = self.es.enter_context(nc.semaphore('s_' + e))
            self.cnt[e] = 0
        self.dsem = [self.es.enter_context(nc.semaphore('d%d' % i)) for i in range(NDS)]
        self.dcnt = [0] * NDS
        self.dnext = 0
        self.waited = {e: {} for e in self.eng}
        self.lastw = {}
        self.readers = {}
        self.nalloc = 0
        self.dram = {}
        self.bank_rr = 0
        self.muted = False

    def sb(self, stack, shape, dt, name=None):
        self.nalloc += 1
        return stack.enter_context(self.nc.sbuf_tensor('%s_%d' % (name or 't', self.nalloc), list(shape), dt))

    def din(self, name, shape, dt=F32):
        t = self.nc.dram_tensor(name, list(shape), dt, kind="ExternalInput").ap()
        self.dram[name] = t
        return t

    def dout(self, name, shape, dt=F32):
        t = self.nc.dram_tensor(name, list(shape), dt, kind="ExternalOutput").ap()
        self.dram[name] = t
        return t

    def _h(self, s):
        return self.sem[s[1]] if s[0] == 'e' else self.dsem[s[1]]

    def _deps(self, reads, writes):
        need = {}

        def add(sv):
            if sv is None:
                return
            s, v = sv
            if need.get(s, 0) < v:
                need[s] = v
        for r in reads:
            for s, v in self.lastw.get(r, {}).items():
                add((s, v))
        for w in writes:
            for s, v in self.lastw.get(w, {}).items():
                add((s, v))
            for s, v in self.readers.get(w, {}).items():
                add((s, v))
        return need

    def _wait(self, e, need):
        for s, v in need.items():
            if e == 'pe' and s == ('e', 'pe'):
                continue
            if self.waited[e].get(s, 0) >= v:
                continue
            self.eng[e].wait_ge(self._h(s), v)
            self.waited[e][s] = v

    def _commit(self, sv, reads, writes):
        for w in writes:
            self.lastw.setdefault(w, {})[sv[0]] = sv[1]
            self.readers[w] = {}
        for r in reads:
            if r in writes:
                continue
            d = self.readers.setdefault(r, {})
            d[sv[0]] = max(d.get(sv[0], 0), sv[1])

    def op(self, e, fn, reads=(), writes=()):
        if self.muted:
            return
        self._wait(e, self._deps(reads, writes))
        ins = fn(self.eng[e])
        self.cnt[e] += 1
        ins.then_inc(self.sem[e], 1)
        self._commit((('e', e), self.cnt[e]), reads, writes)

    def dma(self, q, out, in_, reads=(), writes=(), **kw):
        if self.muted:
            return
        i = self.dnext
        self.dnext = (i + 1) % NDS
        need = self._deps(reads, writes)
        if self.dcnt[i] > 0:
            need[('d', i)] = max(need.get(('d', i), 0), self.dcnt[i])
        self._wait(q, need)
        ins = self.eng[q].dma_start(out=out, in_=in_, **kw)
        self.dcnt[i] += 16
        ins.then_inc(self.dsem[i], 16)
        self._commit((('d', i), self.dcnt[i]), reads, writes)

    def barrier(self, engines=('pe', 'act', 'dve', 'pool', 'sp')):
        if self.muted:
            return
        need = {}
        for e in ('pe', 'act', 'dve', 'pool'):
            if self.cnt[e] > 0:
                need[('e', e)] = self.cnt[e]
        for i in range(NDS):
            if self.dcnt[i] > 0:
                need[('d', i)] = self.dcnt[i]
        for e in engines:
            n2 = dict(need)
            self._wait(e, n2)

    def mm(self, out, lhsT, rhs, start, stop, reads, writes):
        self.op('pe', lambda e: e.matmul(out, lhsT=lhsT, rhs=rhs, start=start, stop=stop), reads, writes)

    def tr(self, out, in_, ident, reads, writes):
        self.op('pe', lambda e: e.transpose(out=out, in_=in_, identity=ident), reads, writes)

    def act(self, out, in_, func, reads, writes, **kw):
        self.op('act', lambda e: e.activation(out=out, in_=in_, func=func, **kw), reads, writes)

    def tt(self, eng, out, in0, in1, op, reads, writes):
        self.op(eng, lambda e: e.tensor_tensor(out=out, in0=in0, in1=in1, op=op), reads, writes)

    def ts(self, eng, out, in0, s1, s2, op0, op1, reads, writes):
        if s2 is None:
            self.op(eng, lambda e: e.tensor_scalar(out=out, in0=in0, scalar1=s1, scalar2=None, op0=op0), reads, writes)
        else:
            self.op(eng, lambda e: e.tensor_scalar(out=out, in0=in0, scalar1=s1, scalar2=s2, op0=op0, op1=op1), reads, writes)

    def stt(self, out, in0, scalar, in1, op0, op1, reads, writes):
        self.op('dve', lambda e: e.scalar_tensor_tensor(out=out, in0=in0, scalar=scalar, in1=in1, op0=op0, op1=op1), reads, writes)

    def cp(self, eng, out, in_, reads, writes):
        if eng == 'act':
            self.act(out, in_, AF.Copy, reads, writes)
        else:
            self.op(eng, lambda e: e.tensor_copy(out=out, in_=in_), reads, writes)

    def memset(self, eng, ap, val, writes):
        self.op(eng, lambda e: e.memset(ap, val), (), writes)


def bkey(t0):
    return min(t0 // 512, 4)


class Stop(Exception):
    pass


class Prog:
    def chk(self, tag):
        if self.phases is not None and ('stop:' + tag) in self.phases:
            self.k.barrier()
            self.k.muted = True

    def __init__(self, nc, phases=None, dbg=None):
        self.nc = nc
        self.k = KB(nc)
        self.phases = phases
        self.n_dummy = 0
        self.burst_every = 24
        self.dbg = dbg or {}
        self.declare_io()

    def on(self, name):
        return self.phases is None or name in self.phases

    def flag(self, name):
        return self.phases is not None and name in self.phases

    def declare_io(self):
        k = self.k
        i = k.din
        i('xp', [TP, D]); i('xs', [2 * TSQ, D])
        i('csbk', [L, 2, 1024, 384]); i('csbv', [L, 2, 1024, 384])
        i('smc', [L, 2, 4, 96, 96]); i('smn', [L, 2, 4, 96]); i('smm', [L, 2, 4])
        i('ssr', [L, 2, 16, 64]); i('ssi', [L, 2, 16, 64])
        i('sfc', [L, 2, 2, DFF])
        i('cmk', [L, 2, 256, D]); i('cmv', [L, 2, 256, D])
        i('memp', [256, D])
        i('ln_mix_g', [L, D]); i('w_in', [L, D, N_IN]); i('b_in', [L, N_IN])
        i('gn_a_g', [L, 384]); i('gn_b_g', [L, 384]); i('gn_c_g', [L, 256])
        for n in ('ssm_a_re', 'ssm_a_im', 'ssm_log_dt'):
            i(n, [L, 16, 64])
        i('ssm_b_re', [L, 16, 64, 16]); i('ssm_b_im', [L, 16, 64, 16])
        i('ssm_c_re', [L, 16, 16, 64]); i('ssm_c_im', [L, 16, 16, 64])
        i('ssm_d', [L, 256]); i('w_glu', [L, 256, 256]); i('b_glu', [L, 256])
        i('w_out', [L, D, D]); i('ln_x_g', [L, D]); i('ln_mem_g', [L, D])
        for n in ('w_xq', 'w_xk', 'w_xv', 'w_xo'):
            i(n, [L, D, D])
        i('ln_ffn_g', [L, D]); i('w_ffn_a', [L, D, DFF]); i('w_ffn_b', [L, D, DFF])
        i('ffn_conv_w', [L, 3, DFF]); i('ffn_conv_b', [L, DFF]); i('w_ffn_down', [L, DFF, D])
        i('ln_f_g', [D]); i('cst', [128, NCST])
        o = k.dout
        o('y_p', [TP, D]); o('y_s', [2 * TSQ, D])
        o('p_sb_k', [L, TP, 384]); o('p_sb_v', [L, TP, 384])
        o('p_ml_c', [L, 4, 96, 96]); o('p_ml_n', [L, 4, 96]); o('p_ml_m', [L, 4])
        o('p_ssm_re', [L, 16, 64]); o('p_ssm_im', [L, 16, 64])
        o('p_ffn_conv', [L, 2, DFF]); o('p_mem_k', [L, 256, D]); o('p_mem_v', [L, 256, D])
        o('s_sb_k', [L, 2 * TSQ, 384]); o('s_sb_v', [L, 2 * TSQ, 384])
        o('s_ml_c', [L, 2, 4, 96, 96]); o('s_ml_n', [L, 2, 4, 96]); o('s_ml_m', [L, 2, 4])
        o('s_ssm_re', [L, 2, 16, 64]); o('s_ssm_im', [L, 2, 16, 64])
        o('s_ffn_conv', [L, 2, 2, DFF])
        for name, shape in self.dbg.items():
            o(name, shape)

    def setup(self):
        k, nc = self.k, self.nc
        es = k.es
        d = k.dram
        self.xT = k.sb(es, [128, 8, NT], F32, 'xT')
        self.hT = k.sb(es, [128, 8, NT], BF16, 'hT')
        self.cst = k.sb(es, [128, NCST], F32, 'cst')
        self.cbf = k.sb(es, [128, NCBF], BF16, 'cbf')
        self.ones_bf = k.sb(es, [128, 128], BF16, 'ones')
        self.nones_bf = k.sb(es, [128, 128], BF16, 'nones')
        self.ones32 = k.sb(es, [128, 128], F32, 'ones32')
        self.gains = k.sb(es, [128, 7, 8], F32, 'gains')
        self.ps = [es.enter_context(nc.psum_tensor('ps%d' % i, [128, 512], F32)) for i in range(8)]
        k.dma('sp', self.cst[:], d['cst'][:, :], writes=['cst'])
        k.cp('dve', self.cbf[:], self.cst[:, 0:NCBF], ['cst'], ['cbf'])
        k.memset('dve', self.ones_bf[:], 1.0, ['ones'])
        k.memset('dve', self.nones_bf[:], -1.0, ['nones'])
        k.memset('dve', self.ones32[:], 1.0, ['ones32'])
        with nc.allow_non_contiguous_dma(reason="small gain vectors"):
            for j, (nm, l) in enumerate([('ln_mix_g', 0), ('ln_mix_g', 1), ('ln_x_g', 0), ('ln_x_g', 1),
                                         ('ln_ffn_g', 0), ('ln_ffn_g', 1)]):
                k.dma('sp', self.gains[:, j, :], d[nm][l].rearrange("(k p) -> p k", p=128), writes=['gains'])
            k.dma('sp', self.gains[:, 6, :], d['ln_f_g'].rearrange("(k p) -> p k", p=128), writes=['gains'])
        self.load_xT()

    def cid(self):
        return self.cst[:, C_ID:C_ID + 128]

    def load_xT(self):
        k = self.k
        d = k.dram
        with contextlib.ExitStack() as ph:
            stg = k.sb(ph, [128, 4, D], F32, 'xstg')
            for bi, (t0, tn) in enumerate(TMB):
                sl = bi % 4
                src = d['xp'][t0:t0 + tn, :] if t0 < TP else d['xs'][t0 - TP:t0 - TP + tn, :]
                k.dma('sp', stg[:tn, sl, :], src, writes=['xstg%d' % sl])
                for half in range(2):
                    bank = 2 * sl + half
                    for c in range(4):
                        kk = half * 4 + c
                        k.tr(self.ps[bank][:, c * 128:c * 128 + tn], stg[:tn, sl, kk * 128:(kk + 1) * 128],
                             self.cst[:tn, C_ID:C_ID + tn], ['xstg%d' % sl, 'cst'], ['ps%d' % bank])
                    src_ps = self.ps[bank][:, :].rearrange("p (c t) -> p c t", c=4)[:, :, :tn]
                    k.cp('act' if half == 0 else 'dve', self.xT[:, half * 4:half * 4 + 4, t0:t0 + tn], src_ps,
                         ['ps%d' % bank], ['xT.%d' % bkey(t0)])
            k.barrier()

    def rmsnorm_fm(self, gidx, ph, out_fn=None, out_keys=None):
        k = self.k
        sq = k.sb(ph, [128, 2, 512], BF16, 'sq')
        rs = k.sb(ph, [128, 2, 512], F32, 'rs')
        n = 0
        for b, (t0, tn) in enumerate(TB):
            bank = b % 2
            for kk in range(8):
                sl = n % 2
                n += 1
                k.act(sq[:, sl, :tn], self.xT[:, kk, t0:t0 + tn], AF.Square, ['xT.%d' % b], ['sq%d' % sl])
                k.mm(self.ps[bank][:, :tn], self.ones_bf[:, :], sq[:, sl, :tn], kk == 0, kk == 7,
                     ['sq%d' % sl, 'ones'], ['ps%d' % bank])
            r = rs[:, b % 2, :tn]
            rk = 'rs%d' % (b % 2)
            k.act(r, self.ps[bank][:, :tn], AF.Ln, ['ps%d' % bank], [rk], scale=1.0 / D, bias=EPS)
            k.act(r, r, AF.Exp, [rk], [rk], scale=-0.5)
            for kk in range(8):
                if out_fn is None:
                    dst, wk = self.hT[:, kk, t0:t0 + tn], ['hT.%d' % b]
                else:
                    dst, wk = out_fn(kk, t0, tn), out_keys(b)
                k.stt(dst, self.xT[:, kk, t0:t0 + tn], self.gains[:, gidx, kk:kk + 1], r, ALU.mult, ALU.mult,
                      ['xT.%d' % b, rk, 'gains'], wk)

    def add_proj(self, wname, l, row0, nch, yT, ykey, ph, banks=(6, 7), wt=None):
        k = self.k
        if wt is None:
            wt = k.sb(ph, [128, 2, nch, 256], BF16, 'wo')
        src = k.dram[wname][l][row0:row0 + 128 * nch, :].rearrange("(c p) n -> p c n", p=128)
        j = 0
        for n2 in range(4):
            sl = n2 % 2
            k.dma('pool', wt[:, sl, :, :], src[:, :, n2 * 256:(n2 + 1) * 256], writes=['wo%d' % sl])
            for nn in range(2):
                n = 2 * n2 + nn
                for b, (t0, tn) in enumerate(TB):
                    bank = banks[j % len(banks)]
                    j += 1
                    for c in range(nch):
                        k.mm(self.ps[bank][:, :tn], wt[:, sl, c, nn * 128:(nn + 1) * 128], yT[:, c, t0:t0 + tn], c == 0, c == nch - 1,
                             ['wo%d' % sl, ykey], ['ps%d' % bank])
                    k.tt('dve', self.xT[:, n, t0:t0 + tn], self.xT[:, n, t0:t0 + tn], self.ps[bank][:, :tn], ALU.add,
                         ['ps%d' % bank, 'xT.%d' % b], ['xT.%d' % b])

    def phase_sb(self, l):
        k = self.k
        d = k.dram
        ps = self.ps
        cb = self.cbf
        with contextlib.ExitStack() as ph:
            qT = k.sb(ph, [128, 3, NT], BF16, 'qT')
            kT = k.sb(ph, [128, 3, NT + 128], BF16, 'kT')
            k.memset('pool', kT[:, :, NT:NT + 128], 0.0, ['kT'])
            vB = k.sb(ph, [128, 16, 384], BF16, 'vB')
            vS = k.sb(ph, [32, 2, 384], BF16, 'vS')
            ybT = k.sb(ph, [128, 3, NT], BF16, 'ybT')
            kTp = k.sb(ph, [128, 2, 3, 1024], BF16, 'kTp')
            vP = k.sb(ph, [128, 2, 8, 384], BF16, 'vP')
            bfm = k.sb(ph, [128, 6], F32, 'bfm')
            gb = k.sb(ph, [128, 3], F32, 'gb')
            with contextlib.ExitStack() as ph2:
                wsb = k.sb(ph2, [128, 8, 1152], BF16, 'wsb')
                brow = k.sb(ph2, [1, 768], BF16, 'brow')
                kvst = k.sb(ph2, [128, 2, 768], F32, 'kvst')
                wsrc = d['w_in'][l].rearrange("(c p) n -> p c n", p=128)
                for c0 in range(0, 8, 2):
                    k.dma('pool', wsb[:, c0:c0 + 2, :], wsrc[:, c0:c0 + 2, 1544:2696], writes=['wsb%d' % (c0 // 2)])
                k.dma('pool', brow[:, :], d['b_in'][l:l + 1, 1928:2696], writes=['brow'])
                with self.nc.allow_non_contiguous_dma(reason="small vectors"):
                    k.dma('sp', bfm[:, :], d['b_in'][l, 1544:2312].rearrange("(c p) -> p c", p=128), writes=['bfm'])
                    k.dma('sp', gb[:, :], d['gn_b_g'][l].rearrange("(c p) -> p c", p=128), writes=['gb'])
                for j in range(2):
                    k.dma('pool', vP[:, j, :, :], d['csbv'][l, j].rearrange("(n p) f -> p n f", p=128), writes=['vP%d' % j])
                self.chk('sb_load')
                n = 0
                for c in range(6):
                    for b, (t0, tn) in enumerate(TB):
                        bank = n % 2
                        n += 1
                        for kk in range(8):
                            k.mm(ps[bank][:, :tn], wsb[:, kk, c * 128:(c + 1) * 128], self.hT[:, kk, t0:t0 + tn], kk == 0, kk == 7,
                                 ['wsb%d' % (kk // 2), 'hT.%d' % b], ['ps%d' % bank])
                        if c < 3:
                            k.ts('dve', qT[:, c, t0:t0 + tn], ps[bank][:, :tn], bfm[:, c:c + 1], 0.125, ALU.add, ALU.mult,
                                 ['ps%d' % bank, 'bfm'], ['qT'])
                        else:
                            k.act(kT[:, c - 3, t0:t0 + tn], ps[bank][:, :tn], AF.Identity, ['ps%d' % bank, 'bfm'], ['kT'],
                                  bias=bfm[:, c:c + 1])
                self.chk('sb_fm')
                for bi, (t0, tn) in enumerate(TMB):
                    sl = bi % 2
                    b = bkey(t0)
                    for part in range(2):
                        bank = 2 + 2 * sl + part
                        for kk in range(8):
                            k.mm(ps[bank][:tn, 0:384], self.hT[:, kk, t0:t0 + tn], wsb[:, kk, 384 + 384 * part:768 + 384 * part],
                                 kk == 0, False, ['wsb%d' % (kk // 2), 'hT.%d' % b], ['ps%d' % bank])
                        k.mm(ps[bank][:tn, 0:384], self.ones_bf[0:1, :tn], brow[0:1, 384 * part:384 * part + 384], False, True,
                             ['ones', 'brow'], ['ps%d' % bank])
                        k.cp('act' if part == 0 else 'dve', kvst[:tn, sl, 384 * part:384 * part + 384], ps[bank][:tn, 0:384],
                             ['ps%d' % bank], ['kvst%d' % sl])
                    if t0 < TP:
                        k.cp('pool', vB[:, bi, :], kvst[:, sl, 384:768], ['kvst%d' % sl], ['vB'])
                        k.dma('sp', d['p_sb_k'][l, t0:t0 + tn, :], kvst[:tn, sl, 0:384], reads=['kvst%d' % sl])
                        k.dma('sp', d['p_sb_v'][l, t0:t0 + tn, :], kvst[:tn, sl, 384:768], reads=['kvst%d' % sl])
                    else:
                        j = (t0 - TP) // TSQ
                        k.cp('pool', vS[:, j, :], kvst[:tn, sl, 384:768], ['kvst%d' % sl], ['vB'])
                        k.dma('sp', d['s_sb_k'][l, t0 - TP:t0 - TP + tn, :], kvst[:tn, sl, 0:384], reads=['kvst%d' % sl])
                        k.dma('sp', d['s_sb_v'][l, t0 - TP:t0 - TP + tn, :], kvst[:tn, sl, 384:768], reads=['kvst%d' % sl])
                k.barrier()
            self.chk('sb_tm')
            with contextlib.ExitStack() as ph3:
                e32 = k.sb(ph3, [128, 3, 512], F32, 'e32')
                Lp = k.sb(ph3, [128, 3, 512], BF16, 'Lp')
                At = k.sb(ph3, [128, 2, 512], BF16, 'At')
                sqo = k.sb(ph3, [128, 2, 512], BF16, 'sqo')
                rso = k.sb(ph3, [128, 2, 512], F32, 'rso')
                self._tile_n = 0
                self._grp_n = 0
                k.memset('pool', sqo[:], 0.0, ['sqo0', 'sqo1'])
                qTs = k.sb(ph3, [128, 3, 2, 2 * TSQ], BF16, 'qTs')
                for par in range(2):
                    k.ts('dve', qTs[:, :, par, :], qT[:, :, TP:NT], self.cst[:, C_MISC + par:C_MISC + par + 1], None, ALU.mult, None, ['qT', 'cst'], ['qTs'])
                with contextlib.ExitStack() as phk:
                    kst = k.sb(phk, [128, 4, 384], F32, 'kst')
                    n = 0
                    for j in range(2):
                        for kb in range(8):
                            sl = n % 4
                            n += 1
                            k.dma('sp', kst[:, sl, :], d['csbk'][l, j, kb * 128:(kb + 1) * 128, :], writes=['kst%d' % sl])
                            bank = 4 + sl
                            for c in range(3):
                                k.tr(ps[bank][:, c * 128:(c + 1) * 128], kst[:, sl, c * 128:(c + 1) * 128], self.cid(),
                                     ['kst%d' % sl, 'cst'], ['ps%d' % bank])
                            k.cp('act', kTp[:, j, :, kb * 128:(kb + 1) * 128],
                                 ps[bank][:, 0:384].rearrange("p (c t) -> p c t", c=3), ['ps%d' % bank], ['kTp%d' % j])
                    k.barrier()

                Ls = k.sb(ph3, [128, 2, 4, 512], BF16, 'Ls')
                tiles = []

                def group(segs, W, kblocks, fin):
                    g = self._grp_n % 2
                    self._grp_n += 1
                    obank = 4 + g
                    okey = 'ps%d' % obank
                    touched = set()
                    nkb = len(kblocks)
                    for bi, (kbid, nk, clo, diag) in enumerate(kblocks):
                        first, last = bi == 0, bi == nkb - 1
                        t = self._tile_n % 3
                        t2 = self._tile_n % 2
                        sbank = abank = self._tile_n % 4
                        self._tile_n += 1
                        sk = ak = 'ps%d' % sbank
                        ek, lk, atk = 'e32%d' % t, 'Lp%d' % t, 'At%d' % t2

                        def stage1(first=first, last=last, bi=bi, kbid=kbid, nk=nk, clo=clo, diag=diag, t=t, sbank=sbank, sk=sk, ek=ek, lk=lk):
                            if first:
                                k.memset('pool', Ls[:, g, :, :W], 0.0, ['Ls%d_0' % g, 'Ls%d_1' % g, 'Ls%d_2' % g, 'Ls%d_3' % g])
                            for si, (c0, ncol, q_ap, pb, kfn, vfn) in enumerate(segs):
                                lo = clo if len(segs) == 1 else 0
                                k.mm(ps[sbank][:, c0 + lo:c0 + ncol], kfn(kbid), q_ap[:, lo:ncol], si == 0, False,
                                     ['qT', 'qTs', 'kT', 'kTp0', 'kTp1'], [sk])
                            k.act(e32[:nk, t, clo:W], ps[sbank][:nk, clo:W], AF.Exp, [sk], [ek])

                        def stage1b(first=first, last=last, bi=bi, kbid=kbid, nk=nk, clo=clo, diag=diag, t=t, sbank=sbank, sk=sk, ek=ek, lk=lk):
                            k.act(Lp[:nk, t, clo:W], e32[:nk, t, clo:W], AF.Ln, [ek], [lk], bias=1.0)
                            if diag is not None:
                                dc, dn = diag
                                for (c0, ncol, q_ap, pb, kfn, vfn) in segs:
                                    k.tt('pool', Lp[:dn, t, c0 + dc:c0 + dc + dn], Lp[:dn, t, c0 + dc:c0 + dc + dn],
                                         cb[:dn, C_SBM:C_SBM + dn], ALU.mult, [lk, 'cbf'], [lk])
                            if not last:
                                k.tt('pool', Ls[:nk, g, (bi + 1) % 4, clo:W], Ls[:nk, g, bi % 4, clo:W], Lp[:nk, t, clo:W], ALU.add,
                                     ['Ls%d_%d' % (g, bi % 4), lk], ['Ls%d_%d' % (g, (bi + 1) % 4)])

                        def stage2(first=first, last=last, bi=bi, kbid=kbid, nk=nk, clo=clo, diag=diag, t=t, t2=t2, abank=abank, ak=ak, lk=lk, atk=atk):
                            k.mm(ps[abank][:, clo:W], cb[:nk, C_NTI:C_NTI + 128], Lp[:nk, t, clo:W], False, first and diag is None,
                                 [lk, 'cbf'], [ak])
                            if not first:
                                k.mm(ps[abank][:, clo:W], self.nones_bf[:, :], Ls[:, g, bi % 4, clo:W], False, diag is None,
                                     ['Ls%d_%d' % (g, bi % 4), 'nones'], [ak])
                            if diag is not None:
                                dc, dn = diag
                                for si, (c0, ncol, q_ap, pb, kfn, vfn) in enumerate(segs):
                                    k.mm(ps[abank][:, c0 + dc:c0 + dc + dn], cb[:dn, C_ID:C_ID + 128], cb[:dn, C_SBNEG:C_SBNEG + dn],
                                         False, si == len(segs) - 1, ['cbf'], [ak])
                            k.act(At[:nk, t2, clo:W], ps[abank][:nk, clo:W], AF.Exp, [ak], [atk])

                        def stage2b(first=first, last=last, bi=bi, kbid=kbid, nk=nk, clo=clo, diag=diag, t=t, t2=t2, abank=abank, ak=ak, lk=lk, atk=atk):
                            for si, (c0, ncol, q_ap, pb, kfn, vfn) in enumerate(segs):
                                lo = clo if len(segs) == 1 else 0
                                st = pb not in touched
                                touched.add(pb)
                                k.mm(ps[obank][pb:pb + 64, c0 + lo:c0 + ncol], vfn(kbid), At[:nk, t2, c0 + lo:c0 + ncol], st, last,
                                     [atk, 'vB', 'vP0', 'vP1'], [okey])

                        epiA = epiB = None
                        if last:
                            pbs = sorted(set(s_[3] for s_ in segs))
                            sbk = 6 + g

                            def epiA(pbs=pbs, sbk=sbk):
                                for pb in pbs:
                                    k.act(sqo[pb:pb + 64, g, :W], ps[obank][pb:pb + 64, :W], AF.Square, [okey], ['sqo%d' % g])
                                k.mm(ps[sbk][:, :W], cb[:, C_BLK:C_BLK + 128], sqo[:, g, :W], True, True, ['sqo%d' % g, 'cbf'], ['ps%d' % sbk])

                            def epiB(pbs=pbs, sbk=sbk):
                                for pb in pbs:
                                    k.act(rso[pb:pb + 64, g, :W], ps[sbk][pb:pb + 64, :W], AF.Ln, ['ps%d' % sbk], ['rso%d' % g],
                                          scale=1.0 / 64, bias=EPS)
                                    k.act(rso[pb:pb + 64, g, :W], rso[pb:pb + 64, g, :W], AF.Exp, ['rso%d' % g], ['rso%d' % g], scale=-0.5)
                                fin(obank, okey, rso, g)
                        tiles.append((stage1, stage2, epiA, epiB, stage1b, stage2b))

                def run_pipeline():
                    pendA = pendB = None
                    n = len(tiles)
                    def epi_step(i):
                        nonlocal pendA, pendB, pendA_b
                        if pendB is not None:
                            pendB()
                            pendB = None
                        if pendA is not None:
                            pendA()
                            pendB = pendA_b
                            pendA = None
                        if i >= 0 and tiles[i][2] is not None:
                            pendA, pendA_b = tiles[i][2], tiles[i][3]
                    pendA_b = None
                    for j in range(min(3, n)):
                        tiles[j][0]()
                    for j in range(min(2, n)):
                        tiles[j][4]()
                    for i in range(n + 1):
                        if i + 3 < n:
                            tiles[i + 3][0]()
                        if i + 2 < n:
                            tiles[i + 2][4]()
                        if i < n:
                            tiles[i][1]()
                        if i >= 1:
                            tiles[i - 1][5]()
                            epi_step(i - 1)
                    epi_step(-1)
                    epi_step(-1)
                    del tiles[:]

                for h in range(6 if self.on('sb_prompt') else 0):
                    c, pb = h // 2, 64 * (h % 2)
                    for qg in range(4):
                        q0 = 512 * qg
                        seg = (0, 512, qT[pb:pb + 64, c, q0:q0 + 512], pb,
                               lambda kb, c=c, pb=pb: kT[pb:pb + 64, c, kb * 128:(kb + 1) * 128],
                               lambda kb, h=h: vB[:, kb, 64 * h:64 * h + 64])
                        kbl = []
                        for kb in range(4 * qg + 3, -1, -1):
                            r = kb - 4 * qg
                            if r >= 0:
                                kbl.append((kb, 128, 128 * r, (128 * r, 128)))
                            else:
                                kbl.append((kb, 128, 0, None))

                        def fin(obank, okey, rso, g, c=c, pb=pb, q0=q0):
                            k.stt(ybT[pb:pb + 64, c, q0:q0 + 512], ps[obank][pb:pb + 64, :512], gb[pb:pb + 64, c:c + 1],
                                  rso[pb:pb + 64, g, :512], ALU.mult, ALU.mult, [okey, 'rso%d' % g, 'gb'], ['ybT'])
                        group([seg], 512, kbl, fin)
                for j in range(2):
                    q0 = TP + TSQ * j
                    segs = []
                    for h in range(6):
                        c, pb = h // 2, 64 * (h % 2)

                        def kfn(kb, c=c, pb=pb, j=j, q0=q0):
                            if kb == 8:
                                return kT[:, c, q0:q0 + 128]
                            return kTp[:, j, c, kb * 128:(kb + 1) * 128]

                        def vfn(kb, h=h, j=j):
                            if kb == 8:
                                return vS[:, j, 64 * h:64 * h + 64]
                            return vP[:, j, kb, 64 * h:64 * h + 64]
                        segs.append((32 * h, 32, qTs[:, c, h % 2, TSQ * j:TSQ * j + TSQ], pb, kfn, vfn))
                    kbl = [(8, 32, 0, (0, 32))] + [(kb, 128, 0, None) for kb in range(7, -1, -1)]

                    def fin(obank, okey, rso, g, q0=q0):
                        for h in range(6):
                            c, pb = h // 2, 64 * (h % 2)
                            k.stt(ybT[pb:pb + 64, c, q0:q0 + TSQ], ps[obank][pb:pb + 64, 32 * h:32 * h + 32], gb[pb:pb + 64, c:c + 1],
                                  rso[pb:pb + 64, g, 32 * h:32 * h + 32], ALU.mult, ALU.mult, [okey, 'rso%d' % g, 'gb'], ['ybT'])
                    group(segs, 192, kbl, fin)
                run_pipeline()
                k.barrier()
            if 'ybT' in self.dbg:
                k.dbg_dump = True
                for c in range(3):
                    k.dma('pool', d['ybT'][c], ybT[:, c, :], reads=['ybT'])
                k.barrier()
            self.add_proj('w_out', l, 384, 3, ybT, 'ybT', ph)
            k.barrier()

    def phase_cross(self, l):
        k = self.k
        d = k.dram
        ps = self.ps
        with contextlib.ExitStack() as ph:
            self.rmsnorm_fm(2 + l, ph)
            mkT = k.sb(ph, [128, 3, 8, 256], BF16, 'mkT')
            mv = k.sb(ph, [128, 3, 2, D], BF16, 'mv')
            gm = k.sb(ph, [128, 8], F32, 'gm')
            with self.nc.allow_non_contiguous_dma(reason="small vectors"):
                k.dma('sp', gm[:, :], d['ln_mem_g'][l].rearrange("(c p) -> p c", p=128), writes=['gm'])
            for j in range(2):
                k.dma('pool', mv[:, 1 + j, :, :], d['cmv'][l, j].rearrange("(n p) f -> p n f", p=128), writes=['mv'])
            with contextlib.ExitStack() as ph2:
                mst2 = [k.sb(ph2, [128, 2, D], F32, 'mst%d' % q) for q in range(2)]
                mnT = k.sb(ph2, [128, 8, 256], BF16, 'mnT')
                ss = k.sb(ph2, [128, 4], F32, 'ss')
                junk = k.sb(ph2, [128, D], BF16, 'junk')
                ost = k.sb(ph2, [128, 2, 512], F32, 'ost')
                wk = k.sb(ph2, [128, 8, D], BF16, 'wk')
                wv = k.sb(ph2, [128, 8, D], BF16, 'wv')
                for (wt_, nm) in ((wk, 'w_xk'), (wv, 'w_xv')):
                    src = d[nm][l].rearrange("(c p) n -> p c n", p=128)
                    for c0 in range(0, 8, 2):
                        k.dma('pool', wt_[:, c0:c0 + 2, :], src[:, c0:c0 + 2, :], writes=['%s%d' % (nm, c0 // 2)])
                for j in range(2):
                    mst, mk_ = mst2[j], 'mst%d' % j
                    k.dma('sp', mst[:, :, :], d['cmk'][l, j].rearrange("(n p) f -> p n f", p=128), writes=[mk_])
                    for mb in range(2):
                        for half in range(2):
                            bank = 2 * mb + half
                            for c in range(4):
                                kk = half * 4 + c
                                k.tr(ps[bank][:, c * 128:(c + 1) * 128], mst[:, mb, kk * 128:(kk + 1) * 128], self.cid(),
                                     [mk_, 'cst'], ['ps%d' % bank])
                            k.cp('act' if half == 0 else 'dve', mkT[:, 1 + j, half * 4:half * 4 + 4, mb * 128:(mb + 1) * 128],
                                 ps[bank][:, :].rearrange("p (c t) -> p c t", c=4), ['ps%d' % bank], ['mkT'])
                self.chk('x_a')
                mst = mst2[0]
                k.dma('sp', mst[:, :, :], d['memp'].rearrange("(n p) f -> p n f", p=128), writes=['mst', 'mst0'])
                for mb in range(2):
                    k.act(junk[:, :], mst[:, mb, :], AF.Square, ['mst'], ['junk', 'ss'], accum_out=ss[:, mb:mb + 1])
                k.act(ss[:, 2:4], ss[:, 0:2], AF.Ln, ['ss'], ['ss2'], scale=1.0 / D, bias=EPS)
                k.act(ss[:, 2:4], ss[:, 2:4], AF.Exp, ['ss2'], ['ss2'], scale=-0.5)
                for mb in range(2):
                    k.ts('dve', mst[:, mb, :], mst[:, mb, :], ss[:, 2 + mb:3 + mb], None, ALU.mult, None, ['mst', 'ss2'], ['mst'])
                for mb in range(2):
                    for half in range(2):
                        bank = 4 + 2 * mb + half
                        for c in range(4):
                            kk = half * 4 + c
                            k.tr(ps[bank][:, c * 128:(c + 1) * 128], mst[:, mb, kk * 128:(kk + 1) * 128], self.cid(),
                                 ['mst', 'cst'], ['ps%d' % bank])
                        for c in range(4):
                            kk = half * 4 + c
                            k.ts('dve', mnT[:, kk, mb * 128:(mb + 1) * 128], ps[bank][:, c * 128:(c + 1) * 128], gm[:, kk:kk + 1], None,
                                 ALU.mult, None, ['ps%d' % bank, 'gm'], ['mnT'])
                self.chk('x_b')
                for n in range(8):
                    bank = n % 2
                    for kk in range(8):
                        k.mm(ps[bank][:, 0:256], wk[:, kk, n * 128:(n + 1) * 128], mnT[:, kk, :], kk == 0, kk == 7,
                             ['w_xk%d' % (kk // 2), 'mnT'], ['ps%d' % bank])
                    k.cp('act', mkT[:, 0, n, :], ps[bank][:, 0:256], ['ps%d' % bank], ['mkT'])
                self.chk('x_c')
                n = 0
                for (wt_, nm, onm) in ((wk, 'w_xk', 'p_mem_k'), (wv, 'w_xv', 'p_mem_v')):
                    for mb in range(2):
                        for nh in range(2):
                            bank = 2 + n % 2
                            sl = n % 2
                            n += 1
                            for kk in range(8):
                                k.mm(ps[bank][:, :], mnT[:, kk, mb * 128:(mb + 1) * 128], wt_[:, kk, nh * 512:(nh + 1) * 512], kk == 0, kk == 7,
                                     ['%s%d' % (nm, kk // 2), 'mnT'], ['ps%d' % bank])
                            k.cp('act', ost[:, sl, :], ps[bank][:, :], ['ps%d' % bank], ['ost%d' % sl])
                            if nm == 'w_xv' and not self.flag('x_nomv'):
                                k.cp('pool', mv[:, 0, mb, nh * 512:(nh + 1) * 512], ost[:, sl, :], ['ost%d' % sl], ['mv'])
                            if not self.flag('x_nodma'):
                                k.dma('sp', d[onm][l, mb * 128:(mb + 1) * 128, nh * 512:(nh + 1) * 512], ost[:, sl, :], reads=['ost%d' % sl])
                            self.chk('x_c%d' % n)
                k.barrier()
            self.chk('x_d')
            oT = k.sb(ph, [128, 8, NT], BF16, 'oT')
            with contextlib.ExitStack() as ph3:
                wq = k.sb(ph3, [128, 2, 8, 256], BF16, 'wq')
                qh = k.sb(ph3, [128, 2, 2, NT], BF16, 'qh')
                pT = k.sb(ph3, [128, 2, 2, 512], BF16, 'pT')
                rden = k.sb(ph3, [128, 2, 512], F32, 'rden')
                qsrc = d['w_xq'][l].rearrange("(c p) n -> p c n", p=128)
                blocks = [(t0, tn, 0) for (t0, tn) in TB[:4]] + [(TP, TSQ, 1), (TP + TSQ, TSQ, 2)]
                it = 0
                def qgroups(h):
                    sl = h % 2
                    k.dma('pool', wq[:, sl, :, :], qsrc[:, :, 256 * h:256 * h + 256], writes=['wq%d' % sl])
                    gl = []
                    for dc in range(2):
                        for b, (t0, tn) in enumerate(TB):
                            def g_(dc=dc, b=b, t0=t0, tn=tn, sl=sl):
                                bank = b % 2
                                for kk in range(8):
                                    k.mm(ps[bank][:, :tn], wq[:, sl, kk, dc * 128:(dc + 1) * 128], self.hT[:, kk, t0:t0 + tn], kk == 0, kk == 7,
                                         ['wq%d' % sl, 'hT.%d' % b], ['ps%d' % bank])
                                k.cp('act' if b % 2 == 0 else 'dve', qh[:, sl, dc, t0:t0 + tn], ps[bank][:, :tn], ['ps%d' % bank], ['qh%d' % sl])
                            gl.append(g_)
                    return gl
                for g_ in qgroups(0):
                    g_()
                for h in range(4):
                    sl = h % 2
                    pend = qgroups(h + 1) if h < 3 else []
                    for (t0, tn, mset) in blocks:
                        i2 = it % 2
                        it += 1
                        for mc in range(2):
                            bank = 2 + mc
                            for dc in range(2):
                                k.mm(ps[bank][:, :tn], mkT[:, mset, 2 * h + dc, mc * 128:(mc + 1) * 128], qh[:, sl, dc, t0:t0 + tn], dc == 0, dc == 1,
                                     ['mkT', 'qh%d' % sl], ['ps%d' % bank])
                            k.act(pT[:, i2, mc, :tn], ps[bank][:, :tn], AF.Exp, ['ps%d' % bank], ['pT%d' % i2], scale=1.0 / 16)
                        for _ in range(2):
                            if pend:
                                pend.pop(0)()
                        for mc in range(2):
                            k.mm(ps[4][:, :tn], self.ones_bf[:, :], pT[:, i2, mc, :tn], mc == 0, mc == 1, ['pT%d' % i2, 'ones'], ['ps4'])
                        k.act(rden[:, i2, :tn], ps[4][:, :tn], AF.Ln, ['ps4'], ['rden%d' % i2])
                        k.act(rden[:, i2, :tn], rden[:, i2, :tn], AF.Exp, ['rden%d' % i2], ['rden%d' % i2], scale=-1.0)
                        for dc in range(2):
                            bank = 5 + dc
                            for mc in range(2):
                                k.mm(ps[bank][:, :tn], mv[:, mset, mc, 256 * h + 128 * dc:256 * h + 128 * dc + 128], pT[:, i2, mc, :tn], mc == 0, mc == 1,
                                     ['mv', 'pT%d' % i2], ['ps%d' % bank])
                            k.tt('dve', oT[:, 2 * h + dc, t0:t0 + tn], ps[bank][:, :tn], rden[:, i2, :tn], ALU.mult,
                                 ['ps%d' % bank, 'rden%d' % i2], ['oT'])
                    for g_ in pend:
                        g_()
                k.barrier()
            self.chk('x_e')
            self.add_proj('w_xo', l, 0, 8, oT, 'oT', ph)
            k.barrier()

    def phase_ffn(self, l):
        k = self.k
        d = k.dram
        ps = self.ps
        NG = 11
        with contextlib.ExitStack() as ph:
            self.rmsnorm_fm(4 + l, ph)
            cw = k.sb(ph, [128, 3, NF], F32, 'cw')
            cbi = k.sb(ph, [128, NF], F32, 'cbi')
            pv = k.sb(ph, [128, 2, 2, NF], F32, 'pv')
            cvst = k.sb(ph, [128, 2, 3, 128], F32, 'cvst')
            with self.nc.allow_non_contiguous_dma(reason="small conv params / states"):
                for j in range(3):
                    k.dma('sp', cw[:, j, :], d['ffn_conv_w'][l, j].rearrange("(c p) -> p c", p=128), writes=['cw'])
                k.dma('sp', cbi[:, :], d['ffn_conv_b'][l].rearrange("(c p) -> p c", p=128), writes=['cw'])
                for j in range(2):
                    for t in range(2):
                        k.dma('sp', pv[:, j, t, :], d['sfc'][l, j, t].rearrange("(c p) -> p c", p=128), writes=['pv'])
            yT = k.sb(ph, [128, NG, NT], BF16, 'yT')
            wa = k.sb(ph, [128, 2, 8, 128], BF16, 'wa')
            wb = k.sb(ph, [128, 2, 8, 128], BF16, 'wb')
            aS = k.sb(ph, [128, 2, 2 + TP], F32, 'aS')
            aSs = k.sb(ph, [128, 2, 2, 2 + TSQ], F32, 'aSs')
            cc = k.sb(ph, [128, 2, 512], F32, 'cc')
            sg = k.sb(ph, [128, 2, 512], F32, 'sg')
            wt_down = k.sb(ph, [128, 2, NG, 256], BF16, 'wo')
            asrc = d['w_ffn_a'][l].rearrange("(c p) n -> p c n", p=128)
            bsrc = d['w_ffn_b'][l].rearrange("(c p) n -> p c n", p=128)
            k.memset('pool', aS[:, :, 0:2], 0.0, ['aS0', 'aS1'])
            n = 0
            for g in range(2):
                if True:
                    for fl in range(NG):
                        fc = g * NG + fl
                        sl = fl % 2
                        k.dma('pool', wa[:, sl, :, :], asrc[:, :, fc * 128:(fc + 1) * 128], writes=['wa%d' % sl])
                        k.dma('pool', wb[:, sl, :, :], bsrc[:, :, fc * 128:(fc + 1) * 128], writes=['wb%d' % sl])
                        ak = 'aS%d' % sl
                        for j in range(2):
                            k.cp('pool', aSs[:, sl, j, 0:2], pv[:, j, :, fc], ['pv'], [ak])
                        for b, (t0, tn) in enumerate(TB):
                            bankA = b % 2
                            bank = 2 + b % 2
                            i2 = n % 2
                            n += 1
                            ck = 'cc%d' % i2
                            for kk in range(8):
                                k.mm(ps[bankA][:, :tn], wa[:, sl, kk, :], self.hT[:, kk, t0:t0 + tn], kk == 0, kk == 7,
                                     ['wa%d' % sl, 'hT.%d' % b], ['ps%d' % bankA])
                            if b < 4:
                                cur, m1, m2 = aS[:, sl, 2 + t0:2 + t0 + tn], aS[:, sl, 1 + t0:1 + t0 + tn], aS[:, sl, t0:t0 + tn]
                                co = cc[:, i2, :tn]
                                so = sg[:, i2, :tn]
                                pa_ = ps[bankA][:, :tn]
                                pb_ = ps[bank][:, :tn]
                                yo = yT[:, fl, t0:t0 + tn]
                            else:
                                cur, m1, m2 = aSs[:, sl, :, 2:2 + TSQ], aSs[:, sl, :, 1:1 + TSQ], aSs[:, sl, :, 0:TSQ]
                                co = cc[:, i2, 0:2 * TSQ].rearrange("p (j t) -> p j t", j=2)
                                so = sg[:, i2, 0:2 * TSQ].rearrange("p (j t) -> p j t", j=2)
                                pa_ = ps[bankA][:, 0:2 * TSQ].rearrange("p (j t) -> p j t", j=2)
                                pb_ = ps[bank][:, 0:2 * TSQ].rearrange("p (j t) -> p j t", j=2)
                                yo = yT[:, fl, t0:t0 + tn].rearrange("p (j t) -> p j t", j=2)
                            k.cp('act', cur, pa_, ['ps%d' % bankA], [ak])
                            k.act(co, pa_, AF.Identity, ['ps%d' % bankA, 'cw'], [ck], scale=cw[:, 2, fc:fc + 1], bias=cbi[:, fc:fc + 1])
                            for kk in range(8):
                                k.mm(ps[bank][:, :tn], wb[:, sl, kk, :], self.hT[:, kk, t0:t0 + tn], kk == 0, kk == 7,
                                     ['wb%d' % sl, 'hT.%d' % b], ['ps%d' % bank])
                            k.stt(co, m1, cw[:, 1, fc:fc + 1], co, ALU.mult, ALU.add, [ak, 'cw', ck], [ck])
                            k.stt(co, m2, cw[:, 0, fc:fc + 1], co, ALU.mult, ALU.add, [ak, 'cw', ck], [ck])
                            k.act(so, co, AF.Silu, [ck], ['sg%d' % i2])
                            k.tt('dve', yo, so, pb_, ALU.mult, ['sg%d' % i2, 'ps%d' % bank], ['yT'])
                        for si, tend in enumerate((TP, TP + TSQ, NT)):
                            bank = 4 + si
                            for kk in range(8):
                                k.mm(ps[bank][:, 0:128], self.hT[:, kk, tend - 128:tend], wa[:, sl, kk, :], kk == 0, kk == 7,
                                     ['wa%d' % sl, 'hT.3', 'hT.4'], ['ps%d' % bank])
                        for si in range(3):
                            bank = 4 + si
                            k.cp('dve', cvst[96:128, sl, si, :], ps[bank][96:128, 0:128], ['ps%d' % bank], ['cvst%d.%d' % (sl, si)])
                            dst = d['p_ffn_conv'][l, :, fc * 128:(fc + 1) * 128] if si == 0 else d['s_ffn_conv'][l, si - 1, :, fc * 128:(fc + 1) * 128]
                            k.dma('sp', dst, cvst[126:128, sl, si, :], reads=['cvst%d.%d' % (sl, si)])
                self.add_proj('w_ffn_down', l, 128 * NG * g, NG, yT, 'yT', ph, wt=wt_down)
            k.barrier()

    def phase_mlstm(self, l):
        k = self.k
        d = k.dram
        ps = self.ps
        cst = self.cst
        MI = C_MISC
        SC = 96 ** -0.5
        with contextlib.ExitStack() as ph:
            yaT = k.sb(ph, [128, 3, NT], BF16, 'yaT')
            wqk = k.sb(ph, [128, 8, 768], BF16, 'wqk')
            wkvo = k.sb(ph, [128, 8, 1152], BF16, 'wkvo')
            wg = k.sb(ph, [128, 8, 64], BF16, 'wg')
            brow = k.sb(ph, [1, 1152], BF16, 'browa')
            bqk = k.sb(ph, [96, 8], F32, 'bqk')
            bif = k.sb(ph, [8, 4], F32, 'bif')
            m0t = k.sb(ph, [8, 4], F32, 'm0t')
            gna = k.sb(ph, [128, 384], F32, 'gna')
            NB = k.sb(ph, [8, NT], F32, 'NB')
            G = k.sb(ph, [8, NT], F32, 'G')
            M = k.sb(ph, [8, NT], F32, 'M')
            Cst = k.sb(ph, [96, 4, 128], F32, 'Cst')
            Cbf = k.sb(ph, [96, 4, 97], BF16, 'Cbf')
            c0l = k.sb(ph, [96, 4, 128], F32, 'c0l')
            wsrc = d['w_in'][l].rearrange("(c p) n -> p c n", p=128)
            for c0 in range(0, 8, 2):
                k.dma('pool', wqk[:, c0:c0 + 2, :], wsrc[:, c0:c0 + 2, 0:768], writes=['wqk%d' % (c0 // 2)])
                k.dma('pool', wkvo[:, c0:c0 + 2, :], wsrc[:, c0:c0 + 2, 384:1536], writes=['wkvo%d' % (c0 // 2)])
            k.memset('pool', wg[:], 0.0, ['wg'])
            k.memset('dve', m0t[:], 0.0, ['m0t'])
            with self.nc.allow_non_contiguous_dma(reason="small vectors"):
                for q in range(2):
                    k.dma('pool', wg[:, :, 4 * q:4 * q + 4], wsrc[:, :, 1536:1540], writes=['wg'])
                    k.dma('pool', wg[:, :, 32 + 4 * q:36 + 4 * q], wsrc[:, :, 1540:1544], writes=['wg'])
                    k.dma('sp', bif[4 * q:4 * q + 4, 0:1], d['b_in'][l, 1536:1540].rearrange("(p o) -> p o", o=1), writes=['bif'])
                    k.dma('sp', bif[4 * q:4 * q + 4, 1:2], d['b_in'][l, 1540:1544].rearrange("(p o) -> p o", o=1), writes=['bif'])
                    for j in range(2):
                        k.dma('sp', m0t[4 * q:4 * q + 4, 1 + j:2 + j], d['smm'][l, j].rearrange("(p o) -> p o", o=1), writes=['m0t'])
                k.dma('sp', bqk[:, :], d['b_in'][l, 0:768].rearrange("(c p) -> p c", p=96), writes=['bqk'])
                k.dma('sp', gna[:, :], d['gn_a_g'][l:l + 1, :].to_broadcast([128, 384]), writes=['gna'])
            k.dma('pool', brow[:, :], d['b_in'][l:l + 1, 384:1536], writes=['browa'])
            k.ts('dve', bif[:, 2:3], bif[:, 1:2], -1.0, None, ALU.mult, None, ['bif'], ['bif'])
            if self.on('mixnorm'):
                with contextlib.ExitStack() as phn:
                    self.rmsnorm_fm(0 + l, phn)
                    k.barrier()
            for b, (t0, tn) in enumerate(TB):
                for kk in range(8):
                    k.mm(ps[0][0:32, :tn], wg[:, kk, 0:32], self.hT[:, kk, t0:t0 + tn], kk == 0, kk == 7, ['wg', 'hT.%d' % b], ['ps0'])
                for kk in range(8):
                    k.mm(ps[1][0:32, :tn], wg[:, kk, 32:64], self.hT[:, kk, t0:t0 + tn], kk == 0, kk == 7, ['wg', 'hT.%d' % b], ['ps1'])
                k.act(G[:, t0:t0 + tn], ps[0][0:8, :tn], AF.Identity, ['ps0', 'bif'], ['G'], bias=bif[:, 0:1])
                k.act(M[:, t0:t0 + tn], ps[1][0:8, :tn], AF.Exp, ['ps1', 'bif'], ['M'], bias=bif[:, 2:3], scale=-1.0)
                k.act(M[:, t0:t0 + tn], M[:, t0:t0 + tn], AF.Ln, ['M'], ['M'], bias=1.0)
            segs = [(0, TP, 0), (TP, TSQ, 1), (TP + TSQ, TSQ, 2)]
            for (t0, tn, si) in segs:
                k.op('dve', lambda e, t0=t0, tn=tn: e.tensor_tensor_scan(
                    out=NB[:, t0:t0 + tn], data0=self.ones32[0:8, 0:1].to_broadcast([8, tn]), data1=M[:, t0:t0 + tn],
                    initial=0.0, op0=ALU.mult, op1=ALU.add), ['M', 'ones32'], ['NB'])
            k.tt('dve', G[:, :], G[:, :], NB[:, :], ALU.add, ['G', 'NB'], ['G'])
            for (t0, tn, si) in segs:
                k.op('dve', lambda e, t0=t0, tn=tn, si=si: e.tensor_tensor_scan(
                    out=M[:, t0:t0 + tn], data0=G[:, t0:t0 + tn], data1=G[:, t0:t0 + tn],
                    initial=m0t[:, si:si + 1], op0=ALU.max, op1=ALU.max), ['G', 'm0t', 'NB'], ['M'])
            self.chk('ml_gates')
            with contextlib.ExitStack() as ph2:
                qc = k.sb(ph2, [96, 4, 128], BF16, 'qc')
                kc = k.sb(ph2, [96, 4, 128], BF16, 'kc')
                ktmp = k.sb(ph2, [96, 4, 128], F32, 'ktmp')
                kTM = k.sb(ph2, [128, 384], BF16, 'kTM')
                vc = k.sb(ph2, [128, 4, 97], BF16, 'vc')
                vw = k.sb(ph2, [128, 4, 97], BF16, 'vw')
                og = k.sb(ph2, [128, 384], F32, 'og')
                gm1 = k.sb(ph2, [8, 128], F32, 'gm1')
                LH = k.sb(ph2, [8, 4, 128], F32, 'LH')
                RH = k.sb(ph2, [8, 128], F32, 'RH')
                gsm = k.sb(ph2, [8, 3, 128], F32, 'gsm')
                sml = k.sb(ph2, [8, 8], F32, 'sml')
                gTM = k.sb(ph2, [128, 24], F32, 'gTM')
                wT = k.sb(ph2, [128, 512], F32, 'wT')
                Sw = k.sb(ph2, [128, 512], BF16, 'Sw')
                tmp = k.sb(ph2, [128, 4, 97], F32, 'tmp')
                nd = k.sb(ph2, [128, 4, 97], F32, 'nd')
                hg = k.sb(ph2, [128, 384], F32, 'hg')
                sq = k.sb(ph2, [128, 384], F32, 'sqa')
                st = k.sb(ph2, [128, 16], F32, 'st')
                decb = k.sb(ph2, [96, 4], F32, 'decb')
                cout = k.sb(ph2, [128, 4, 96], F32, 'cout')
                k.memset('pool', vc[:, :, 96:97], 1.0, ['vc'])
                k.memset('pool', c0l[:], 0.0, ['c0l'])
                chunks = [(128 * i, 128, 0) for i in range(16)] + [(TP, TSQ, 1), (TP + TSQ, TSQ, 2)]
                for ci, (t0, n, si) in enumerate(chunks):
                    b = bkey(t0)
                    hk = 'hT.%d' % b
                    first = (ci == 0) or si > 0
                    last = (ci == 15) or si > 0
                    if first:
                        if si == 0:
                            k.memset('pool', Cst[:], 0.0, ['Cst'])
                        else:
                            j = si - 1
                            k.memset('pool', Cst[:], 0.0, ['Cst'])
                            k.dma('sp', c0l[:, :, 0:96], d['smc'][l, j].rearrange("h v k -> v h k"), writes=['c0l'])
                            with self.nc.allow_non_contiguous_dma(reason="n0 state"):
                                k.dma('sp', Cst[:, :, 96], d['smn'][l, j].rearrange("h k -> k h"), writes=['Cst'])
                            for h in range(4):
                                k.tr(ps[7][:, h * 96:(h + 1) * 96], c0l[:, h, :], cst[0:96, C_ID:C_ID + 96], ['c0l', 'cst'], ['ps7'])
                            k.cp('dve', Cst[:, :, 0:96], ps[7][0:96, 0:384].rearrange("p (h v) -> p h v", h=4), ['ps7'], ['Cst'])
                        k.cp('act', Cbf[:, :, :], Cst[:, :, 0:97], ['Cst'], ['Cbf'])
                    for hc in range(8):
                        bank = hc // 4
                        for kk in range(8):
                            k.mm(ps[bank][0:96, (hc % 4) * 128:(hc % 4) * 128 + n], wqk[:, kk, 96 * hc:96 * hc + 96], self.hT[:, kk, t0:t0 + n],
                                 kk == 0, kk == 7, ['wqk%d' % (kk // 2), hk], ['ps%d' % bank])
                    p0 = ps[0][0:96, :].rearrange("p (h t) -> p h t", h=4)[:, :, :n]
                    p1 = ps[1][0:96, :].rearrange("p (h t) -> p h t", h=4)[:, :, :n]
                    k.tt('dve', qc[:, :, :n], p0, bqk[:, 0:4].unsqueeze(2).to_broadcast([96, 4, n]), ALU.add, ['ps0', 'bqk'], ['qc'])
                    k.tt('dve', ktmp[:, :, :n], p1, bqk[:, 4:8].unsqueeze(2).to_broadcast([96, 4, n]), ALU.add, ['ps1', 'bqk'], ['ktmp'])
                    k.act(kc[:, :, :n], ktmp[:, :, :n], AF.Copy, ['ktmp'], ['kc'], scale=SC)
                    for part in range(3):
                        bank = 2 + part
                        for kk in range(8):
                            k.mm(ps[bank][:n, 0:384], self.hT[:, kk, t0:t0 + n], wkvo[:, kk, 384 * part:384 * part + 384], kk == 0, False,
                                 ['wkvo%d' % (kk // 2), hk], ['ps%d' % bank])
                        k.mm(ps[bank][:n, 0:384], self.ones_bf[0:1, :n], brow[0:1, 384 * part:384 * part + 384], False, True,
                             ['ones', 'browa'], ['ps%d' % bank])
                    k.act(kTM[:n, :], ps[2][:n, 0:384], AF.Copy, ['ps2'], ['kTM'], scale=SC)
                    k.cp('dve', vc[:n, :, 0:96], ps[3][:n, 0:384].rearrange("p (h v) -> p h v", h=4), ['ps3'], ['vc'])
                    k.act(og[:n, :], ps[4][:n, 0:384], AF.Exp, ['ps4'], ['og'], scale=-1.0)
                    k.act(og[:n, :], og[:n, :], AF.Ln, ['og'], ['og'], bias=1.0)
                    k.act(og[:n, :], og[:n, :], AF.Exp, ['og'], ['og'], scale=-1.0)
                    k.ts('dve', gm1[:, :n], G[:, t0:t0 + n], cst[0:8, MI + 2:MI + 3], cst[0:8, MI + 3:MI + 4], ALU.mult, ALU.add, ['G', 'cst'], ['gm1'])
                    for h in range(4):
                        k.ts('dve', LH[:, h, :n], gm1[:, :n], cst[0:8, MI + 4 + h:MI + 5 + h], None, ALU.mult, None, ['gm1', 'cst'], ['LH'])
                    k.ts('dve', RH[:, :n], M[:, t0:t0 + n], cst[0:8, MI + 10:MI + 11], cst[0:8, MI + 2:MI + 3], ALU.mult, ALU.add, ['M', 'cst'], ['RH'])
                    mprev = m0t[:, si:si + 1] if first else M[:, t0 - 1:t0]
                    k.act(gsm[:, 0, :n], M[:, t0:t0 + n], AF.Exp, ['M', 'm0t'], ['gsm'], scale=-1.0, bias=mprev)
                    k.ts('dve', sml[:, 0:1], M[:, t0 + n - 1:t0 + n], -1.0, None, ALU.mult, None, ['M'], ['sml'])
                    k.act(gsm[:, 1, :n], G[:, t0:t0 + n], AF.Exp, ['G', 'sml'], ['gsm'], bias=sml[:, 0:1])
                    k.tt('dve', gsm[:, 2, :n], NB[:, t0:t0 + n], M[:, t0:t0 + n], ALU.subtract, ['NB', 'M'], ['gsm2'])
                    k.act(gsm[:, 2, :n], gsm[:, 2, :n], AF.Exp, ['gsm2'], ['gsm2'])
                    for h in range(4):
                        k.mm(ps[5][:n, h * 128:h * 128 + n], LH[:, h, :n], RH[:, :n], h == 0, False, ['LH', 'RH'], ['ps5'])
                    for h in range(4):
                        k.mm(ps[5][:n, h * 128:h * 128 + n], cst[:n, C_ID:C_ID + n], cst[:n, C_MLNEG:C_MLNEG + n], False, h == 3, ['cst'], ['ps5'])
                    p5 = ps[5][:n, :].rearrange("p (h t) -> p h t", h=4)[:, :, :n]
                    wT3 = wT[:n, :].rearrange("p (h t) -> p h t", h=4)[:, :, :n]
                    Sw3 = Sw[:n, :].rearrange("p (h t) -> p h t", h=4)[:, :, :n]
                    k.act(wT3, p5, AF.Exp, ['ps5'], ['wT'])
                    for h in range(4):
                        k.mm(ps[6][:n, h * 128:h * 128 + n], kc[:, h, :n], qc[:, h, :n], h == 0, h == 3, ['kc', 'qc'], ['ps6'])
                    p6 = ps[6][:n, :].rearrange("p (h t) -> p h t", h=4)[:, :, :n]
                    k.tt('dve', Sw3, p6, wT3, ALU.mult, ['ps6', 'wT'], ['Sw'])
                    for h in range(4):
                        k.mm(ps[7][:n, h * 97:(h + 1) * 97], Sw[:n, h * 128:h * 128 + n], vc[:n, h, :], h == 0, h == 3, ['Sw', 'vc'], ['ps7'])
                    for h in range(4):
                        k.mm(ps[0][:n, h * 97:(h + 1) * 97], qc[:, h, :n], Cbf[:, h, :], h == 0, h == 3, ['qc', 'Cbf'], ['ps0'])
                    for q in range(3):
                        k.mm(ps[1][:n, q * 8:(q + 1) * 8], gsm[:, q, :n], cst[0:8, C_ID:C_ID + 8], q == 0, q == 2, ['gsm', 'gsm2', 'cst'], ['ps1'])
                    k.cp('act', gTM[:n, :], ps[1][:n, 0:24], ['ps1'], ['gTM'])
                    p7 = ps[7][:n, 0:388].rearrange("p (h v) -> p h v", h=4)
                    p0b = ps[0][:n, 0:388].rearrange("p (h v) -> p h v", h=4)
                    k.tt('dve', tmp[:n], p0b, gTM[:n, 0:4].unsqueeze(2).to_broadcast([n, 4, 97]), ALU.mult, ['ps0', 'gTM'], ['tmp'])
                    k.tt('dve', nd[:n], p7, tmp[:n], ALU.add, ['ps7', 'tmp'], ['nd'])
                    k.act(st[:n, 0:4], nd[:n, :, 96], AF.Abs, ['nd'], ['st'])
                    k.tt('dve', st[:n, 0:4], st[:n, 0:4], gTM[:n, 16:20], ALU.max, ['st', 'gTM'], ['st'])
                    k.op('dve', lambda e, n=n: e.reciprocal(out=st[:n, 4:8], in_=st[:n, 0:4]), ['st'], ['st'])
                    hg3 = hg[:n, :].rearrange("p (h v) -> p h v", h=4)
                    k.tt('dve', hg3, nd[:n, :, 0:96], st[:n, 4:8].unsqueeze(2).to_broadcast([n, 4, 96]), ALU.mult, ['nd', 'st'], ['hg'])
                    k.tt('dve', hg[:n, :], hg[:n, :], og[:n, :], ALU.mult, ['hg', 'og'], ['hg'])
                    k.tt('dve', sq[:n, :], hg[:n, :], hg[:n, :], ALU.mult, ['hg'], ['sqa'])
                    k.op('dve', lambda e, n=n: e.tensor_reduce(out=st[:n, 8:12], in_=sq[:n, :].rearrange("p (h v) -> p h v", h=4),
                                                               axis=AX.X, op=ALU.add), ['sqa'], ['st2'])
                    k.act(st[:n, 12:16], st[:n, 8:12], AF.Ln, ['st2'], ['st3'], scale=1.0 / 96, bias=EPS)
                    k.act(st[:n, 12:16], st[:n, 12:16], AF.Exp, ['st3'], ['st3'], scale=-0.5)
                    k.tt('dve', hg3, hg3, st[:n, 12:16].unsqueeze(2).to_broadcast([n, 4, 96]), ALU.mult, ['hg', 'st3'], ['hg'])
                    k.tt('dve', hg[:n, :], hg[:n, :], gna[:n, :], ALU.mult, ['hg', 'gna'], ['hg'])
                    for c in range(3):
                        k.tr(ps[3][:, c * 128:c * 128 + n], hg[:n, c * 128:(c + 1) * 128], cst[:n, C_ID:C_ID + n], ['hg', 'cst'], ['ps3'])
                    k.cp('act', yaT[:, :, t0:t0 + n], ps[3][:, 0:384].rearrange("p (c t) -> p c t", c=3)[:, :, :n], ['ps3'], ['yaT'])
                    k.tt('dve', vw[:n], vc[:n], gTM[:n, 8:12].unsqueeze(2).to_broadcast([n, 4, 97]), ALU.mult, ['vc', 'gTM'], ['vw'])
                    for h in range(4):
                        k.mm(ps[2][0:96, h * 97:(h + 1) * 97], kTM[:n, 96 * h:96 * h + 96], vw[:n, h, :], h == 0, h == 3, ['kTM', 'vw'], ['ps2'])
                    k.ts('dve', sml[:, 4:8], cst[0:8, MI + 12:MI + 16], gsm[:, 0, n - 1:n], None, ALU.mult, None, ['gsm', 'cst'], ['sml2'])
                    k.mm(ps[4][0:96, 0:4], self.ones32[0:8, 0:96], sml[:, 4:8], True, True, ['sml2', 'ones32'], ['ps4'])
                    k.cp('act', decb[:, :], ps[4][0:96, 0:4], ['ps4'], ['decb'])
                    k.tt('dve', Cst[:, :, 0:97], Cst[:, :, 0:97], decb[:, :].unsqueeze(2).to_broadcast([96, 4, 97]), ALU.mult, ['Cst', 'decb'], ['Cst'])
                    k.tt('dve', Cst[:, :, 0:97], Cst[:, :, 0:97], ps[2][0:96, 0:388].rearrange("p (h v) -> p h v", h=4), ALU.add,
                         ['Cst', 'ps2'], ['Cst'])
                    if not last:
                        k.cp('act', Cbf[:, :, :], Cst[:, :, 0:97], ['Cst'], ['Cbf'])
                    else:
                        for h in range(4):
                            k.tr(ps[6][:, h * 96:(h + 1) * 96], Cst[:, h, :], cst[0:96, C_ID:C_ID + 96], ['Cst', 'cst'], ['ps6'])
                        k.cp('act', cout[:, :, :], ps[6][:, 0:384].rearrange("p (h k) -> p h k", h=4), ['ps6'], ['cout'])
                        k.tt('dve', sml[:, 1:2], M[:, t0 + n - 1:t0 + n], NB[:, t0 + n - 1:t0 + n], ALU.subtract, ['M', 'NB'], ['sml3'])
                        if si == 0:
                            dc_, dn_, dm_ = d['p_ml_c'][l], d['p_ml_n'][l], d['p_ml_m'][l]
                        else:
                            dc_, dn_, dm_ = d['s_ml_c'][l, si - 1], d['s_ml_n'][l, si - 1], d['s_ml_m'][l, si - 1]
                        k.dma('sp', dc_.rearrange("h v k -> v h k"), cout[0:96, :, :], reads=['cout'])
                        k.dma('sp', dn_.rearrange("(o h) k -> o h k", o=1), cout[96:97, :, :], reads=['cout'])
                        with self.nc.allow_non_contiguous_dma(reason="m state"):
                            k.dma('sp', dm_.rearrange("(p o) -> p o", o=1), sml[0:4, 1:2], reads=['sml3'])
                k.barrier()
            self.chk('ml_core')
            self.add_proj('w_out', l, 0, 3, yaT, 'yaT', ph)
            k.barrier()

    def phase_s5(self, l):
        k = self.k
        d = k.dram
        ps = self.ps
        cst = self.cst
        MI = C_MISC
        NC_ = NT // 8
        TWO_PI = 2.0 * np.pi

        def bc(ap, shape):
            return ap.unsqueeze(2).to_broadcast(shape)

        with contextlib.ExitStack() as ph:
            WinR = k.sb(ph, [128, 2, 8, 128], BF16, 'WinR')
            WinI = k.sb(ph, [128, 2, 8, 128], BF16, 'WinI')
            Wout = k.sb(ph, [128, 2, 2, 8, 128], BF16, 'Wout')
            Kbd = k.sb(ph, [128, 2, 8, 128], BF16, 'Kbd')
            Win3R = k.sb(ph, [128, 2, 8, 128], BF16, 'Win3R')
            Win3I = k.sb(ph, [128, 2, 8, 128], BF16, 'Win3I')
            Wout3 = k.sb(ph, [128, 2, 2, 8, 64], BF16, 'Wout3')
            uT = k.sb(ph, [128, 2, 8, NC_], BF16, 'uT')
            Sprev = k.sb(ph, [128, 8, 2, NC_], BF16, 'Sprev')
            MUr = k.sb(ph, [128, 8, 8], F32, 'MUr')
            MUi = k.sb(ph, [128, 8, 8], F32, 'MUi')
            nMUi = k.sb(ph, [128, 8, 8], F32, 'nMUi')
            h0 = k.sb(ph, [128, 8, 2, 2], F32, 'h0')
            with contextlib.ExitStack() as pp:
                _n = [0]

                def T(shape, dt=F32):
                    _n[0] += 1
                    return k.sb(pp, shape, dt, 'p%d' % _n[0])
                are, aim, ldt = T([128, 8]), T([128, 8]), T([128, 8])
                with self.nc.allow_non_contiguous_dma(reason="small ssm params"):
                    for t_, nm in ((are, 'ssm_a_re'), (aim, 'ssm_a_im'), (ldt, 'ssm_log_dt')):
                        k.dma('sp', t_[:, :], d[nm][l].rearrange("(a g) p -> (g p) a", g=2), writes=['prm'])
                    for ri, nm in enumerate(('ssr', 'ssi')):
                        for j in range(2):
                            k.dma('sp', h0[:, :, ri, j], d[nm][l, j].rearrange("(a g) p -> (g p) a", g=2), writes=['h0'])
                R_ = ['prm']
                dt = T([128, 8]); ang = T([128, 8]); mag = T([128, 8])
                k.act(dt[:], ldt[:], AF.Exp, R_, R_)
                k.tt('dve', ang[:], aim[:], dt[:], ALU.mult, R_, R_)
                k.tt('dve', mag[:], are[:], dt[:], ALU.mult, R_, R_)
                k.act(mag[:], mag[:], AF.Exp, R_, R_)
                sc = T([128, 2, 8])
                xx = T([128, 8]); ti = T([128, 8], mybir.dt.int32); tf = T([128, 8]); gg = T([128, 8])
                for q in range(2):
                    if q == 0:
                        k.cp('dve', xx[:], ang[:], R_, R_)
                    else:
                        k.ts('dve', xx[:], ang[:], float(np.pi / 2), None, ALU.add, None, R_, R_)
                    k.ts('dve', tf[:], xx[:], float(1.0 / TWO_PI), None, ALU.mult, None, R_, R_)
                    k.cp('dve', ti[:], tf[:], R_, R_)
                    k.cp('dve', tf[:], ti[:], R_, R_)
                    k.stt(xx[:], tf[:], -TWO_PI, xx[:], ALU.mult, ALU.add, R_, R_)
                    k.ts('dve', gg[:], xx[:], float(np.pi), None, ALU.is_gt, None, R_, R_)
                    k.stt(xx[:], gg[:], -TWO_PI, xx[:], ALU.mult, ALU.add, R_, R_)
                    k.ts('dve', gg[:], xx[:], float(-np.pi), None, ALU.is_lt, None, R_, R_)
                    k.stt(xx[:], gg[:], TWO_PI, xx[:], ALU.mult, ALU.add, R_, R_)
                    k.act(sc[:, q, :], xx[:], AF.Sin, R_, R_)
                LAMr = T([128, 9, 8]); LAMi = T([128, 9, 8]); nLAMi = T([128, 9, 8])
                k.memset('dve', LAMr[:, 0, :], 1.0, R_)
                k.memset('dve', LAMi[:, 0, :], 0.0, R_)
                k.tt('dve', LAMr[:, 1, :], mag[:], sc[:, 1, :], ALU.mult, R_, R_)
                k.tt('dve', LAMi[:, 1, :], mag[:], sc[:, 0, :], ALU.mult, R_, R_)
                t1 = T([128, 8]); t2 = T([128, 8])

                def cmul(or_, oi_, ar, ai, br, bi):
                    k.tt('dve', t1[:], ar, br, ALU.mult, R_, R_)
                    k.tt('dve', t2[:], ai, bi, ALU.mult, R_, R_)
                    k.tt('dve', or_, t1[:], t2[:], ALU.subtract, R_, R_)
                    k.tt('dve', t1[:], ar, bi, ALU.mult, R_, R_)
                    k.tt('dve', t2[:], ai, br, ALU.mult, R_, R_)
                    k.tt('dve', oi_, t1[:], t2[:], ALU.add, R_, R_)
                for kk in range(1, 8):
                    cmul(LAMr[:, kk + 1, :], LAMi[:, kk + 1, :], LAMr[:, kk, :], LAMi[:, kk, :], LAMr[:, 1, :], LAMi[:, 1, :])
                k.ts('dve', nLAMi[:], LAMi[:], -1.0, None, ALU.mult, None, R_, R_)
                k.cp('dve', MUr[:, 0, :], LAMr[:, 8, :], R_, ['MU'])
                k.cp('dve', MUi[:, 0, :], LAMi[:, 8, :], R_, ['MU'])
                RM = ['prm', 'MU']
                for kk in range(7):
                    k.tt('dve', t1[:], MUr[:, kk, :], MUr[:, kk, :], ALU.mult, RM, R_)
                    k.tt('dve', t2[:], MUi[:, kk, :], MUi[:, kk, :], ALU.mult, RM, R_)
                    k.tt('dve', MUr[:, kk + 1, :], t1[:], t2[:], ALU.subtract, R_, ['MU'])
                    k.tt('dve', t1[:], MUr[:, kk, :], MUi[:, kk, :], ALU.mult, RM, R_)
                    k.ts('dve', MUi[:, kk + 1, :], t1[:], 2.0, None, ALU.mult, None, R_, ['MU'])
                k.ts('dve', nMUi[:], MUi[:], -1.0, None, ALU.mult, None, RM, ['MU'])
                den = T([128, 8]); nr = T([128, 8]); zr = T([128, 8]); zi = T([128, 8])
                k.tt('dve', den[:], are[:], are[:], ALU.mult, R_, R_)
                k.tt('dve', t1[:], aim[:], aim[:], ALU.mult, R_, R_)
                k.tt('dve', den[:], den[:], t1[:], ALU.add, R_, R_)
                k.op('dve', lambda e: e.reciprocal(out=den[:], in_=den[:]), R_, R_)
                k.ts('dve', nr[:], LAMr[:, 1, :], -1.0, None, ALU.add, None, R_, R_)
                k.tt('dve', t1[:], nr[:], are[:], ALU.mult, R_, R_)
                k.tt('dve', t2[:], LAMi[:, 1, :], aim[:], ALU.mult, R_, R_)
                k.tt('dve', zr[:], t1[:], t2[:], ALU.add, R_, R_)
                k.tt('dve', zr[:], zr[:], den[:], ALU.mult, R_, R_)
                k.tt('dve', t1[:], LAMi[:, 1, :], are[:], ALU.mult, R_, R_)
                k.tt('dve', t2[:], nr[:], aim[:], ALU.mult, R_, R_)
                k.tt('dve', zi[:], t1[:], t2[:], ALU.subtract, R_, R_)
                k.tt('dve', zi[:], zi[:], den[:], ALU.mult, R_, R_)
                bre = T([128, 8, 16]); bim = T([128, 8, 16]); bbr = T([128, 8, 16]); bbi = T([128, 8, 16])
                u1 = T([128, 8, 16]); u2 = T([128, 8, 16])
                with self.nc.allow_non_contiguous_dma(reason="ssm B"):
                    k.dma('sp', bre[:], d['ssm_b_re'][l].rearrange("(a g) p c -> (g p) a c", g=2), writes=['prm'])
                    k.dma('sp', bim[:], d['ssm_b_im'][l].rearrange("(a g) p c -> (g p) a c", g=2), writes=['prm'])
                S16 = [128, 8, 16]

                def cmul16(or_, oi_, ar, ai, br, bi):
                    k.tt('dve', u1[:], ar, bc(br, S16), ALU.mult, R_, R_)
                    k.tt('dve', u2[:], ai, bc(bi, S16), ALU.mult, R_, R_)
                    k.tt('dve', or_, u1[:], u2[:], ALU.subtract, R_, R_)
                    k.tt('dve', u1[:], ar, bc(bi, S16), ALU.mult, R_, R_)
                    k.tt('dve', u2[:], ai, bc(br, S16), ALU.mult, R_, R_)
                    k.tt('dve', oi_, u1[:], u2[:], ALU.add, R_, R_)
                cmul16(bbr[:], bbi[:], bre[:], bim[:], zr[:], zi[:])
                cnat = T([128, 2, 2, 64])
                for ri, nm in enumerate(('ssm_c_re', 'ssm_c_im')):
                    k.dma('sp', cnat[:, ri, :, :], d[nm][l].rearrange("(hh g) c p -> (g c) hh p", hh=2), writes=['prm'])
                X = T([128, 2, 2, 2, 64])
                for g2 in range(2):
                    k.ts('dve', X[:, :, :, g2, :], cnat[:, :, :, :], cst[:, MI + 8 + g2:MI + 9 + g2], None, ALU.mult, None, R_ + ['cst'], R_)
                CT = T([128, 3, 2, 128])
                for ri in range(2):
                    for hh in range(2):
                        k.tr(ps[0][:, (2 * ri + hh) * 128:(2 * ri + hh + 1) * 128],
                             X[:, ri, hh, :, :].rearrange("p g q -> p (g q)"), self.cid(), R_ + ['cst'], ['ps0'])
                k.cp('act', CT[:, 0:2, :, :], ps[0][:, :].rearrange("p (r h c) -> p r h c", r=2, h=2), ['ps0'], R_)
                k.ts('dve', CT[:, 2, :, :], CT[:, 1, :, :], -1.0, None, ALU.mult, None, R_, R_)
                S32 = [128, 8, 32]
                vv = [(T([128, 8, 32]), T([128, 8, 32]), T([128, 8, 32]), T([128, 8, 32])) for _ in range(2)]

                def c8(ri):
                    return CT[:, ri, :, :].rearrange("p h (a x) -> p (h a) x", a=4)
                r4 = lambda t_: t_[:].rearrange("p (h a) x -> p h a x", h=2)
                for j in range(8):
                    lr, li, nli = LAMr[:, j + 1, :], LAMi[:, j + 1, :], nLAMi[:, j + 1, :]
                    v1, v2, v3, v4 = vv[j % 2]
                    kz = ['wv%d.%d' % (j % 2, q) for q in range(4)]
                    k.tt('dve', v1[:], c8(0), bc(lr, S32), ALU.mult, R_, [kz[0]])
                    k.tt('pool', v2[:], c8(1), bc(li, S32), ALU.mult, R_, [kz[1]])
                    k.tt('dve', v3[:], c8(0), bc(nli, S32), ALU.mult, R_, [kz[2]])
                    k.tt('pool', v4[:], c8(1), bc(lr, S32), ALU.mult, R_, [kz[3]])
                    k.tt('dve', Wout[:, :, 0, j, :].rearrange("p h (a x) -> p h a x", a=4), r4(v1), r4(v2), ALU.subtract, [kz[0], kz[1]], ['Wout'])
                    k.tt('dve', Wout[:, :, 1, j, :].rearrange("p h (a x) -> p h a x", a=4), r4(v3), r4(v4), ALU.subtract, [kz[2], kz[3]], ['Wout'])
                Yf = T([128, 2, 8, 256])
                yy = [tuple(T([128, 8, 16]) for _ in range(6)) for _ in range(2)]
                for dd in range(8):
                    a1, a2, a3, a4, pr, pi_ = yy[dd % 2]
                    kz = ['yv%d.%d' % (dd % 2, q) for q in range(6)]
                    lr, li = LAMr[:, dd, :], LAMi[:, dd, :]
                    k.tt('dve', a1[:], bbr[:], bc(lr, S16), ALU.mult, R_, [kz[0]])
                    k.tt('pool', a2[:], bbi[:], bc(li, S16), ALU.mult, R_, [kz[1]])
                    k.tt('dve', a3[:], bbr[:], bc(li, S16), ALU.mult, R_, [kz[2]])
                    k.tt('pool', a4[:], bbi[:], bc(lr, S16), ALU.mult, R_, [kz[3]])
                    k.tt('dve', pr[:], a1[:], a2[:], ALU.subtract, [kz[0], kz[1]], [kz[4]])
                    k.tt('pool', pi_[:], a3[:], a4[:], ALU.add, [kz[2], kz[3]], [kz[5]])
                    for ri, src in enumerate((pr, pi_)):
                        for g2 in range(2):
                            k.ts('dve' if g2 == 0 else 'pool', Yf[:, ri, dd, :].rearrange("p (a g c) -> p a g c", g=2, c=16)[:, :, g2, :], src[:],
                                 cst[:, MI + g2:MI + g2 + 1], None, ALU.mult, None, [kz[4 + ri], 'cst'], ['Yf%d' % dd])
                nb = 0
                for ri, W_ in enumerate((WinR, WinI)):
                    for hh in range(2):
                        for i0 in range(0, 8, 4):
                            bank = 1 + nb % 2
                            nb += 1
                            for ii in range(4):
                                i = i0 + ii
                                k.tr(ps[bank][:, ii * 128:(ii + 1) * 128], Yf[:, ri, 7 - i, hh * 128:(hh + 1) * 128], self.cid(),
                                     ['Yf%d' % (7 - i), 'cst'], ['ps%d' % bank])
                            k.cp('act' if bank == 1 else 'dve', W_[:, hh, i0:i0 + 4, :], ps[bank][:, :].rearrange("p (i c) -> p i c", i=4),
                                 ['ps%d' % bank], ['Win'])
                for W3, W_ in ((Win3R, WinR), (Win3I, WinI)):
                    k.ts('pool', W3[:].rearrange("p h i c -> p (h i c)"), W_[:].rearrange("p h i c -> p (h i c)"),
                         cst[:, MI + 16:MI + 17], None, ALU.mult, None, ['Win', 'cst'], ['Win'])
                k.memset('pool', Wout3[:], 0.0, ['Wout'])
                k.cp('pool', Wout3[:, :, :, :, 32:64], Wout[:, :, :, :, 96:128], ['Wout'], ['Wout'])
                k.memset('pool', Kbd[:], 0.0, ['Kbd'])
                for hh in range(2):
                    for dd in range(8):
                        col = (hh * 8 + dd) * 32
                        f0 = hh == 0 and dd == 0
                        l0 = hh == 1 and dd == 7
                        for a4 in range(3):
                            cs = slice(hh * 128 + 32 * a4, hh * 128 + 32 * a4 + 32)
                            k.mm(ps[3][32 * a4:32 * a4 + 32, col:col + 32], Yf[:, 0, dd, cs], CT[:, 0, hh, 32 * a4:32 * a4 + 32],
                                 f0, False, ['Yf%d' % dd] + R_, ['ps3'])
                            k.mm(ps[3][32 * a4:32 * a4 + 32, col:col + 32], Yf[:, 1, dd, cs], CT[:, 2, hh, 32 * a4:32 * a4 + 32],
                                 False, l0, ['Yf%d' % dd] + R_, ['ps3'])
                        cs = slice(hh * 128 + 64, hh * 128 + 128)
                        k.mm(ps[4][64:128, col:col + 32], Yf[:, 0, dd, cs], CT[:, 0, hh, 96:128], f0, False, ['Yf%d' % dd] + R_, ['ps4'])
                        k.mm(ps[4][64:128, col:col + 32], Yf[:, 1, dd, cs], CT[:, 2, hh, 96:128], False, l0, ['Yf%d' % dd] + R_, ['ps4'])
                for a4 in range(4):
                    src = ps[3] if a4 < 3 else ps[4]
                    k.cp('act', Kbd[32 * a4:32 * a4 + 32, :, :, 32 * a4:32 * a4 + 32],
                         src[32 * a4:32 * a4 + 32, :].rearrange("p (h d c) -> p h d c", h=2, d=8), ['ps3', 'ps4'], ['Kbd'])
                k.barrier()
            self.chk('s5_prep')
            with contextlib.ExitStack() as pm:
                wu = k.sb(pm, [128, 8, 256], BF16, 'wu')
                bu = k.sb(pm, [128, 2], F32, 'bu')
                Z = k.sb(pm, [128, 8, 2, NC_], F32, 'Z')
                Z2 = k.sb(pm, [128, 8, 2, NC_], F32, 'Z2')
                fin = k.sb(pm, [128, 8, 2, 4], F32, 'fin')
                hm = k.sb(pm, [128, 8, 2, 2], F32, 'hm')
                hm2 = k.sb(pm, [128, 8, 2, 2], F32, 'hm2')
                wsrc = d['w_in'][l].rearrange("(c p) n -> p c n", p=128)
                k.dma('pool', wu[:, :, :], wsrc[:, :, 2696:2952], writes=['wu'])
                with self.nc.allow_non_contiguous_dma(reason="small"):
                    k.dma('sp', bu[:, :], d['b_in'][l, 2696:2952].rearrange("(h p) -> p h", p=128), writes=['bu'])
                for hh in range(2):
                    for b, (t0, tn) in enumerate(TB):
                        bank = b % 2
                        for kk in range(8):
                            k.mm(ps[bank][:, :tn], wu[:, kk, hh * 128:(hh + 1) * 128], self.hT[:, kk, t0:t0 + tn], kk == 0, kk == 7,
                                 ['wu', 'hT.%d' % b], ['ps%d' % bank])
                        k.act(uT[:, hh, :, t0 // 8:(t0 + tn) // 8].rearrange("p i c -> p c i"),
                              ps[bank][:, :tn].rearrange("p (c i) -> p c i", i=8), AF.Identity, ['ps%d' % bank, 'bu'], ['uT'],
                              bias=bu[:, hh:hh + 1])
                nb = 0
                for a in range(8):
                    hh, a4 = a // 4, a % 4
                    for ri, W_ in enumerate((WinR, WinI)):
                        bank = 2 + nb % 4
                        nb += 1
                        for i in range(8):
                            if a4 < 3:
                                k.mm(ps[bank][:, 0:NC_], W_[32 * a4:32 * a4 + 32, hh, i, :], uT[32 * a4:32 * a4 + 32, hh, i, :], i == 0, i == 7,
                                     ['Win', 'uT'], ['ps%d' % bank])
                            else:
                                W3 = Win3R if ri == 0 else Win3I
                                k.mm(ps[bank][:, 0:NC_], W3[:, hh, i, :], uT[:, hh, i, :], i == 0, i == 7, ['Win', 'uT'], ['ps%d' % bank])
                        k.cp('act' if nb % 2 == 0 else 'dve', Z[:, a, ri, :], ps[bank][:, 0:NC_], ['ps%d' % bank], ['Z'])
                S22 = [128, 8, 2]
                zc = lambda ri: Z[:, :, ri, 256:NC_:4]
                k.tt('dve', hm[:, :, 0, :], h0[:, :, 0, :], bc(MUr[:, 0, :], S22), ALU.mult, ['h0', 'MU'], ['hm'])
                k.tt('dve', hm[:, :, 1, :], h0[:, :, 1, :], bc(MUi[:, 0, :], S22), ALU.mult, ['h0', 'MU'], ['hm'])
                k.tt('dve', hm2[:, :, 0, :], h0[:, :, 1, :], bc(MUr[:, 0, :], S22), ALU.mult, ['h0', 'MU'], ['hm'])
                k.tt('dve', hm2[:, :, 1, :], h0[:, :, 0, :], bc(MUi[:, 0, :], S22), ALU.mult, ['h0', 'MU'], ['hm'])
                k.tt('dve', zc(0), zc(0), hm[:, :, 0, :], ALU.add, ['Z', 'hm'], ['Z'])
                k.tt('dve', zc(0), zc(0), hm[:, :, 1, :], ALU.subtract, ['Z', 'hm'], ['Z'])
                k.tt('dve', zc(1), zc(1), hm2[:, :, 0, :], ALU.add, ['Z', 'hm'], ['Z'])
                k.tt('dve', zc(1), zc(1), hm2[:, :, 1, :], ALU.add, ['Z', 'hm'], ['Z'])
                X_, Y_ = Z, Z2
                xk, yk = 'Z', 'Z2'
                allk = lambda nm: ['%s.%d.%d' % (nm, a, ri) for a in range(8) for ri in range(2)]
                k.cp('pool', Z2[:, :, :, 256:NC_], Z[:, :, :, 256:NC_], ['Z'], allk('Z') + allk('Z2'))
                for kk in range(8):
                    dd = 1 << kk
                    k.cp('pool', Y_[:, :, :, 0:dd], X_[:, :, :, 0:dd], allk(xk), allk(yk))
                    mu = lambda a: (MUr[:, kk, a:a + 1], MUi[:, kk, a:a + 1], nMUi[:, kk, a:a + 1])
                    K_ = lambda nm, a, ri: '%s.%d.%d' % (nm, a, ri)
                    for a in range(8):
                        k.stt(Y_[:, a, 0, dd:256], X_[:, a, 0, 0:256 - dd], mu(a)[0], X_[:, a, 0, dd:256], ALU.mult, ALU.add,
                              [K_(xk, a, 0), 'MU'], [K_(yk, a, 0)])
                    for a in range(8):
                        k.stt(Y_[:, a, 1, dd:256], X_[:, a, 1, 0:256 - dd], mu(a)[0], X_[:, a, 1, dd:256], ALU.mult, ALU.add,
                              [K_(xk, a, 1), 'MU'], [K_(yk, a, 1)])
                    for a in range(8):
                        k.stt(Y_[:, a, 0, dd:256], X_[:, a, 1, 0:256 - dd], mu(a)[2], Y_[:, a, 0, dd:256], ALU.mult, ALU.add,
                              [K_(xk, a, 1), 'MU', K_(yk, a, 0)], [K_(yk, a, 0)])
                    for a in range(8):
                        k.stt(Y_[:, a, 1, dd:256], X_[:, a, 0, 0:256 - dd], mu(a)[1], Y_[:, a, 1, dd:256], ALU.mult, ALU.add,
                              [K_(xk, a, 0), 'MU', K_(yk, a, 1)], [K_(yk, a, 1)])
                    if kk < 2:
                        def sv(T_, a, ri, lo, hi):
                            return T_[:, a, ri, 256:NC_].rearrange("p (j c) -> p j c", j=2)[:, :, lo:hi]
                        for a in range(8):
                            k.stt(sv(Y_, a, 0, dd, 4), sv(X_, a, 0, 0, 4 - dd), mu(a)[0], sv(X_, a, 0, dd, 4), ALU.mult, ALU.add,
                                  [K_(xk, a, 0), 'MU'], [K_(yk, a, 0)])
                        for a in range(8):
                            k.stt(sv(Y_, a, 1, dd, 4), sv(X_, a, 1, 0, 4 - dd), mu(a)[0], sv(X_, a, 1, dd, 4), ALU.mult, ALU.add,
                                  [K_(xk, a, 1), 'MU'], [K_(yk, a, 1)])
                        for a in range(8):
                            k.stt(sv(Y_, a, 0, dd, 4), sv(X_, a, 1, 0, 4 - dd), mu(a)[2], sv(Y_, a, 0, dd, 4), ALU.mult, ALU.add,
                                  [K_(xk, a, 1), 'MU', K_(yk, a, 0)], [K_(yk, a, 0)])
                        for a in range(8):
                            k.stt(sv(Y_, a, 1, dd, 4), sv(X_, a, 0, 0, 4 - dd), mu(a)[1], sv(Y_, a, 1, dd, 4), ALU.mult, ALU.add,
                                  [K_(xk, a, 0), 'MU', K_(yk, a, 1)], [K_(yk, a, 1)])
                        k.cp('pool', Y_[:, :, :, 256:NC_].rearrange("p a r (j c) -> p a r j c", j=2)[:, :, :, :, 0:dd],
                             X_[:, :, :, 256:NC_].rearrange("p a r (j c) -> p a r j c", j=2)[:, :, :, :, 0:dd], allk(xk), allk(yk))
                    X_, Y_ = Y_, X_
                    xk, yk = yk, xk
                k.cp('pool', hm[:, 0, 0, 0:1], hm[:, 0, 0, 0:1], allk('Z') + allk('Z2'), ['Z'])
                assert X_ is Z
                k.memset('pool', Sprev[:, :, :, 0:1], 0.0, ['Sprev'])
                k.cp('pool', Sprev[:, :, :, 1:256], Z[:, :, :, 0:255], ['Z'], ['Sprev'])
                for j in range(2):
                    c0 = 256 + 4 * j
                    k.cp('pool', Sprev[:, :, :, c0], h0[:, :, :, j], ['h0'], ['Sprev'])
                    k.cp('pool', Sprev[:, :, :, c0 + 1:c0 + 4], Z[:, :, :, c0:c0 + 3], ['Z'], ['Sprev'])
                for q, c in enumerate((255, 259, 263)):
                    k.cp('dve', fin[:, :, :, q], Z[:, :, :, c], ['Z'], ['fin'])
                with self.nc.allow_non_contiguous_dma(reason="ssm state out"):
                    for ri, (pn, sn) in enumerate((('p_ssm_re', 's_ssm_re'), ('p_ssm_im', 's_ssm_im'))):
                        k.dma('sp', d[pn][l].rearrange("(a g) p -> (g p) a", g=2), fin[:, :, ri, 0], reads=['fin'])
                        for j in range(2):
                            k.dma('sp', d[sn][l, j].rearrange("(a g) p -> (g p) a", g=2), fin[:, :, ri, 1 + j], reads=['fin'])
                k.barrier()
            self.chk('s5_scan')
            with contextlib.ExitStack() as po:
                ysT = k.sb(po, [128, 2, NT], F32, 'ysT')
                gb = k.sb(po, [128, 2, NT], BF16, 'gb')
                ycb = k.sb(po, [128, 2, NT], BF16, 'ycb')
                tg = k.sb(po, [128, 2, 512], F32, 'tg')
                sgl = k.sb(po, [128, 2, 512], F32, 'sgl')
                sq = k.sb(po, [128, 2, 512], BF16, 'sqc')
                rs = k.sb(po, [128, 2, 512], F32, 'rsc')
                wgl = k.sb(po, [128, 2, 256], BF16, 'wgl')
                sv_ = k.sb(po, [128, 8], F32, 'sv')
                k.dma('pool', wgl[:, :, :], d['w_glu'][l].rearrange("(c p) n -> p c n", p=128), writes=['wgl'])
                with self.nc.allow_non_contiguous_dma(reason="small"):
                    for q, nm in enumerate(('ssm_d', 'b_glu', 'gn_c_g')):
                        k.dma('sp', sv_[:, 2 * q:2 * q + 2], d[nm][l].rearrange("(h p) -> p h", p=128), writes=['sv'])
                nb = 0
                for hh in range(2):
                    for j in range(8):
                        bank = nb % 4
                        nb += 1
                        for i in range(j + 1):
                            k.mm(ps[bank][:, 0:NC_], Kbd[:, hh, j - i, :], uT[:, hh, i, :], i == 0, False, ['Kbd', 'uT'], ['ps%d' % bank])
                        for a4 in range(4):
                            for ri in range(2):
                                if a4 < 3:
                                    k.mm(ps[bank][32 * a4:32 * a4 + 32, 0:NC_], Wout[:, hh, ri, j, 32 * a4:32 * a4 + 32], Sprev[:, 4 * hh + a4, ri, :],
                                         False, False, ['Wout', 'Sprev'], ['ps%d' % bank])
                                else:
                                    k.mm(ps[bank][64:128, 0:NC_], Wout3[:, hh, ri, j, :], Sprev[:, 4 * hh + a4, ri, :],
                                         False, ri == 1, ['Wout', 'Sprev'], ['ps%d' % bank])
                        k.stt(ysT[:, hh, :].rearrange("p (c j) -> p c j", j=8)[:, :, j], uT[:, hh, j, :], sv_[:, hh:hh + 1], ps[bank][:, 0:NC_],
                              ALU.mult, ALU.add, ['uT', 'sv', 'ps%d' % bank], ['ysT'])
                self.chk('s5_y')
                n2 = 0
                for hh in range(2):
                    for b, (t0, tn) in enumerate(TB):
                        i2 = n2 % 2
                        n2 += 1
                        y_ = ysT[:, hh, t0:t0 + tn]
                        k.tt('pool', tg[:, i2, :tn], y_, y_, ALU.mult, ['ysT'], ['tg%d' % i2])
                        k.ts('dve', tg[:, i2, :tn], tg[:, i2, :tn], 0.044715, 1.0, ALU.mult, ALU.add, ['tg%d' % i2], ['tg%d' % i2])
                        k.tt('dve', tg[:, i2, :tn], tg[:, i2, :tn], y_, ALU.mult, ['tg%d' % i2, 'ysT'], ['tg%d' % i2])
                        k.act(tg[:, i2, :tn], tg[:, i2, :tn], AF.Sigmoid, ['tg%d' % i2], ['tg%d' % i2], scale=1.5957691216057308)
                        k.tt('dve', gb[:, hh, t0:t0 + tn], tg[:, i2, :tn], y_, ALU.mult, ['tg%d' % i2, 'ysT'], ['gb'])
                for n in range(2):
                    for b, (t0, tn) in enumerate(TB):
                        bank = 4 + n2 % 2
                        i2 = n2 % 2
                        n2 += 1
                        for c in range(2):
                            k.mm(ps[bank][:, :tn], wgl[:, c, n * 128:(n + 1) * 128], gb[:, c, t0:t0 + tn], c == 0, c == 1, ['wgl', 'gb'], ['ps%d' % bank])
                        k.act(sgl[:, i2, :tn], ps[bank][:, :tn], AF.Sigmoid, ['ps%d' % bank, 'sv'], ['sgl%d' % i2], bias=sv_[:, 2 + n:3 + n])
                        k.tt('dve', ysT[:, n, t0:t0 + tn], gb[:, n, t0:t0 + tn], sgl[:, i2, :tn], ALU.mult, ['gb', 'sgl%d' % i2, 'ysT'], ['ysT'])
                for b, (t0, tn) in enumerate(TB):
                    bank = 6 + b % 2
                    for n in range(2):
                        k.act(sq[:, n, :tn], ysT[:, n, t0:t0 + tn], AF.Square, ['ysT'], ['sqc%d' % n])
                        k.mm(ps[bank][:, :tn], self.ones_bf[:, :], sq[:, n, :tn], n == 0, n == 1, ['sqc%d' % n, 'ones'], ['ps%d' % bank])
                    r = rs[:, b % 2, :tn]
                    rk = 'rsc%d' % (b % 2)
                    k.act(r, ps[bank][:, :tn], AF.Ln, ['ps%d' % bank], [rk], scale=1.0 / 256, bias=EPS)
                    k.act(r, r, AF.Exp, [rk], [rk], scale=-0.5)
                    for n in range(2):
                        k.stt(ycb[:, n, t0:t0 + tn], ysT[:, n, t0:t0 + tn], sv_[:, 4 + n:5 + n], r, ALU.mult, ALU.mult, ['ysT', 'sv', rk], ['ycb'])
                k.barrier()
                self.chk('s5_glu')
                self.add_proj('w_out', l, 768, 2, ycb, 'ycb', po)
                k.barrier()

    def finish(self, raw=False):
        k = self.k
        d = k.dram
        with contextlib.ExitStack() as ph:
            yst = k.sb(ph, [128, 3, D], F32, 'yst')
            if raw:
                src = self.xT
                skey = lambda b: ['xT.%d' % b]
            else:
                src = None
            if not raw:
                xn = k.sb(ph, [128, 8, 512], F32, 'xn')
            blocks = [(128 * i, 128, 0) for i in range(16)] + [(NT - 128, 128, 64)]
            for bi, (t0, tn, r0) in enumerate(blocks):
                b = bkey(t0 + r0)
                sl = bi % 3
                if not raw and ((t0 + r0) % 512 == 0):
                    self._norm_block(b, xn, ph)
                for half in range(2):
                    bank = 2 * sl + half
                    for c in range(4):
                        kk = half * 4 + c
                        if raw:
                            inp = self.xT[:, kk, t0:t0 + tn]
                            rk = ['xT.3', 'xT.4'] if r0 else ['xT.%d' % b]
                        else:
                            if r0:
                                inp = xn[:, kk, 0:128]
                            else:
                                o = t0 - TB[b][0]
                                inp = xn[:, kk, o:o + tn]
                            rk = ['xn']
                        k.tr(self.ps[bank][:tn, c * 128:(c + 1) * 128], inp, self.cid(), rk + ['cst'], ['ps%d' % bank])
                    k.cp('act' if half == 0 else 'dve', yst[:tn, sl, half * 512:(half + 1) * 512], self.ps[bank][:tn, :],
                         ['ps%d' % bank], ['yst%d' % sl])
                if r0 == 0:
                    k.dma('sp', d['y_p'][t0:t0 + tn, :], yst[:tn, sl, :], reads=['yst%d' % sl])
                elif raw:
                    k.dma('sp', d['y_s'][:, :], yst[64:128, sl, :], reads=['yst%d' % sl])
                else:
                    k.dma('sp', d['y_s'][:, :], yst[0:64, sl, :], reads=['yst%d' % sl])
            k.barrier()

    def _norm_block(self, b, xn, ph):
        k = self.k
        t0, tn = TB[b]
        if not hasattr(self, '_nb'):
            self._nb = (k.sb(ph, [128, 2, 512], BF16, 'sqf'), k.sb(ph, [128, 512], F32, 'rsf'))
        sq, rs = self._nb
        bank = 6
        for kk in range(8):
            sl = kk % 2
            k.act(sq[:, sl, :tn], self.xT[:, kk, t0:t0 + tn], AF.Square, ['xT.%d' % b], ['sqf%d' % sl])
            k.mm(self.ps[bank][:, :tn], self.ones_bf[:, :], sq[:, sl, :tn], kk == 0, kk == 7, ['sqf%d' % sl, 'ones'], ['ps6'])
        k.act(rs[:, :tn], self.ps[bank][:, :tn], AF.Ln, ['ps6'], ['rsf'], scale=1.0 / D, bias=EPS)
        k.act(rs[:, :tn], rs[:, :tn], AF.Exp, ['rsf'], ['rsf'], scale=-0.5)
        for kk in range(8):
            k.stt(xn[:, kk, :tn], self.xT[:, kk, t0:t0 + tn], self.gains[:, 6, kk:kk + 1], rs[:, :tn], ALU.mult, ALU.mult,
                  ['xT.%d' % b, 'rsf', 'gains'], ['xn'])

    def build(self):
        k = self.k
        self.setup()
        self.layers()
        k.muted = False
        self.finish(raw=self.flag('raw'))
        k.barrier()
        k.es.close()

    def layers(self):
        k = self.k
        for l in range(L):
            if self.phases is not None and l > 0 and not self.on('l1'):
                break
            if self.on('mixnorm') and not self.on('mlstm'):
                with contextlib.ExitStack() as ph:
                    self.rmsnorm_fm(0 + l, ph)
                    k.barrier()
            if self.on('mlstm'):
                self.phase_mlstm(l)
            if self.on('sb'):
                self.phase_sb(l)
            if self.on('s5'):
                self.phase_s5(l)
            if self.on('cross'):
                self.phase_cross(l)
            if self.on('ffn'):
                self.phase_ffn(l)


def build_program(phases=None, dbg=None):
    nc = bass.Bass("TRN2", target_bir_lowering=False)
    p = Prog(nc, phases, dbg)
    p.build()
    return nc, p


WEIGHT_NAMES = ['ln_mix_g', 'w_in', 'b_in', 'gn_a_g', 'gn_b_g', 'gn_c_g', 'ssm_a_re', 'ssm_a_im', 'ssm_log_dt',
                'ssm_b_re', 'ssm_b_im', 'ssm_c_re', 'ssm_c_im', 'ssm_d', 'w_glu', 'b_glu', 'w_out', 'ln_x_g', 'ln_mem_g',
                'w_xq', 'w_xk', 'w_xv', 'w_xo', 'ln_ffn_g', 'w_ffn_a', 'w_ffn_b', 'ffn_conv_w', 'ffn_conv_b', 'w_ffn_down',
                'ln_f_g']


def make_in_maps(inp, cores):
    f = lambda a: np.ascontiguousarray(np.asarray(a, dtype=np.float32))
    cst = make_consts()
    shared = {n: f(inp[n]) for n in WEIGHT_NAMES}
    maps = []
    for c in cores:
        s2 = slice(2 * c, 2 * c + 2)
        m = dict(shared)
        m['cst'] = cst
        m['xp'] = f(inp['x_prompt'][c])
        m['xs'] = f(np.asarray(inp['x_sample'])[s2].reshape(2 * TSQ, D))
        m['csbk'] = f(np.asarray(inp['cache_sb_k'])[:, s2].reshape(L, 2, 1024, 384))
        m['csbv'] = f(np.asarray(inp['cache_sb_v'])[:, s2].reshape(L, 2, 1024, 384))
        m['smc'] = f(np.asarray(inp['state_mlstm_c'])[:, s2])
        m['smn'] = f(np.asarray(inp['state_mlstm_n'])[:, s2])
        m['smm'] = f(np.asarray(inp['state_mlstm_m'])[:, s2])
        m['ssr'] = f(np.asarray(inp['state_ssm_re'])[:, s2])
        m['ssi'] = f(np.asarray(inp['state_ssm_im'])[:, s2])
        m['sfc'] = f(np.asarray(inp['state_ffn_conv'])[:, s2])
        m['cmk'] = f(np.asarray(inp['cache_mem_k'])[:, s2].reshape(L, 2, 256, D))
        m['cmv'] = f(np.asarray(inp['cache_mem_v'])[:, s2].reshape(L, 2, 256, D))
        m['memp'] = f(inp['mem_prompt'][c])
        maps.append(m)
    return maps


def assemble(results):
    n = len(results)
    g = lambda name: [np.asarray(r[name]) for r in results]
    st1 = lambda name: np.stack(g(name), axis=1)
    cat1 = lambda name: np.concatenate(g(name), axis=1)
    y_prompt = np.stack(g('y_p'), 0)
    y_sample = np.concatenate([a.reshape(2, TSQ, D) for a in g('y_s')], 0)
    p_sb_k = st1('p_sb_k').reshape(L, n, TP, 6, 64)
    p_sb_v = st1('p_sb_v').reshape(L, n, TP, 6, 64)
    p_mem_k = st1('p_mem_k').reshape(L, n, 256, 4, 256)
    p_mem_v = st1('p_mem_v').reshape(L, n, 256, 4, 256)
    s_sb_k = np.concatenate([a.reshape(L, 2, TSQ, 6, 64) for a in g('s_sb_k')], 1)
    s_sb_v = np.concatenate([a.reshape(L, 2, TSQ, 6, 64) for a in g('s_sb_v')], 1)
    outs = (y_prompt, y_sample, p_sb_k, p_sb_v, st1('p_ml_c'), st1('p_ml_n'), st1('p_ml_m'),
            st1('p_ssm_re'), st1('p_ssm_im'), st1('p_ffn_conv'), p_mem_k, p_mem_v,
            s_sb_k, s_sb_v, cat1('s_ml_c'), cat1('s_ml_n'), cat1('s_ml_m'), cat1('s_ssm_re'), cat1('s_ssm_im'),
            cat1('s_ffn_conv'))
    return tuple(np.ascontiguousarray(o.astype(np.float32)) for o in outs)


def kernel(**inputs):
    nc, _ = build_program()
    maps = make_in_maps(inputs, list(range(NCORES)))
    res = run_bass_kernel_spmd(nc, maps, core_ids=list(range(NCORES)))
    return assemble(res.results)
```

```python
import contextlib
import numpy as np
import concourse.bass as bass
import concourse.mybir as mybir
from concourse.bass_utils import run_bass_kernel_spmd

F32 = mybir.dt.float32
BF16 = mybir.dt.bfloat16
AF = mybir.ActivationFunctionType
ALU = mybir.AluOpType
AX = mybir.AxisListType

NCORES = 8
L = 2
D = 1024
TP = 2048
TSQ = 32
NT = TP + 2 * TSQ
EPS = 1e-6
N_IN = 2952
DFF = 2816
NF = 22
TB = [(0, 512), (512, 512), (1024, 512), (1536, 512), (2048, 64)]
TMB = [(128 * i, 128) for i in range(16)] + [(2048, 32), (2080, 32)]
NDS = 40
NEG = -30000.0

C_ID, C_NTI, C_SBNEG, C_SBM, C_MLNEG, C_BLK, C_MISC = 0, 128, 256, 384, 512, 640, 768
NCST = 832
NCBF = 768


def make_consts():
    c = np.zeros((128, NCST), np.float32)
    p = np.arange(128)
    c[:, C_ID:C_ID + 128] = np.eye(128)
    c[:, C_NTI:C_NTI + 128] = -(p[:, None] >= p[None, :]).astype(np.float32)
    c[:, C_SBNEG:C_SBNEG + 128] = NEG * (p[:, None] >= p[None, :])
    c[:, C_SBM:C_SBM + 128] = (p[:, None] < p[None, :]).astype(np.float32)
    c[:, C_MLNEG:C_MLNEG + 128] = NEG * (p[:, None] > p[None, :])
    c[:, C_BLK:C_BLK + 128] = (p[:, None] // 64 == p[None, :] // 64)
    m = C_MISC
    c[:, m + 0] = (p < 64)
    c[:, m + 1] = (p >= 64)
    c[:, m + 2] = (p < 4)
    c[:, m + 3] = (p >= 4) & (p < 8)
    for h in range(4):
        c[:, m + 4 + h] = (p == h) | (p == 4 + h)
    c[:, m + 8] = ((p // 16) % 2 == 0)
    c[:, m + 10] = -1.0 * ((p >= 4) & (p < 8))
    c[:, m + 16] = (p >= 96)
    for h in range(4):
        c[:, m + 12 + h] = (p == h)
    c[:, m + 9] = ((p // 16) % 2 == 1)
    return c


class KB:
    def __init__(self, nc):
        self.nc = nc
        self.es = contextlib.ExitStack()
        self.eng = {'pe': nc.tensor, 'act': nc.scalar, 'dve': nc.vector, 'pool': nc.gpsimd, 'sp': nc.sync}
        self.sem = {}
        self.cnt = {}
        for e in ('pe', 'act', 'dve', 'pool'):
            self.sem[e] = self.es.enter_context(nc.semaphore('s_' + e))
            self.cnt[e] = 0
        self.dsem = [self.es.enter_context(nc.semaphore('d%d' % i)) for i in range(NDS)]
        self.dcnt = [0] * NDS
        self.dnext = 0
        self.waited = {e: {} for e in self.eng}
        self.lastw = {}
        self.readers = {}
        self.nalloc = 0
        self.dram = {}
        self.bank_rr = 0
        self.muted = False

    def sb(self, stack, shape, dt, name=None):
        self.nalloc += 1
        return stack.enter_context(self.nc.sbuf_tensor('%s_%d' % (name or 't', self.nalloc), list(shape), dt))

    def din(self, name, shape, dt=F32):
        t = self.nc.dram_tensor(name, list(shape), dt, kind="ExternalInput").ap()
        self.dram[name] = t
        return t

    def dout(self, name, shape, dt=F32):
        t = self.nc.dram_tensor(name, list(shape), dt, kind="ExternalOutput").ap()
        self.dram[name] = t
        return t

    def _h(self, s):
        return self.sem[s[1]] if s[0] == 'e' else self.dsem[s[1]]

    def _deps(self, reads, writes):
        need = {}

        def add(sv):
            if sv is None:
                return
            s, v = sv
            if need.get(s, 0) < v:
                need[s] = v
        for r in reads:
            for s, v in self.lastw.get(r, {}).items():
                add((s, v))
        for w in writes:
            for s, v in self.lastw.get(w, {}).items():
                add((s, v))
            for s, v in self.readers.get(w, {}).items():
                add((s, v))
        return need

    def _wait(self, e, need):
        for s, v in need.items():
            if e == 'pe' and s == ('e', 'pe'):
                continue
            if self.waited[e].get(s, 0) >= v:
                continue
            self.eng[e].wait_ge(self._h(s), v)
            self.waited[e][s] = v

    def _commit(self, sv, reads, writes):
        for w in writes:
            self.lastw.setdefault(w, {})[sv[0]] = sv[1]
            self.readers[w] = {}
        for r in reads:
            if r in writes:
                continue
            d = self.readers.setdefault(r, {})
            d[sv[0]] = max(d.get(sv[0], 0), sv[1])

    def op(self, e, fn, reads=(), writes=()):
        if self.muted:
            return
        self._wait(e, self._deps(reads, writes))
        ins = fn(self.eng[e])
        self.cnt[e] += 1
        ins.then_inc(self.sem[e], 1)
        self._commit((('e', e), self.cnt[e]), reads, writes)

    def dma(self, q, out, in_, reads=(), writes=(), **kw):
        if self.muted:
            return
        i = self.dnext
        self.dnext = (i + 1) % NDS
        need = self._deps(reads, writes)
        if self.dcnt[i] > 0:
            need[('d', i)] = max(need.get(('d', i), 0), self.dcnt[i])
        self._wait(q, need)
        ins = self.eng[q].dma_start(out=out, in_=in_, **kw)
        self.dcnt[i] += 16
        ins.then_inc(self.dsem[i], 16)
        self._commit((('d', i), self.dcnt[i]), reads, writes)

    def barrier(self, engines=('pe', 'act', 'dve', 'pool', 'sp')):
        if self.muted:
            return
        need = {}
        for e in ('pe', 'act', 'dve', 'pool'):
            if self.cnt[e] > 0:
                need[('e', e)] = self.cnt[e]
        for i in range(NDS):
            if self.dcnt[i] > 0:
                need[('d', i)] = self.dcnt[i]
        for e in engines:
            n2 = dict(need)
            self._wait(e, n2)

    def mm(self, out, lhsT, rhs, start, stop, reads, writes):
        self.op('pe', lambda e: e.matmul(out, lhsT=lhsT, rhs=rhs, start=start, stop=stop), reads, writes)

    def tr(self, out, in_, ident, reads, writes):
        self.op('pe', lambda e: e.transpose(out=out, in_=in_, identity=ident), reads, writes)

    def act(self, out, in_, func, reads, writes, **kw):
        self.op('act', lambda e: e.activation(out=out, in_=in_, func=func, **kw), reads, writes)

    def tt(self, eng, out, in0, in1, op, reads, writes):
        self.op(eng, lambda e: e.tensor_tensor(out=out, in0=in0, in1=in1, op=op), reads, writes)

    def ts(self, eng, out, in0, s1, s2, op0, op1, reads, writes):
        if s2 is None:
            self.op(eng, lambda e: e.tensor_scalar(out=out, in0=in0, scalar1=s1, scalar2=None, op0=op0), reads, writes)
        else:
            self.op(eng, lambda e: e.tensor_scalar(out=out, in0=in0, scalar1=s1, scalar2=s2, op0=op0, op1=op1), reads, writes)

    def stt(self, out, in0, scalar, in1, op0, op1, reads, writes):
        self.op('dve', lambda e: e.scalar_tensor_tensor(out=out, in0=in0, scalar=scalar, in1=in1, op0=op0, op1=op1), reads, writes)

    def cp(self, eng, out, in_, reads, writes):
        if eng == 'act':
            self.act(out, in_, AF.Copy, reads, writes)
        else:
            self.op(eng, lambda e: e.tensor_copy(out=out, in_=in_), reads, writes)

    def memset(self, eng, ap, val, writes):
        self.op(eng, lambda e: e.memset(ap, val), (), writes)


def bkey(t0):
    return min(t0 // 512, 4)


class Stop(Exception):
    pass


class Prog:
    def chk(self, tag):
        if self.phases is not None and ('stop:' + tag) in self.phases:
            self.k.barrier()
            self.k.muted = True

    def __init__(self, nc, phases=None, dbg=None):
        self.nc = nc
        self.k = KB(nc)
        self.phases = phases
        self.n_dummy = 0
        self.burst_every = 24
        self.dbg = dbg or {}
        self.declare_io()

    def on(self, name):
        return self.phases is None or name in self.phases

    def flag(self, name):
        return self.phases is not None and name in self.phases

    def declare_io(self):
        k = self.k
        i = k.din
        i('xp', [TP, D]); i('xs', [2 * TSQ, D])
        i('csbk', [L, 2, 1024, 384]); i('csbv', [L, 2, 1024, 384])
        i('smc', [L, 2, 4, 96, 96]); i('smn', [L, 2, 4, 96]); i('smm', [L, 2, 4])
        i('ssr', [L, 2, 16, 64]); i('ssi', [L, 2, 16, 64])
        i('sfc', [L, 2, 2, DFF])
        i('cmk', [L, 2, 256, D]); i('cmv', [L, 2, 256, D])
        i('memp', [256, D])
        i('ln_mix_g', [L, D]); i('w_in', [L, D, N_IN]); i('b_in', [L, N_IN])
        i('gn_a_g', [L, 384]); i('gn_b_g', [L, 384]); i('gn_c_g', [L, 256])
        for n in ('ssm_a_re', 'ssm_a_im', 'ssm_log_dt'):
            i(n, [L, 16, 64])
        i('ssm_b_re', [L, 16, 64, 16]); i('ssm_b_im', [L, 16, 64, 16])
        i('ssm_c_re', [L, 16, 16, 64]); i('ssm_c_im', [L, 16, 16, 64])
        i('ssm_d', [L, 256]); i('w_glu', [L, 256, 256]); i('b_glu', [L, 256])
        i('w_out', [L, D, D]); i('ln_x_g', [L, D]); i('ln_mem_g', [L, D])
        for n in ('w_xq', 'w_xk', 'w_xv', 'w_xo'):
            i(n, [L, D, D])
        i('ln_ffn_g', [L, D]); i('w_ffn_a', [L, D, DFF]); i('w_ffn_b', [L, D, DFF])
        i('ffn_conv_w', [L, 3, DFF]); i('ffn_conv_b', [L, DFF]); i('w_ffn_down', [L, DFF, D])
        i('ln_f_g', [D]); i('cst', [128, NCST])
        o = k.dout
        o('y_p', [TP, D]); o('y_s', [2 * TSQ, D])
        o('p_sb_k', [L, TP, 384]); o('p_sb_v', [L, TP, 384])
        o('p_ml_c', [L, 4, 96, 96]); o('p_ml_n', [L, 4, 96]); o('p_ml_m', [L, 4])
        o('p_ssm_re', [L, 16, 64]); o('p_ssm_im', [L, 16, 64])
        o('p_ffn_conv', [L, 2, DFF]); o('p_mem_k', [L, 256, D]); o('p_mem_v', [L, 256, D])
        o('s_sb_k', [L, 2 * TSQ, 384]); o('s_sb_v', [L, 2 * TSQ, 384])
        o('s_ml_c', [L, 2, 4, 96, 96]); o('s_ml_n', [L, 2, 4, 96]); o('s_ml_m', [L, 2, 4])
        o('s_ssm_re', [L, 2, 16, 64]); o('s_ssm_im', [L, 2, 16, 64])
        o('s_ffn_conv', [L, 2, 2, DFF])
        for name, shape in self.dbg.items():
            o(name, shape)

    def setup(self):
        k, nc = self.k, self.nc
        es = k.es
        d = k.dram
        self.xT = k.sb(es, [128, 8, NT], F32, 'xT')
        self.hT = k.sb(es, [128, 8, NT], BF16, 'hT')
        self.cst = k.sb(es, [128, NCST], F32, 'cst')
        self.cbf = k.sb(es, [128, NCBF], BF16, 'cbf')
        self.ones_bf = k.sb(es, [128, 128], BF16, 'ones')
        self.nones_bf = k.sb(es, [128, 128], BF16, 'nones')
        self.ones32 = k.sb(es, [128, 128], F32, 'ones32')
        self.gains = k.sb(es, [128, 7, 8], F32, 'gains')
        self.ps = [es.enter_context(nc.psum_tensor('ps%d' % i, [128, 512], F32)) for i in range(8)]
        k.dma('sp', self.cst[:], d['cst'][:, :], writes=['cst'])
        k.cp('dve', self.cbf[:], self.cst[:, 0:NCBF], ['cst'], ['cbf'])
        k.memset('dve', self.ones_bf[:], 1.0, ['ones'])
        k.memset('dve', self.nones_bf[:], -1.0, ['nones'])
        k.memset('dve', self.ones32[:], 1.0, ['ones32'])
        with nc.allow_non_contiguous_dma(reason="small gain vectors"):
            for j, (nm, l) in enumerate([('ln_mix_g', 0), ('ln_mix_g', 1), ('ln_x_g', 0), ('ln_x_g', 1),
                                         ('ln_ffn_g', 0), ('ln_ffn_g', 1)]):
                k.dma('sp', self.gains[:, j, :], d[nm][l].rearrange("(k p) -> p k", p=128), writes=['gains'])
            k.dma('sp', self.gains[:, 6, :], d['ln_f_g'].rearrange("(k p) -> p k", p=128), writes=['gains'])
        self.load_xT()

    def cid(self):
        return self.cst[:, C_ID:C_ID + 128]

    def load_xT(self):
        k = self.k
        d = k.dram
        with contextlib.ExitStack() as ph:
            stg = k.sb(ph, [128, 4, D], F32, 'xstg')
            for bi, (t0, tn) in enumerate(TMB):
                sl = bi % 4
                src = d['xp'][t0:t0 + tn, :] if t0 < TP else d['xs'][t0 - TP:t0 - TP + tn, :]
                k.dma('sp', stg[:tn, sl, :], src, writes=['xstg%d' % sl])
                for half in range(2):
                    bank = 2 * sl + half
                    for c in range(4):
                        kk = half * 4 + c
                        k.tr(self.ps[bank][:, c * 128:c * 128 + tn], stg[:tn, sl, kk * 128:(kk + 1) * 128],
                             self.cst[:tn, C_ID:C_ID + tn], ['xstg%d' % sl, 'cst'], ['ps%d' % bank])
                    src_ps = self.ps[bank][:, :].rearrange("p (c t) -> p c t", c=4)[:, :, :tn]
                    k.cp('act' if half == 0 else 'dve', self.xT[:, half * 4:half * 4 + 4, t0:t0 + tn], src_ps,
                         ['ps%d' % bank], ['xT.%d' % bkey(t0)])
            k.barrier()

    def rmsnorm_fm(self, gidx, ph, out_fn=None, out_keys=None):
        k = self.k
        sq = k.sb(ph, [128, 2, 512], BF16, 'sq')
        rs = k.sb(ph, [128, 2, 512], F32, 'rs')
        n = 0
        for b, (t0, tn) in enumerate(TB):
            bank = b % 2
            for kk in range(8):
                sl = n % 2
                n += 1
                k.act(sq[:, sl, :tn], self.xT[:, kk, t0:t0 + tn], AF.Square, ['xT.%d' % b], ['sq%d' % sl])
                k.mm(self.ps[bank][:, :tn], self.ones_bf[:, :], sq[:, sl, :tn], kk == 0, kk == 7,
                     ['sq%d' % sl, 'ones'], ['ps%d' % bank])
            r = rs[:, b % 2, :tn]
            rk = 'rs%d' % (b % 2)
            k.act(r, self.ps[bank][:, :tn], AF.Ln, ['ps%d' % bank], [rk], scale=1.0 / D, bias=EPS)
            k.act(r, r, AF.Exp, [rk], [rk], scale=-0.5)
            for kk in range(8):
                if out_fn is None:
                    dst, wk = self.hT[:, kk, t0:t0 + tn], ['hT.%d' % b]
                else:
                    dst, wk = out_fn(kk, t0, tn), out_keys(b)
                k.stt(dst, self.xT[:, kk, t0:t0 + tn], self.gains[:, gidx, kk:kk + 1], r, ALU.mult, ALU.mult,
                      ['xT.%d' % b, rk, 'gains'], wk)

    def add_proj(self, wname, l, row0, nch, yT, ykey, ph, banks=(6, 7), wt=None):
        k = self.k
        if wt is None:
            wt = k.sb(ph, [128, 2, nch, 256], BF16, 'wo')
        src = k.dram[wname][l][row0:row0 + 128 * nch, :].rearrange("(c p) n -> p c n", p=128)
        j = 0
        for n2 in range(4):
            sl = n2 % 2
            k.dma('pool', wt[:, sl, :, :], src[:, :, n2 * 256:(n2 + 1) * 256], writes=['wo%d' % sl])
            for nn in range(2):
                n = 2 * n2 + nn
                for b, (t0, tn) in enumerate(TB):
                    bank = banks[j % len(banks)]
                    j += 1
                    for c in range(nch):
                        k.mm(self.ps[bank][:, :tn], wt[:, sl, c, nn * 128:(nn + 1) * 128], yT[:, c, t0:t0 + tn], c == 0, c == nch - 1,
                             ['wo%d' % sl, ykey], ['ps%d' % bank])
                    k.tt('dve', self.xT[:, n, t0:t0 + tn], self.xT[:, n, t0:t0 + tn], self.ps[bank][:, :tn], ALU.add,
                         ['ps%d' % bank, 'xT.%d' % b], ['xT.%d' % b])

    def phase_sb(self, l):
        k = self.k
        d = k.dram
        ps = self.ps
        cb = self.cbf
        with contextlib.ExitStack() as ph:
            qT = k.sb(ph, [128, 3, NT], BF16, 'qT')
            kT = k.sb(ph, [128, 3, NT + 128], BF16, 'kT')
            k.memset('pool', kT[:, :, NT:NT + 128], 0.0, ['kT'])
            vB = k.sb(ph, [128, 16, 384], BF16, 'vB')
            vS = k.sb(ph, [32, 2, 384], BF16, 'vS')
            ybT = k.sb(ph, [128, 3, NT], BF16, 'ybT')
            kTp = k.sb(ph, [128, 2, 3, 1024], BF16, 'kTp')
            vP = k.sb(ph, [128, 2, 8, 384], BF16, 'vP')
            bfm = k.sb(ph, [128, 6], F32, 'bfm')
            gb = k.sb(ph, [128, 3], F32, 'gb')
            with contextlib.ExitStack() as ph2:
                wsb = k.sb(ph2, [128, 8, 1152], BF16, 'wsb')
                brow = k.sb(ph2, [1, 768], BF16, 'brow')
                kvst = k.sb(ph2, [128, 2, 768], F32, 'kvst')
                wsrc = d['w_in'][l].rearrange("(c p) n -> p c n", p=128)
                for c0 in range(0, 8, 2):
                    k.dma('pool', wsb[:, c0:c0 + 2, :], wsrc[:, c0:c0 + 2, 1544:2696], writes=['wsb%d' % (c0 // 2)])
                k.dma('pool', brow[:, :], d['b_in'][l:l + 1, 1928:2696], writes=['brow'])
                with self.nc.allow_non_contiguous_dma(reason="small vectors"):
                    k.dma('sp', bfm[:, :], d['b_in'][l, 1544:2312].rearrange("(c p) -> p c", p=128), writes=['bfm'])
                    k.dma('sp', gb[:, :], d['gn_b_g'][l].rearrange("(c p) -> p c", p=128), writes=['gb'])
                for j in range(2):
                    k.dma('pool', vP[:, j, :, :], d['csbv'][l, j].rearrange("(n p) f -> p n f", p=128), writes=['vP%d' % j])
                self.chk('sb_load')
                n = 0
                for c in range(6):
                    for b, (t0, tn) in enumerate(TB):
                        bank = n % 2
                        n += 1
                        for kk in range(8):
                            k.mm(ps[bank][:, :tn], wsb[:, kk, c * 128:(c + 1) * 128], self.hT[:, kk, t0:t0 + tn], kk == 0, kk == 7,
                                 ['wsb%d' % (kk // 2), 'hT.%d' % b], ['ps%d' % bank])
                        if c < 3:
                            k.ts('dve', qT[:, c, t0:t0 + tn], ps[bank][:, :tn], bfm[:, c:c + 1], 0.125, ALU.add, ALU.mult,
                                 ['ps%d' % bank, 'bfm'], ['qT'])
                        else:
                            k.act(kT[:, c - 3, t0:t0 + tn], ps[bank][:, :tn], AF.Identity, ['ps%d' % bank, 'bfm'], ['kT'],
                                  bias=bfm[:, c:c + 1])
                self.chk('sb_fm')
                for bi, (t0, tn) in enumerate(TMB):
                    sl = bi % 2
                    b = bkey(t0)
                    for part in range(2):
                        bank = 2 + 2 * sl + part
                        for kk in range(8):
                            k.mm(ps[bank][:tn, 0:384], self.hT[:, kk, t0:t0 + tn], wsb[:, kk, 384 + 384 * part:768 + 384 * part],
                                 kk == 0, False, ['wsb%d' % (kk // 2), 'hT.%d' % b], ['ps%d' % bank])
                        k.mm(ps[bank][:tn, 0:384], self.ones_bf[0:1, :tn], brow[0:1, 384 * part:384 * part + 384], False, True,
                             ['ones', 'brow'], ['ps%d' % bank])
                        k.cp('act' if part == 0 else 'dve', kvst[:tn, sl, 384 * part:384 * part + 384], ps[bank][:tn, 0:384],
                             ['ps%d' % bank], ['kvst%d' % sl])
                    if t0 < TP:
                        k.cp('pool', vB[:, bi, :], kvst[:, sl, 384:768], ['kvst%d' % sl], ['vB'])
                        k.dma('sp', d['p_sb_k'][l, t0:t0 + tn, :], kvst[:tn, sl, 0:384], reads=['kvst%d' % sl])
                        k.dma('sp', d['p_sb_v'][l, t0:t0 + tn, :], kvst[:tn, sl, 384:768], reads=['kvst%d' % sl])
                    else:
                        j = (t0 - TP) // TSQ
                        k.cp('pool', vS[:, j, :], kvst[:tn, sl, 384:768], ['kvst%d' % sl], ['vB'])
                        k.dma('sp', d['s_sb_k'][l, t0 - TP:t0 - TP + tn, :], kvst[:tn, sl, 0:384], reads=['kvst%d' % sl])
                        k.dma('sp', d['s_sb_v'][l, t0 - TP:t0 - TP + tn, :], kvst[:tn, sl, 384:768], reads=['kvst%d' % sl])
                k.barrier()
            self.chk('sb_tm')
            with contextlib.ExitStack() as ph3:
                e32 = k.sb(ph3, [128, 3, 512], F32, 'e32')
                Lp = k.sb(ph3, [128, 3, 512], BF16, 'Lp')
                At = k.sb(ph3, [128, 2, 512], BF16, 'At')
                sqo = k.sb(ph3, [128, 2, 512], BF16, 'sqo')
                rso = k.sb(ph3, [128, 2, 512], F32, 'rso')
                self._tile_n = 0
                self._grp_n = 0
                k.memset('pool', sqo[:], 0.0, ['sqo0', 'sqo1'])
                qTs = k.sb(ph3, [128, 3, 2, 2 * TSQ], BF16, 'qTs')
                for par in range(2):
                    k.ts('dve', qTs[:, :, par, :], qT[:, :, TP:NT], self.cst[:, C_MISC + par:C_MISC + par + 1], None, ALU.mult, None, ['qT', 'cst'], ['qTs'])
                with contextlib.ExitStack() as phk:
                    kst = k.sb(phk, [128, 4, 384], F32, 'kst')
                    n = 0
                    for j in range(2):
                        for kb in range(8):
                            sl = n % 4
                            n += 1
                            k.dma('sp', kst[:, sl, :], d['csbk'][l, j, kb * 128:(kb + 1) * 128, :], writes=['kst%d' % sl])
                            bank = 4 + sl
                            for c in range(3):
                                k.tr(ps[bank][:, c * 128:(c + 1) * 128], kst[:, sl, c * 128:(c + 1) * 128], self.cid(),
                                     ['kst%d' % sl, 'cst'], ['ps%d' % bank])
                            k.cp('act', kTp[:, j, :, kb * 128:(kb + 1) * 128],
                                 ps[bank][:, 0:384].rearrange("p (c t) -> p c t", c=3), ['ps%d' % bank], ['kTp%d' % j])
                    k.barrier()

                Ls = k.sb(ph3, [128, 2, 4, 512], BF16, 'Ls')
                tiles = []

                def group(segs, W, kblocks, fin):
                    g = self._grp_n % 2
                    self._grp_n += 1
                    obank = 4 + g
                    okey = 'ps%d' % obank
                    touched = set()
                    nkb = len(kblocks)
                    for bi, (kbid, nk, clo, diag) in enumerate(kblocks):
                        first, last = bi == 0, bi == nkb - 1
                        t = self._tile_n % 3
                        t2 = self._tile_n % 2
                        sbank = abank = self._tile_n % 4
                        self._tile_n += 1
                        sk = ak = 'ps%d' % sbank
                        ek, lk, atk = 'e32%d' % t, 'Lp%d' % t, 'At%d' % t2

                        def stage1(first=first, last=last, bi=bi, kbid=kbid, nk=nk, clo=clo, diag=diag, t=t, sbank=sbank, sk=sk, ek=ek, lk=lk):
                            if first:
                                k.memset('pool', Ls[:, g, :, :W], 0.0, ['Ls%d_0' % g, 'Ls%d_1' % g, 'Ls%d_2' % g, 'Ls%d_3' % g])
                            for si, (c0, ncol, q_ap, pb, kfn, vfn) in enumerate(segs):
                                lo = clo if len(segs) == 1 else 0
                                k.mm(ps[sbank][:, c0 + lo:c0 + ncol], kfn(kbid), q_ap[:, lo:ncol], si == 0, False,
                                     ['qT', 'qTs', 'kT', 'kTp0', 'kTp1'], [sk])
                            k.act(e32[:nk, t, clo:W], ps[sbank][:nk, clo:W], AF.Exp, [sk], [ek])

                        def stage1b(first=first, last=last, bi=bi, kbid=kbid, nk=nk, clo=clo, diag=diag, t=t, sbank=sbank, sk=sk, ek=ek, lk=lk):
                            k.act(Lp[:nk, t, clo:W], e32[:nk, t, clo:W], AF.Ln, [ek], [lk], bias=1.0)
                            if diag is not None:
                                dc, dn = diag
                                for (c0, ncol, q_ap, pb, kfn, vfn) in segs:
                                    k.tt('pool', Lp[:dn, t, c0 + dc:c0 + dc + dn], Lp[:dn, t, c0 + dc:c0 + dc + dn],
                                         cb[:dn, C_SBM:C_SBM + dn], ALU.mult, [lk, 'cbf'], [lk])
                            if not last:
                                k.tt('pool', Ls[:nk, g, (bi + 1) % 4, clo:W], Ls[:nk, g, bi % 4, clo:W], Lp[:nk, t, clo:W], ALU.add,
                                     ['Ls%d_%d' % (g, bi % 4), lk], ['Ls%d_%d' % (g, (bi + 1) % 4)])

                        def stage2(first=first, last=last, bi=bi, kbid=kbid, nk=nk, clo=clo, diag=diag, t=t, t2=t2, abank=abank, ak=ak, lk=lk, atk=atk):
                            k.mm(ps[abank][:, clo:W], cb[:nk, C_NTI:C_NTI + 128], Lp[:nk, t, clo:W], False, first and diag is None,
                                 [lk, 'cbf'], [ak])
                            if not first:
                                k.mm(ps[abank][:, clo:W], self.nones_bf[:, :], Ls[:, g, bi % 4, clo:W], False, diag is None,
                                     ['Ls%d_%d' % (g, bi % 4), 'nones'], [ak])
                            if diag is not None:
                                dc, dn = diag
                                for si, (c0, ncol, q_ap, pb, kfn, vfn) in enumerate(segs):
                                    k.mm(ps[abank][:, c0 + dc:c0 + dc + dn], cb[:dn, C_ID:C_ID + 128], cb[:dn, C_SBNEG:C_SBNEG + dn],
                                         False, si == len(segs) - 1, ['cbf'], [ak])
                            k.act(At[:nk, t2, clo:W], ps[abank][:nk, clo:W], AF.Exp, [ak], [atk])

                        def stage2b(first=first, last=last, bi=bi, kbid=kbid, nk=nk, clo=clo, diag=diag, t=t, t2=t2, abank=abank, ak=ak, lk=lk, atk=atk):
                            for si, (c0, ncol, q_ap, pb, kfn, vfn) in enumerate(segs):
                                lo = clo if len(segs) == 1 else 0
                                st = pb not in touched
                                touched.add(pb)
                                k.mm(ps[obank][pb:pb + 64, c0 + lo:c0 + ncol], vfn(kbid), At[:nk, t2, c0 + lo:c0 + ncol], st, last,
                                     [atk, 'vB', 'vP0', 'vP1'], [okey])

                        epiA = epiB = None
                        if last:
                            pbs = sorted(set(s_[3] for s_ in segs))
                            sbk = 6 + g

                            def epiA(pbs=pbs, sbk=sbk):
                                for pb in pbs:
                                    k.act(sqo[pb:pb + 64, g, :W], ps[obank][pb:pb + 64, :W], AF.Square, [okey], ['sqo%d' % g])
                                k.mm(ps[sbk][:, :W], cb[:, C_BLK:C_BLK + 128], sqo[:, g, :W], True, True, ['sqo%d' % g, 'cbf'], ['ps%d' % sbk])

                            def epiB(pbs=pbs, sbk=sbk):
                                for pb in pbs:
                                    k.act(rso[pb:pb + 64, g, :W], ps[sbk][pb:pb + 64, :W], AF.Ln, ['ps%d' % sbk], ['rso%d' % g],
                                          scale=1.0 / 64, bias=EPS)
                                    k.act(rso[pb:pb + 64, g, :W], rso[pb:pb + 64, g, :W], AF.Exp, ['rso%d' % g], ['rso%d' % g], scale=-0.5)
                                fin(obank, okey, rso, g)
                        tiles.append((stage1, stage2, epiA, epiB, stage1b, stage2b))

                def run_pipeline():
                    pendA = pendB = None
                    n = len(tiles)
                    def epi_step(i):
                        nonlocal pendA, pendB, pendA_b
                        if pendB is not None:
                            pendB()
                            pendB = None
                        if pendA is not None:
                            pendA()
                            pendB = pendA_b
                            pendA = None
                        if i >= 0 and tiles[i][2] is not None:
                            pendA, pendA_b = tiles[i][2], tiles[i][3]
                    pendA_b = None
                    for j in range(min(3, n)):
                        tiles[j][0]()
                    for j in range(min(2, n)):
                        tiles[j][4]()
                    for i in range(n + 1):
                        if i + 3 < n:
                            tiles[i + 3][0]()
                        if i + 2 < n:
                            tiles[i + 2][4]()
                        if i < n:
                            tiles[i][1]()
                        if i >= 1:
                            tiles[i - 1][5]()
                            epi_step(i - 1)
                    epi_step(-1)
                    epi_step(-1)
                    del tiles[:]

                for h in range(6 if self.on('sb_prompt') else 0):
                    c, pb = h // 2, 64 * (h % 2)
                    for qg in range(4):
                        q0 = 512 * qg
                        seg = (0, 512, qT[pb:pb + 64, c, q0:q0 + 512], pb,
                               lambda kb, c=c, pb=pb: kT[pb:pb + 64, c, kb * 128:(kb + 1) * 128],
                               lambda kb, h=h: vB[:, kb, 64 * h:64 * h + 64])
                        kbl = []
                        for kb in range(4 * qg + 3, -1, -1):
                            r = kb - 4 * qg
                            if r >= 0:
                                kbl.append((kb, 128, 128 * r, (128 * r, 128)))
                            else:
                                kbl.append((kb, 128, 0, None))

                        def fin(obank, okey, rso, g, c=c, pb=pb, q0=q0):
                            k.stt(ybT[pb:pb + 64, c, q0:q0 + 512], ps[obank][pb:pb + 64, :512], gb[pb:pb + 64, c:c + 1],
                                  rso[pb:pb + 64, g, :512], ALU.mult, ALU.mult, [okey, 'rso%d' % g, 'gb'], ['ybT'])
                        group([seg], 512, kbl, fin)
                for j in range(2):
                    q0 = TP + TSQ * j
                    segs = []
                    for h in range(6):
                        c, pb = h // 2, 64 * (h % 2)

                        def kfn(kb, c=c, pb=pb, j=j, q0=q0):
                            if kb == 8:
                                return kT[:, c, q0:q0 + 128]
                            return kTp[:, j, c, kb * 128:(kb + 1) * 128]

                        def vfn(kb, h=h, j=j):
                            if kb == 8:
                                return vS[:, j, 64 * h:64 * h + 64]
                            return vP[:, j, kb, 64 * h:64 * h + 64]
                        segs.append((32 * h, 32, qTs[:, c, h % 2, TSQ * j:TSQ * j + TSQ], pb, kfn, vfn))
                    kbl = [(8, 32, 0, (0, 32))] + [(kb, 128, 0, None) for kb in range(7, -1, -1)]

                    def fin(obank, okey, rso, g, q0=q0):
                        for h in range(6):
                            c, pb = h // 2, 64 * (h % 2)
                            k.stt(ybT[pb:pb + 64, c, q0:q0 + TSQ], ps[obank][pb:pb + 64, 32 * h:32 * h + 32], gb[pb:pb + 64, c:c + 1],
                                  rso[pb:pb + 64, g, 32 * h:32 * h + 32], ALU.mult, ALU.mult, [okey, 'rso%d' % g, 'gb'], ['ybT'])
                    group(segs, 192, kbl, fin)
                run_pipeline()
                k.barrier()
            if 'ybT' in self.dbg:
                k.dbg_dump = True
                for c in range(3):
                    k.dma('pool', d['ybT'][c], ybT[:, c, :], reads=['ybT'])
                k.barrier()
            self.add_proj('w_out', l, 384, 3, ybT, 'ybT', ph)
            k.barrier()

    def phase_cross(self, l):
        k = self.k
        d = k.dram
        ps = self.ps
        with contextlib.ExitStack() as ph:
            self.rmsnorm_fm(2 + l, ph)
            mkT = k.sb(ph, [128, 3, 8, 256], BF16, 'mkT')
            mv = k.sb(ph, [128, 3, 2, D], BF16, 'mv')
            gm = k.sb(ph, [128, 8], F32, 'gm')
            with self.nc.allow_non_contiguous_dma(reason="small vectors"):
                k.dma('sp', gm[:, :], d['ln_mem_g'][l].rearrange("(c p) -> p c", p=128), writes=['gm'])
            for j in range(2):
                k.dma('pool', mv[:, 1 + j, :, :], d['cmv'][l, j].rearrange("(n p) f -> p n f", p=128), writes=['mv'])
            with contextlib.ExitStack() as ph2:
                mst2 = [k.sb(ph2, [128, 2, D], F32, 'mst%d' % q) for q in range(2)]
                mnT = k.sb(ph2, [128, 8, 256], BF16, 'mnT')
                ss = k.sb(ph2, [128, 4], F32, 'ss')
                junk = k.sb(ph2, [128, D], BF16, 'junk')
                ost = k.sb(ph2, [128, 2, 512], F32, 'ost')
                wk = k.sb(ph2, [128, 8, D], BF16, 'wk')
                wv = k.sb(ph2, [128, 8, D], BF16, 'wv')
                for (wt_, nm) in ((wk, 'w_xk'), (wv, 'w_xv')):
                    src = d[nm][l].rearrange("(c p) n -> p c n", p=128)
                    for c0 in range(0, 8, 2):
                        k.dma('pool', wt_[:, c0:c0 + 2, :], src[:, c0:c0 + 2, :], writes=['%s%d' % (nm, c0 // 2)])
                for j in range(2):
                    mst, mk_ = mst2[j], 'mst%d' % j
                    k.dma('sp', mst[:, :, :], d['cmk'][l, j].rearrange("(n p) f -> p n f", p=128), writes=[mk_])
                    for mb in range(2):
                        for half in range(2):
                            bank = 2 * mb + half
                            for c in range(4):
                                kk = half * 4 + c
                                k.tr(ps[bank][:, c * 128:(c + 1) * 128], mst[:, mb, kk * 128:(kk + 1) * 128], self.cid(),
                                     [mk_, 'cst'], ['ps%d' % bank])
                            k.cp('act' if half == 0 else 'dve', mkT[:, 1 + j, half * 4:half * 4 + 4, mb * 128:(mb + 1) * 128],
                                 ps[bank][:, :].rearrange("p (c t) -> p c t", c=4), ['ps%d' % bank], ['mkT'])
                self.chk('x_a')
                mst = mst2[0]
                k.dma('sp', mst[:, :, :], d['memp'].rearrange("(n p) f -> p n f", p=128), writes=['mst', 'mst0'])
                for mb in range(2):
                    k.act(junk[:, :], mst[:, mb, :], AF.Square, ['mst'], ['junk', 'ss'], accum_out=ss[:, mb:mb + 1])
                k.act(ss[:, 2:4], ss[:, 0:2], AF.Ln, ['ss'], ['ss2'], scale=1.0 / D, bias=EPS)
                k.act(ss[:, 2:4], ss[:, 2:4], AF.Exp, ['ss2'], ['ss2'], scale=-0.5)
                for mb in range(2):
                    k.ts('dve', mst[:, mb, :], mst[:, mb, :], ss[:, 2 + mb:3 + mb], None, ALU.mult, None, ['mst', 'ss2'], ['mst'])
                for mb in range(2):
                    for half in range(2):
                        bank = 4 + 2 * mb + half
                        for c in range(4):
                            kk = half * 4 + c
                            k.tr(ps[bank][:, c * 128:(c + 1) * 128], mst[:, mb, kk * 128:(kk + 1) * 128], self.cid(),
                                 ['mst', 'cst'], ['ps%d' % bank])
                        for c in range(4):
                            kk = half * 4 + c
                            k.ts('dve', mnT[:, kk, mb * 128:(mb + 1) * 128], ps[bank][:, c * 128:(c + 1) * 128], gm[:, kk:kk + 1], None,
                                 ALU.mult, None, ['ps%d' % bank, 'gm'], ['mnT'])
                self.chk('x_b')
                for n in range(8):
                    bank = n % 2
                    for kk in range(8):
                        k.mm(ps[bank][:, 0:256], wk[:, kk, n * 128:(n + 1) * 128], mnT[:, kk, :], kk == 0, kk == 7,
                             ['w_xk%d' % (kk // 2), 'mnT'], ['ps%d' % bank])
                    k.cp('act', mkT[:, 0, n, :], ps[bank][:, 0:256], ['ps%d' % bank], ['mkT'])
                self.chk('x_c')
                n = 0
                for (wt_, nm, onm) in ((wk, 'w_xk', 'p_mem_k'), (wv, 'w_xv', 'p_mem_v')):
                    for mb in range(2):
                        for nh in range(2):
                            bank = 2 + n % 2
                            sl = n % 2
                            n += 1
                            for kk in range(8):
                                k.mm(ps[bank][:, :], mnT[:, kk, mb * 128:(mb + 1) * 128], wt_[:, kk, nh * 512:(nh + 1) * 512], kk == 0, kk == 7,
                                     ['%s%d' % (nm, kk // 2), 'mnT'], ['ps%d' % bank])
                            k.cp('act', ost[:, sl, :], ps[bank][:, :], ['ps%d' % bank], ['ost%d' % sl])
                            if nm == 'w_xv' and not self.flag('x_nomv'):
                                k.cp('pool', mv[:, 0, mb, nh * 512:(nh + 1) * 512], ost[:, sl, :], ['ost%d' % sl], ['mv'])
                            if not self.flag('x_nodma'):
                                k.dma('sp', d[onm][l, mb * 128:(mb + 1) * 128, nh * 512:(nh + 1) * 512], ost[:, sl, :], reads=['ost%d' % sl])
                            self.chk('x_c%d' % n)
                k.barrier()
            self.chk('x_d')
            oT = k.sb(ph, [128, 8, NT], BF16, 'oT')
            with contextlib.ExitStack() as ph3:
                wq = k.sb(ph3, [128, 2, 8, 256], BF16, 'wq')
                qh = k.sb(ph3, [128, 2, 2, NT], BF16, 'qh')
                pT = k.sb(ph3, [128, 2, 2, 512], BF16, 'pT')
                rden = k.sb(ph3, [128, 2, 512], F32, 'rden')
                qsrc = d['w_xq'][l].rearrange("(c p) n -> p c n", p=128)
                blocks = [(t0, tn, 0) for (t0, tn) in TB[:4]] + [(TP, TSQ, 1), (TP + TSQ, TSQ, 2)]
                it = 0
                def qgroups(h):
                    sl = h % 2
                    k.dma('pool', wq[:, sl, :, :], qsrc[:, :, 256 * h:256 * h + 256], writes=['wq%d' % sl])
                    gl = []
                    for dc in range(2):
                        for b, (t0, tn) in enumerate(TB):
                            def g_(dc=dc, b=b, t0=t0, tn=tn, sl=sl):
                                bank = b % 2
                                for kk in range(8):
                                    k.mm(ps[bank][:, :tn], wq[:, sl, kk, dc * 128:(dc + 1) * 128], self.hT[:, kk, t0:t0 + tn], kk == 0, kk == 7,
                                         ['wq%d' % sl, 'hT.%d' % b], ['ps%d' % bank])
                                k.cp('act' if b % 2 == 0 else 'dve', qh[:, sl, dc, t0:t0 + tn], ps[bank][:, :tn], ['ps%d' % bank], ['qh%d' % sl])
                            gl.append(g_)
                    return gl
                for g_ in qgroups(0):
                    g_()
                for h in range(4):
                    sl = h % 2
                    pend = qgroups(h + 1) if h < 3 else []
                    for (t0, tn, mset) in blocks:
                        i2 = it % 2
                        it += 1
                        for mc in range(2):
                            bank = 2 + mc
                            for dc in range(2):
                                k.mm(ps[bank][:, :tn], mkT[:, mset, 2 * h + dc, mc * 128:(mc + 1) * 128], qh[:, sl, dc, t0:t0 + tn], dc == 0, dc == 1,
                                     ['mkT', 'qh%d' % sl], ['ps%d' % bank])
                            k.act(pT[:, i2, mc, :tn], ps[bank][:, :tn], AF.Exp, ['ps%d' % bank], ['pT%d' % i2], scale=1.0 / 16)
                        for _ in range(2):
                            if pend:
                                pend.pop(0)()
                        for mc in range(2):
                            k.mm(ps[4][:, :tn], self.ones_bf[:, :], pT[:, i2, mc, :tn], mc == 0, mc == 1, ['pT%d' % i2, 'ones'], ['ps4'])
                        k.act(rden[:, i2, :tn], ps[4][:, :tn], AF.Ln, ['ps4'], ['rden%d' % i2])
                        k.act(rden[:, i2, :tn], rden[:, i2, :tn], AF.Exp, ['rden%d' % i2], ['rden%d' % i2], scale=-1.0)
                        for dc in range(2):
                            bank = 5 + dc
                            for mc in range(2):
                                k.mm(ps[bank][:, :tn], mv[:, mset, mc, 256 * h + 128 * dc:256 * h + 128 * dc + 128], pT[:, i2, mc, :tn], mc == 0, mc == 1,
                                     ['mv', 'pT%d' % i2], ['ps%d' % bank])
                            k.tt('dve', oT[:, 2 * h + dc, t0:t0 + tn], ps[bank][:, :tn], rden[:, i2, :tn], ALU.mult,
                                 ['ps%d' % bank, 'rden%d' % i2], ['oT'])
                    for g_ in pend:
                        g_()
                k.barrier()
            self.chk('x_e')
            self.add_proj('w_xo', l, 0, 8, oT, 'oT', ph)
            k.barrier()

    def phase_ffn(self, l):
        k = self.k
        d = k.dram
        ps = self.ps
        NG = 11
        with contextlib.ExitStack() as ph:
            self.rmsnorm_fm(4 + l, ph)
            cw = k.sb(ph, [128, 3, NF], F32, 'cw')
            cbi = k.sb(ph, [128, NF], F32, 'cbi')
            pv = k.sb(ph, [128, 2, 2, NF], F32, 'pv')
            cvst = k.sb(ph, [128, 2, 3, 128], F32, 'cvst')
            with self.nc.allow_non_contiguous_dma(reason="small conv params / states"):
                for j in range(3):
                    k.dma('sp', cw[:, j, :], d['ffn_conv_w'][l, j].rearrange("(c p) -> p c", p=128), writes=['cw'])
                k.dma('sp', cbi[:, :], d['ffn_conv_b'][l].rearrange("(c p) -> p c", p=128), writes=['cw'])
                for j in range(2):
                    for t in range(2):
                        k.dma('sp', pv[:, j, t, :], d['sfc'][l, j, t].rearrange("(c p) -> p c", p=128), writes=['pv'])
            yT = k.sb(ph, [128, NG, NT], BF16, 'yT')
            wa = k.sb(ph, [128, 2, 8, 128], BF16, 'wa')
            wb = k.sb(ph, [128, 2, 8, 128], BF16, 'wb')
            aS = k.sb(ph, [128, 2, 2 + TP], F32, 'aS')
            aSs = k.sb(ph, [128, 2, 2, 2 + TSQ], F32, 'aSs')
            cc = k.sb(ph, [128, 2, 512], F32, 'cc')
            sg = k.sb(ph, [128, 2, 512], F32, 'sg')
            wt_down = k.sb(ph, [128, 2, NG, 256], BF16, 'wo')
            asrc = d['w_ffn_a'][l].rearrange("(c p) n -> p c n", p=128)
            bsrc = d['w_ffn_b'][l].rearrange("(c p) n -> p c n", p=128)
            k.memset('pool', aS[:, :, 0:2], 0.0, ['aS0', 'aS1'])
            n = 0
            for g in range(2):
                if True:
                    for fl in range(NG):
                        fc = g * NG + fl
                        sl = fl % 2
                        k.dma('pool', wa[:, sl, :, :], asrc[:, :, fc * 128:(fc + 1) * 128], writes=['wa%d' % sl])
                        k.dma('pool', wb[:, sl, :, :], bsrc[:, :, fc * 128:(fc + 1) * 128], writes=['wb%d' % sl])
                        ak = 'aS%d' % sl
                        for j in range(2):
                            k.cp('pool', aSs[:, sl, j, 0:2], pv[:, j, :, fc], ['pv'], [ak])
                        for b, (t0, tn) in enumerate(TB):
                            bankA = b % 2
                            bank = 2 + b % 2
                            i2 = n % 2
                            n += 1
                            ck = 'cc%d' % i2
                            for kk in range(8):
                                k.mm(ps[bankA][:, :tn], wa[:, sl, kk, :], self.hT[:, kk, t0:t0 + tn], kk == 0, kk == 7,
                                     ['wa%d' % sl, 'hT.%d' % b], ['ps%d' % bankA])
                            if b < 4:
                                cur, m1, m2 = aS[:, sl, 2 + t0:2 + t0 + tn], aS[:, sl, 1 + t0:1 + t0 + tn], aS[:, sl, t0:t0 + tn]
                                co = cc[:, i2, :tn]
                                so = sg[:, i2, :tn]
                                pa_ = ps[bankA][:, :tn]
                                pb_ = ps[bank][:, :tn]
                                yo = yT[:, fl, t0:t0 + tn]
                            else:
                                cur, m1, m2 = aSs[:, sl, :, 2:2 + TSQ], aSs[:, sl, :, 1:1 + TSQ], aSs[:, sl, :, 0:TSQ]
                                co = cc[:, i2, 0:2 * TSQ].rearrange("p (j t) -> p j t", j=2)
                                so = sg[:, i2, 0:2 * TSQ].rearrange("p (j t) -> p j t", j=2)
                                pa_ = ps[bankA][:, 0:2 * TSQ].rearrange("p (j t) -> p j t", j=2)
                                pb_ = ps[bank][:, 0:2 * TSQ].rearrange("p (j t) -> p j t", j=2)
                                yo = yT[:, fl, t0:t0 + tn].rearrange("p (j t) -> p j t", j=2)
                            k.cp('act', cur, pa_, ['ps%d' % bankA], [ak])
                            k.act(co, pa_, AF.Identity, ['ps%d' % bankA, 'cw'], [ck], scale=cw[:, 2, fc:fc + 1], bias=cbi[:, fc:fc + 1])
                            for kk in range(8):
                                k.mm(ps[bank][:, :tn], wb[:, sl, kk, :], self.hT[:, kk, t0:t0 + tn], kk == 0, kk == 7,
                                     ['wb%d' % sl, 'hT.%d' % b], ['ps%d' % bank])
                            k.stt(co, m1, cw[:, 1, fc:fc + 1], co, ALU.mult, ALU.add, [ak, 'cw', ck], [ck])
                            k.stt(co, m2, cw[:, 0, fc:fc + 1], co, ALU.mult, ALU.add, [ak, 'cw', ck], [ck])
                            k.act(so, co, AF.Silu, [ck], ['sg%d' % i2])
                            k.tt('dve', yo, so, pb_, ALU.mult, ['sg%d' % i2, 'ps%d' % bank], ['yT'])
                        for si, tend in enumerate((TP, TP + TSQ, NT)):
                            bank = 4 + si
                            for kk in range(8):
                                k.mm(ps[bank][:, 0:128], self.hT[:, kk, tend - 128:tend], wa[:, sl, kk, :], kk == 0, kk == 7,
                                     ['wa%d' % sl, 'hT.3', 'hT.4'], ['ps%d' % bank])
                        for si in range(3):
                            bank = 4 + si
                            k.cp('dve', cvst[96:128, sl, si, :], ps[bank][96:128, 0:128], ['ps%d' % bank], ['cvst%d.%d' % (sl, si)])
                            dst = d['p_ffn_conv'][l, :, fc * 128:(fc + 1) * 128] if si == 0 else d['s_ffn_conv'][l, si - 1, :, fc * 128:(fc + 1) * 128]
                            k.dma('sp', dst, cvst[126:128, sl, si, :], reads=['cvst%d.%d' % (sl, si)])
                self.add_proj('w_ffn_down', l, 128 * NG * g, NG, yT, 'yT', ph, wt=wt_down)
            k.barrier()

    def phase_mlstm(self, l):
        k = self.k
        d = k.dram
        ps = self.ps
        cst = self.cst
        MI = C_MISC
        SC = 96 ** -0.5
        with contextlib.ExitStack() as ph:
            yaT = k.sb(ph, [128, 3, NT], BF16, 'yaT')
            wqk = k.sb(ph, [128, 8, 768], BF16, 'wqk')
            wkvo = k.sb(ph, [128, 8, 1152], BF16, 'wkvo')
            wg = k.sb(ph, [128, 8, 64], BF16, 'wg')
            brow = k.sb(ph, [1, 1152], BF16, 'browa')
            bqk = k.sb(ph, [96, 8], F32, 'bqk')
            bif = k.sb(ph, [8, 4], F32, 'bif')
            m0t = k.sb(ph, [8, 4], F32, 'm0t')
            gna = k.sb(ph, [128, 384], F32, 'gna')
            NB = k.sb(ph, [8, NT], F32, 'NB')
            G = k.sb(ph, [8, NT], F32, 'G')
            M = k.sb(ph, [8, NT], F32, 'M')
            Cst = k.sb(ph, [96, 4, 128], F32, 'Cst')
            Cbf = k.sb(ph, [96, 4, 97], BF16, 'Cbf')
            c0l = k.sb(ph, [96, 4, 128], F32, 'c0l')
            wsrc = d['w_in'][l].rearrange("(c p) n -> p c n", p=128)
            for c0 in range(0, 8, 2):
                k.dma('pool', wqk[:, c0:c0 + 2, :], wsrc[:, c0:c0 + 2, 0:768], writes=['wqk%d' % (c0 // 2)])
                k.dma('pool', wkvo[:, c0:c0 + 2, :], wsrc[:, c0:c0 + 2, 384:1536], writes=['wkvo%d' % (c0 // 2)])
            k.memset('pool', wg[:], 0.0, ['wg'])
            k.memset('dve', m0t[:], 0.0, ['m0t'])
            with self.nc.allow_non_contiguous_dma(reason="small vectors"):
                for q in range(2):
                    k.dma('pool', wg[:, :, 4 * q:4 * q + 4], wsrc[:, :, 1536:1540], writes=['wg'])
                    k.dma('pool', wg[:, :, 32 + 4 * q:36 + 4 * q], wsrc[:, :, 1540:1544], writes=['wg'])
                    k.dma('sp', bif[4 * q:4 * q + 4, 0:1], d['b_in'][l, 1536:1540].rearrange("(p o) -> p o", o=1), writes=['bif'])
                    k.dma('sp', bif[4 * q:4 * q + 4, 1:2], d['b_in'][l, 1540:1544].rearrange("(p o) -> p o", o=1), writes=['bif'])
                    for j in range(2):
                        k.dma('sp', m0t[4 * q:4 * q + 4, 1 + j:2 + j], d['smm'][l, j].rearrange("(p o) -> p o", o=1), writes=['m0t'])
                k.dma('sp', bqk[:, :], d['b_in'][l, 0:768].rearrange("(c p) -> p c", p=96), writes=['bqk'])
                k.dma('sp', gna[:, :], d['gn_a_g'][l:l + 1, :].to_broadcast([128, 384]), writes=['gna'])
            k.dma('pool', brow[:, :], d['b_in'][l:l + 1, 384:1536], writes=['browa'])
            k.ts('dve', bif[:, 2:3], bif[:, 1:2], -1.0, None, ALU.mult, None, ['bif'], ['bif'])
            if self.on('mixnorm'):
                with contextlib.ExitStack() as phn:
                    self.rmsnorm_fm(0 + l, phn)
                    k.barrier()
            for b, (t0, tn) in enumerate(TB):
                for kk in range(8):
                    k.mm(ps[0][0:32, :tn], wg[:, kk, 0:32], self.hT[:, kk, t0:t0 + tn], kk == 0, kk == 7, ['wg', 'hT.%d' % b], ['ps0'])
                for kk in range(8):
                    k.mm(ps[1][0:32, :tn], wg[:, kk, 32:64], self.hT[:, kk, t0:t0 + tn], kk == 0, kk == 7, ['wg', 'hT.%d' % b], ['ps1'])
                k.act(G[:, t0:t0 + tn], ps[0][0:8, :tn], AF.Identity, ['ps0', 'bif'], ['G'], bias=bif[:, 0:1])
                k.act(M[:, t0:t0 + tn], ps[1][0:8, :tn], AF.Exp, ['ps1', 'bif'], ['M'], bias=bif[:, 2:3], scale=-1.0)
                k.act(M[:, t0:t0 + tn], M[:, t0:t0 + tn], AF.Ln, ['M'], ['M'], bias=1.0)
            segs = [(0, TP, 0), (TP, TSQ, 1), (TP + TSQ, TSQ, 2)]
            for (t0, tn, si) in segs:
                k.op('dve', lambda e, t0=t0, tn=tn: e.tensor_tensor_scan(
                    out=NB[:, t0:t0 + tn], data0=self.ones32[0:8, 0:1].to_broadcast([8, tn]), data1=M[:, t0:t0 + tn],
                    initial=0.0, op0=ALU.mult, op1=ALU.add), ['M', 'ones32'], ['NB'])
            k.tt('dve', G[:, :], G[:, :], NB[:, :], ALU.add, ['G', 'NB'], ['G'])
            for (t0, tn, si) in segs:
                k.op('dve', lambda e, t0=t0, tn=tn, si=si: e.tensor_tensor_scan(
                    out=M[:, t0:t0 + tn], data0=G[:, t0:t0 + tn], data1=G[:, t0:t0 + tn],
                    initial=m0t[:, si:si + 1], op0=ALU.max, op1=ALU.max), ['G', 'm0t', 'NB'], ['M'])
            self.chk('ml_gates')
            with contextlib.ExitStack() as ph2:
                qc = k.sb(ph2, [96, 4, 128], BF16, 'qc')
                kc = k.sb(ph2, [96, 4, 128], BF16, 'kc')
                ktmp = k.sb(ph2, [96, 4, 128], F32, 'ktmp')
                kTM = k.sb(ph2, [128, 384], BF16, 'kTM')
                vc = k.sb(ph2, [128, 4, 97], BF16, 'vc')
                vw = k.sb(ph2, [128, 4, 97], BF16, 'vw')
                og = k.sb(ph2, [128, 384], F32, 'og')
                gm1 = k.sb(ph2, [8, 128], F32, 'gm1')
                LH = k.sb(ph2, [8, 4, 128], F32, 'LH')
                RH = k.sb(ph2, [8, 128], F32, 'RH')
                gsm = k.sb(ph2, [8, 3, 128], F32, 'gsm')
                sml = k.sb(ph2, [8, 8], F32, 'sml')
                gTM = k.sb(ph2, [128, 24], F32, 'gTM')
                wT = k.sb(ph2, [128, 512], F32, 'wT')
                Sw = k.sb(ph2, [128, 512], BF16, 'Sw')
                tmp = k.sb(ph2, [128, 4, 97], F32, 'tmp')
                nd = k.sb(ph2, [128, 4, 97], F32, 'nd')
                hg = k.sb(ph2, [128, 384], F32, 'hg')
                sq = k.sb(ph2, [128, 384], F32, 'sqa')
                st = k.sb(ph2, [128, 16], F32, 'st')
                decb = k.sb(ph2, [96, 4], F32, 'decb')
                cout = k.sb(ph2, [128, 4, 96], F32, 'cout')
                k.memset('pool', vc[:, :, 96:97], 1.0, ['vc'])
                k.memset('pool', c0l[:], 0.0, ['c0l'])
                chunks = [(128 * i, 128, 0) for i in range(16)] + [(TP, TSQ, 1), (TP + TSQ, TSQ, 2)]
                for ci, (t0, n, si) in enumerate(chunks):
                    b = bkey(t0)
                    hk = 'hT.%d' % b
                    first = (ci == 0) or si > 0
                    last = (ci == 15) or si > 0
                    if first:
                        if si == 0:
                            k.memset('pool', Cst[:], 0.0, ['Cst'])
                        else:
                            j = si - 1
                            k.memset('pool', Cst[:], 0.0, ['Cst'])
                            k.dma('sp', c0l[:, :, 0:96], d['smc'][l, j].rearrange("h v k -> v h k"), writes=['c0l'])
                            with self.nc.allow_non_contiguous_dma(reason="n0 state"):
                                k.dma('sp', Cst[:, :, 96], d['smn'][l, j].rearrange("h k -> k h"), writes=['Cst'])
                            for h in range(4):
                                k.tr(ps[7][:, h * 96:(h + 1) * 96], c0l[:, h, :], cst[0:96, C_ID:C_ID + 96], ['c0l', 'cst'], ['ps7'])
                            k.cp('dve', Cst[:, :, 0:96], ps[7][0:96, 0:384].rearrange("p (h v) -> p h v", h=4), ['ps7'], ['Cst'])
                        k.cp('act', Cbf[:, :, :], Cst[:, :, 0:97], ['Cst'], ['Cbf'])
                    for hc in range(8):
                        bank = hc // 4
                        for kk in range(8):
                            k.mm(ps[bank][0:96, (hc % 4) * 128:(hc % 4) * 128 + n], wqk[:, kk, 96 * hc:96 * hc + 96], self.hT[:, kk, t0:t0 + n],
                                 kk == 0, kk == 7, ['wqk%d' % (kk // 2), hk], ['ps%d' % bank])
                    p0 = ps[0][0:96, :].rearrange("p (h t) -> p h t", h=4)[:, :, :n]
                    p1 = ps[1][0:96, :].rearrange("p (h t) -> p h t", h=4)[:, :, :n]
                    k.tt('dve', qc[:, :, :n], p0, bqk[:, 0:4].unsqueeze(2).to_broadcast([96, 4, n]), ALU.add, ['ps0', 'bqk'], ['qc'])
                    k.tt('dve', ktmp[:, :, :n], p1, bqk[:, 4:8].unsqueeze(2).to_broadcast([96, 4, n]), ALU.add, ['ps1', 'bqk'], ['ktmp'])
                    k.act(kc[:, :, :n], ktmp[:, :, :n], AF.Copy, ['ktmp'], ['kc'], scale=SC)
                    for part in range(3):
                        bank = 2 + part
                        for kk in range(8):
                            k.mm(ps[bank][:n, 0:384], self.hT[:, kk, t0:t0 + n], wkvo[:, kk, 384 * part:384 * part + 384], kk == 0, False,
                                 ['wkvo%d' % (kk // 2), hk], ['ps%d' % bank])
                        k.mm(ps[bank][:n, 0:384], self.ones_bf[0:1, :n], brow[0:1, 384 * part:384 * part + 384], False, True,
                             ['ones', 'browa'], ['ps%d' % bank])
                    k.act(kTM[:n, :], ps[2][:n, 0:384], AF.Copy, ['ps2'], ['kTM'], scale=SC)
                    k.cp('dve', vc[:n, :, 0:96], ps[3][:n, 0:384].rearrange("p (h v) -> p h v", h=4), ['ps3'], ['vc'])
                    k.act(og[:n, :], ps[4][:n, 0:384], AF.Exp, ['ps4'], ['og'], scale=-1.0)
                    k.act(og[:n, :], og[:n, :], AF.Ln, ['og'], ['og'], bias=1.0)
                    k.act(og[:n, :], og[:n, :], AF.Exp, ['og'], ['og'], scale=-1.0)
                    k.ts('dve', gm1[:, :n], G[:, t0:t0 + n], cst[0:8, MI + 2:MI + 3], cst[0:8, MI + 3:MI + 4], ALU.mult, ALU.add, ['G', 'cst'], ['gm1'])
                    for h in range(4):
                        k.ts('dve', LH[:, h, :n], gm1[:, :n], cst[0:8, MI + 4 + h:MI + 5 + h], None, ALU.mult, None, ['gm1', 'cst'], ['LH'])
                    k.ts('dve', RH[:, :n], M[:, t0:t0 + n], cst[0:8, MI + 10:MI + 11], cst[0:8, MI + 2:MI + 3], ALU.mult, ALU.add, ['M', 'cst'], ['RH'])
                    mprev = m0t[:, si:si + 1] if first else M[:, t0 - 1:t0]
                    k.act(gsm[:, 0, :n], M[:, t0:t0 + n], AF.Exp, ['M', 'm0t'], ['gsm'], scale=-1.0, bias=mprev)
                    k.ts('dve', sml[:, 0:1], M[:, t0 + n - 1:t0 + n], -1.0, None, ALU.mult, None, ['M'], ['sml'])
                    k.act(gsm[:, 1, :n], G[:, t0:t0 + n], AF.Exp, ['G', 'sml'], ['gsm'], bias=sml[:, 0:1])
                    k.tt('dve', gsm[:, 2, :n], NB[:, t0:t0 + n], M[:, t0:t0 + n], ALU.subtract, ['NB', 'M'], ['gsm2'])
                    k.act(gsm[:, 2, :n], gsm[:, 2, :n], AF.Exp, ['gsm2'], ['gsm2'])
                    for h in range(4):
                        k.mm(ps[5][:n, h * 128:h * 128 + n], LH[:, h, :n], RH[:, :n], h == 0, False, ['LH', 'RH'], ['ps5'])
                    for h in range(4):
                        k.mm(ps[5][:n, h * 128:h * 128 + n], cst[:n, C_ID:C_ID + n], cst[:n, C_MLNEG:C_MLNEG + n], False, h == 3, ['cst'], ['ps5'])
                    p5 = ps[5][:n, :].rearrange("p (h t) -> p h t", h=4)[:, :, :n]
                    wT3 = wT[:n, :].rearrange("p (h t) -> p h t", h=4)[:, :, :n]
                    Sw3 = Sw[:n, :].rearrange("p (h t) -> p h t", h=4)[:, :, :n]
                    k.act(wT3, p5, AF.Exp, ['ps5'], ['wT'])
                    for h in range(4):
                        k.mm(ps[6][:n, h * 128:h * 128 + n], kc[:, h, :n], qc[:, h, :n], h == 0, h == 3, ['kc', 'qc'], ['ps6'])
                    p6 = ps[6][:n, :].rearrange("p (h t) -> p h t", h=4)[:, :, :n]
                    k.tt('dve', Sw3, p6, wT3, ALU.mult, ['ps6', 'wT'], ['Sw'])
                    for h in range(4):
                        k.mm(ps[7][:n, h * 97:(h + 1) * 97], Sw[:n, h * 128:h * 128 + n], vc[:n, h, :], h == 0, h == 3, ['Sw', 'vc'], ['ps7'])
                    for h in range(4):
                        k.mm(ps[0][:n, h * 97:(h + 1) * 97], qc[:, h, :n], Cbf[:, h, :], h == 0, h == 3, ['qc', 'Cbf'], ['ps0'])
                    for q in range(3):
                        k.mm(ps[1][:n, q * 8:(q + 1) * 8], gsm[:, q, :n], cst[0:8, C_ID:C_ID + 8], q == 0, q == 2, ['gsm', 'gsm2', 'cst'], ['ps1'])
                    k.cp('act', gTM[:n, :], ps[1][:n, 0:24], ['ps1'], ['gTM'])
                    p7 = ps[7][:n, 0:388].rearrange("p (h v) -> p h v", h=4)
                    p0b = ps[0][:n, 0:388].rearrange("p (h v) -> p h v", h=4)
                    k.tt('dve', tmp[:n], p0b, gTM[:n, 0:4].unsqueeze(2).to_broadcast([n, 4, 97]), ALU.mult, ['ps0', 'gTM'], ['tmp'])
                    k.tt('dve', nd[:n], p7, tmp[:n], ALU.add, ['ps7', 'tmp'], ['nd'])
                    k.act(st[:n, 0:4], nd[:n, :, 96], AF.Abs, ['nd'], ['st'])
                    k.tt('dve', st[:n, 0:4], st[:n, 0:4], gTM[:n, 16:20], ALU.max, ['st', 'gTM'], ['st'])
                    k.op('dve', lambda e, n=n: e.reciprocal(out=st[:n, 4:8], in_=st[:n, 0:4]), ['st'], ['st'])
                    hg3 = hg[:n, :].rearrange("p (h v) -> p h v", h=4)
                    k.tt('dve', hg3, nd[:n, :, 0:96], st[:n, 4:8].unsqueeze(2).to_broadcast([n, 4, 96]), ALU.mult, ['nd', 'st'], ['hg'])
                    k.tt('dve', hg[:n, :], hg[:n, :], og[:n, :], ALU.mult, ['hg', 'og'], ['hg'])
                    k.tt('dve', sq[:n, :], hg[:n, :], hg[:n, :], ALU.mult, ['hg'], ['sqa'])
                    k.op('dve', lambda e, n=n: e.tensor_reduce(out=st[:n, 8:12], in_=sq[:n, :].rearrange("p (h v) -> p h v", h=4),
                                                               axis=AX.X, op=ALU.add), ['sqa'], ['st2'])
                    k.act(st[:n, 12:16], st[:n, 8:12], AF.Ln, ['st2'], ['st3'], scale=1.0 / 96, bias=EPS)
                    k.act(st[:n, 12:16], st[:n, 12:16], AF.Exp, ['st3'], ['st3'], scale=-0.5)
                    k.tt('dve', hg3, hg3, st[:n, 12:16].unsqueeze(2).to_broadcast([n, 4, 96]), ALU.mult, ['hg', 'st3'], ['hg'])
                    k.tt('dve', hg[:n, :], hg[:n, :], gna[:n, :], ALU.mult, ['hg', 'gna'], ['hg'])
                    for c in range(3):
                        k.tr(ps[3][:, c * 128:c * 128 + n], hg[:n, c * 128:(c + 1) * 128], cst[:n, C_ID:C_ID + n], ['hg', 'cst'], ['ps3'])
                    k.cp('act', yaT[:, :, t0:t0 + n], ps[3][:, 0:384].rearrange("p (c t) -> p c t", c=3)[:, :, :n], ['ps3'], ['yaT'])
                    k.tt('dve', vw[:n], vc[:n], gTM[:n, 8:12].unsqueeze(2).to_broadcast([n, 4, 97]), ALU.mult, ['vc', 'gTM'], ['vw'])
                    for h in range(4):
                        k.mm(ps[2][0:96, h * 97:(h + 1) * 97], kTM[:n, 96 * h:96 * h + 96], vw[:n, h, :], h == 0, h == 3, ['kTM', 'vw'], ['ps2'])
                    k.ts('dve', sml[:, 4:8], cst[0:8, MI + 12:MI + 16], gsm[:, 0, n - 1:n], None, ALU.mult, None, ['gsm', 'cst'], ['sml2'])
                    k.mm(ps[4][0:96, 0:4], self.ones32[0:8, 0:96], sml[:, 4:8], True, True, ['sml2', 'ones32'], ['ps4'])
                    k.cp('act', decb[:, :], ps[4][0:96, 0:4], ['ps4'], ['decb'])
                    k.tt('dve', Cst[:, :, 0:97], Cst[:, :, 0:97], decb[:, :].unsqueeze(2).to_broadcast([96, 4, 97]), ALU.mult, ['Cst', 'decb'], ['Cst'])
                    k.tt('dve', Cst[:, :, 0:97], Cst[:, :, 0:97], ps[2][0:96, 0:388].rearrange("p (h v) -> p h v", h=4), ALU.add,
                         ['Cst', 'ps2'], ['Cst'])
                    if not last:
                        k.cp('act', Cbf[:, :, :], Cst[:, :, 0:97], ['Cst'], ['Cbf'])
                    else:
                        for h in range(4):
                            k.tr(ps[6][:, h * 96:(h + 1) * 96], Cst[:, h, :], cst[0:96, C_ID:C_ID + 96], ['Cst', 'cst'], ['ps6'])
                        k.cp('act', cout[:, :, :], ps[6][:, 0:384].rearrange("p (h k) -> p h k", h=4), ['ps6'], ['cout'])
                        k.tt('dve', sml[:, 1:2], M[:, t0 + n - 1:t0 + n], NB[:, t0 + n - 1:t0 + n], ALU.subtract, ['M', 'NB'], ['sml3'])
                        if si == 0:
                            dc_, dn_, dm_ = d['p_ml_c'][l], d['p_ml_n'][l], d['p_ml_m'][l]
                        else:
                            dc_, dn_, dm_ = d['s_ml_c'][l, si - 1], d['s_ml_n'][l, si - 1], d['s_ml_m'][l, si - 1]
                        k.dma('sp', dc_.rearrange("h v k -> v h k"), cout[0:96, :, :], reads=['cout'])
                        k.dma('sp', dn_.rearrange("(o h) k -> o h k", o=1), cout[96:97, :, :], reads=['cout'])
                        with self.nc.allow_non_contiguous_dma(reason="m state"):
                            k.dma('sp', dm_.rearrange("(p o) -> p o", o=1), sml[0:4, 1:2], reads=['sml3'])
                k.barrier()
            self.chk('ml_core')
            self.add_proj('w_out', l, 0, 3, yaT, 'yaT', ph)
            k.barrier()

    def phase_s5(self, l):
        k = self.k
        d = k.dram
        ps = self.ps
        cst = self.cst
        MI = C_MISC
        NC_ = NT // 8
        TWO_PI = 2.0 * np.pi

        def bc(ap, shape):
            return ap.unsqueeze(2).to_broadcast(shape)

        with contextlib.ExitStack() as ph:
            WinR = k.sb(ph, [128, 2, 8, 128], BF16, 'WinR')
            WinI = k.sb(ph, [128, 2, 8, 128], BF16, 'WinI')
            Wout = k.sb(ph, [128, 2, 2, 8, 128], BF16, 'Wout')
            Kbd = k.sb(ph, [128, 2, 8, 128], BF16, 'Kbd')
            Win3R = k.sb(ph, [128, 2, 8, 128], BF16, 'Win3R')
            Win3I = k.sb(ph, [128, 2, 8, 128], BF16, 'Win3I')
            Wout3 = k.sb(ph, [128, 2, 2, 8, 64], BF16, 'Wout3')
            uT = k.sb(ph, [128, 2, 8, NC_], BF16, 'uT')
            Sprev = k.sb(ph, [128, 8, 2, NC_], BF16, 'Sprev')
            MUr = k.sb(ph, [128, 8, 8], F32, 'MUr')
            MUi = k.sb(ph, [128, 8, 8], F32, 'MUi')
            nMUi = k.sb(ph, [128, 8, 8], F32, 'nMUi')
            h0 = k.sb(ph, [128, 8, 2, 2], F32, 'h0')
            with contextlib.ExitStack() as pp:
                _n = [0]

                def T(shape, dt=F32):
                    _n[0] += 1
                    return k.sb(pp, shape, dt, 'p%d' % _n[0])
                are, aim, ldt = T([128, 8]), T([128, 8]), T([128, 8])
                with self.nc.allow_non_contiguous_dma(reason="small ssm params"):
                    for t_, nm in ((are, 'ssm_a_re'), (aim, 'ssm_a_im'), (ldt, 'ssm_log_dt')):
                        k.dma('sp', t_[:, :], d[nm][l].rearrange("(a g) p -> (g p) a", g=2), writes=['prm'])
                    for ri, nm in enumerate(('ssr', 'ssi')):
                        for j in range(2):
                            k.dma('sp', h0[:, :, ri, j], d[nm][l, j].rearrange("(a g) p -> (g p) a", g=2), writes=['h0'])
                R_ = ['prm']
                dt = T([128, 8]); ang = T([128, 8]); mag = T([128, 8])
                k.act(dt[:], ldt[:], AF.Exp, R_, R_)
                k.tt('dve', ang[:], aim[:], dt[:], ALU.mult, R_, R_)
                k.tt('dve', mag[:], are[:], dt[:], ALU.mult, R_, R_)
                k.act(mag[:], mag[:], AF.Exp, R_, R_)
                sc = T([128, 2, 8])
                xx = T([128, 8]); ti = T([128, 8], mybir.dt.int32); tf = T([128, 8]); gg = T([128, 8])
                for q in range(2):
                    if q == 0:
                        k.cp('dve', xx[:], ang[:], R_, R_)
                    else:
                        k.ts('dve', xx[:], ang[:], float(np.pi / 2), None, ALU.add, None, R_, R_)
                    k.ts('dve', tf[:], xx[:], float(1.0 / TWO_PI), None, ALU.mult, None, R_, R_)
                    k.cp('dve', ti[:], tf[:], R_, R_)
                    k.cp('dve', tf[:], ti[:], R_, R_)
                    k.stt(xx[:], tf[:], -TWO_PI, xx[:], ALU.mult, ALU.add, R_, R_)
                    k.ts('dve', gg[:], xx[:], float(np.pi), None, ALU.is_gt, None, R_, R_)
                    k.stt(xx[:], gg[:], -TWO_PI, xx[:], ALU.mult, ALU.add, R_, R_)
                    k.ts('dve', gg[:], xx[:], float(-np.pi), None, ALU.is_lt, None, R_, R_)
                    k.stt(xx[:], gg[:], TWO_PI, xx[:], ALU.mult, ALU.add, R_, R_)
                    k.act(sc[:, q, :], xx[:], AF.Sin, R_, R_)
                LAMr = T([128, 9, 8]); LAMi = T([128, 9, 8]); nLAMi = T([128, 9, 8])
                k.memset('dve', LAMr[:, 0, :], 1.0, R_)
                k.memset('dve', LAMi[:, 0, :], 0.0, R_)
                k.tt('dve', LAMr[:, 1, :], mag[:], sc[:, 1, :], ALU.mult, R_, R_)
                k.tt('dve', LAMi[:, 1, :], mag[:], sc[:, 0, :], ALU.mult, R_, R_)
                t1 = T([128, 8]); t2 = T([128, 8])

                def cmul(or_, oi_, ar, ai, br, bi):
                    k.tt('dve', t1[:], ar, br, ALU.mult, R_, R_)
                    k.tt('dve', t2[:], ai, bi, ALU.mult, R_, R_)
                    k.tt('dve', or_, t1[:], t2[:], ALU.subtract, R_, R_)
                    k.tt('dve', t1[:], ar, bi, ALU.mult, R_, R_)
                    k.tt('dve', t2[:], ai, br, ALU.mult, R_, R_)
                    k.tt('dve', oi_, t1[:], t2[:], ALU.add, R_, R_)
                for kk in range(1, 8):
                    cmul(LAMr[:, kk + 1, :], LAMi[:, kk + 1, :], LAMr[:, kk, :], LAMi[:, kk, :], LAMr[:, 1, :], LAMi[:, 1, :])
                k.ts('dve', nLAMi[:], LAMi[:], -1.0, None, ALU.mult, None, R_, R_)
                k.cp('dve', MUr[:, 0, :], LAMr[:, 8, :], R_, ['MU'])
                k.cp('dve', MUi[:, 0, :], LAMi[:, 8, :], R_, ['MU'])
                RM = ['prm', 'MU']
                for kk in range(7):
                    k.tt('dve', t1[:], MUr[:, kk, :], MUr[:, kk, :], ALU.mult, RM, R_)
                    k.tt('dve', t2[:], MUi[:, kk, :], MUi[:, kk, :], ALU.mult, RM, R_)
                    k.tt('dve', MUr[:, kk + 1, :], t1[:], t2[:], ALU.subtract, R_, ['MU'])
                    k.tt('dve', t1[:], MUr[:, kk, :], MUi[:, kk, :], ALU.mult, RM, R_)
                    k.ts('dve', MUi[:, kk + 1, :], t1[:], 2.0, None, ALU.mult, None, R_, ['MU'])
                k.ts('dve', nMUi[:], MUi[:], -1.0, None, ALU.mult, None, RM, ['MU'])
                den = T([128, 8]); nr = T([128, 8]); zr = T([128, 8]); zi = T([128, 8])
                k.tt('dve', den[:], are[:], are[:], ALU.mult, R_, R_)
                k.tt('dve', t1[:], aim[:], aim[:], ALU.mult, R_, R_)
                k.tt('dve', den[:], den[:], t1[:], ALU.add, R_, R_)
                k.op('dve', lambda e: e.reciprocal(out=den[:], in_=den[:]), R_, R_)
                k.ts('dve', nr[:], LAMr[:, 1, :], -1.0, None, ALU.add, None, R_, R_)
                k.tt('dve', t1[:], nr[:], are[:], ALU.mult, R_, R_)
                k.tt('dve', t2[:], LAMi[:, 1, :], aim[:], ALU.mult, R_, R_)
                k.tt('dve', zr[:], t1[:], t2[:], ALU.add, R_, R_)
                k.tt('dve', zr[:], zr[:], den[:], ALU.mult, R_, R_)
                k.tt('dve', t1[:], LAMi[:, 1, :], are[:], ALU.mult, R_, R_)
                k.tt('dve', t2[:], nr[:], aim[:], ALU.mult, R_, R_)
                k.tt('dve', zi[:], t1[:], t2[:], ALU.subtract, R_, R_)
                k.tt('dve', zi[:], zi[:], den[:], ALU.mult, R_, R_)
                bre = T([128, 8, 16]); bim = T([128, 8, 16]); bbr = T([128, 8, 16]); bbi = T([128, 8, 16])
                u1 = T([128, 8, 16]); u2 = T([128, 8, 16])
                with self.nc.allow_non_contiguous_dma(reason="ssm B"):
                    k.dma('sp', bre[:], d['ssm_b_re'][l].rearrange("(a g) p c -> (g p) a c", g=2), writes=['prm'])
                    k.dma('sp', bim[:], d['ssm_b_im'][l].rearrange("(a g) p c -> (g p) a c", g=2), writes=['prm'])
                S16 = [128, 8, 16]

                def cmul16(or_, oi_, ar, ai, br, bi):
                    k.tt('dve', u1[:], ar, bc(br, S16), ALU.mult, R_, R_)
                    k.tt('dve', u2[:], ai, bc(bi, S16), ALU.mult, R_, R_)
                    k.tt('dve', or_, u1[:], u2[:], ALU.subtract, R_, R_)
                    k.tt('dve', u1[:], ar, bc(bi, S16), ALU.mult, R_, R_)
                    k.tt('dve', u2[:], ai, bc(br, S16), ALU.mult, R_, R_)
                    k.tt('dve', oi_, u1[:], u2[:], ALU.add, R_, R_)
                cmul16(bbr[:], bbi[:], bre[:], bim[:], zr[:], zi[:])
                cnat = T([128, 2, 2, 64])
                for ri, nm in enumerate(('ssm_c_re', 'ssm_c_im')):
                    k.dma('sp', cnat[:, ri, :, :], d[nm][l].rearrange("(hh g) c p -> (g c) hh p", hh=2), writes=['prm'])
                X = T([128, 2, 2, 2, 64])
                for g2 in range(2):
                    k.ts('dve', X[:, :, :, g2, :], cnat[:, :, :, :], cst[:, MI + 8 + g2:MI + 9 + g2], None, ALU.mult, None, R_ + ['cst'], R_)
                CT = T([128, 3, 2, 128])
                for ri in range(2):
                    for hh in range(2):
                        k.tr(ps[0][:, (2 * ri + hh) * 128:(2 * ri + hh + 1) * 128],
                             X[:, ri, hh, :, :].rearrange("p g q -> p (g q)"), self.cid(), R_ + ['cst'], ['ps0'])
                k.cp('act', CT[:, 0:2, :, :], ps[0][:, :].rearrange("p (r h c) -> p r h c", r=2, h=2), ['ps0'], R_)
                k.ts('dve', CT[:, 2, :, :], CT[:, 1, :, :], -1.0, None, ALU.mult, None, R_, R_)
                S32 = [128, 8, 32]
                vv = [(T([128, 8, 32]), T([128, 8, 32]), T([128, 8, 32]), T([128, 8, 32])) for _ in range(2)]

                def c8(ri):
                    return CT[:, ri, :, :].rearrange("p h (a x) -> p (h a) x", a=4)
                r4 = lambda t_: t_[:].rearrange("p (h a) x -> p h a x", h=2)
                for j in range(8):
                    lr, li, nli = LAMr[:, j + 1, :], LAMi[:, j + 1, :], nLAMi[:, j + 1, :]
                    v1, v2, v3, v4 = vv[j % 2]
                    kz = ['wv%d.%d' % (j % 2, q) for q in range(4)]
                    k.tt('dve', v1[:], c8(0), bc(lr, S32), ALU.mult, R_, [kz[0]])
                    k.tt('pool', v2[:], c8(1), bc(li, S32), ALU.mult, R_, [kz[1]])
                    k.tt('dve', v3[:], c8(0), bc(nli, S32), ALU.mult, R_, [kz[2]])
                    k.tt('pool', v4[:], c8(1), bc(lr, S32), ALU.mult, R_, [kz[3]])
                    k.tt('dve', Wout[:, :, 0, j, :].rearrange("p h (a x) -> p h a x", a=4), r4(v1), r4(v2), ALU.subtract, [kz[0], kz[1]], ['Wout'])
                    k.tt('dve', Wout[:, :, 1, j, :].rearrange("p h (a x) -> p h a x", a=4), r4(v3), r4(v4), ALU.subtract, [kz[2], kz[3]], ['Wout'])
                Yf = T([128, 2, 8, 256])
                yy = [tuple(T([128, 8, 16]) for _ in range(6)) for _ in range(2)]
                for dd in range(8):
                    a1, a2, a3, a4, pr, pi_ = yy[dd % 2]
                    kz = ['yv%d.%d' % (dd % 2, q) for q in range(6)]
                    lr, li = LAMr[:, dd, :], LAMi[:, dd, :]
                    k.tt('dve', a1[:], bbr[:], bc(lr, S16), ALU.mult, R_, [kz[0]])
                    k.tt('pool', a2[:], bbi[:], bc(li, S16), ALU.mult, R_, [kz[1]])
                    k.tt('dve', a3[:], bbr[:], bc(li, S16), ALU.mult, R_, [kz[2]])
                    k.tt('pool', a4[:], bbi[:], bc(lr, S16), ALU.mult, R_, [kz[3]])
                    k.tt('dve', pr[:], a1[:], a2[:], ALU.subtract, [kz[0], kz[1]], [kz[4]])
                    k.tt('pool', pi_[:], a3[:], a4[:], ALU.add, [kz[2], kz[3]], [kz[5]])
                    for ri, src in enumerate((pr, pi_)):
                        for g2 in range(2):
                            k.ts('dve' if g2 == 0 else 'pool', Yf[:, ri, dd, :].rearrange("p (a g c) -> p a g c", g=2, c=16)[:, :, g2, :], src[:],
                                 cst[:, MI + g2:MI + g2 + 1], None, ALU.mult, None, [kz[4 + ri], 'cst'], ['Yf%d' % dd])
                nb = 0
                for ri, W_ in enumerate((WinR, WinI)):
                    for hh in range(2):
                        for i0 in range(0, 8, 4):
                            bank = 1 + nb % 2
                            nb += 1
                            for ii in range(4):
                                i = i0 + ii
                                k.tr(ps[bank][:, ii * 128:(ii + 1) * 128], Yf[:, ri, 7 - i, hh * 128:(hh + 1) * 128], self.cid(),
                                     ['Yf%d' % (7 - i), 'cst'], ['ps%d' % bank])
                            k.cp('act' if bank == 1 else 'dve', W_[:, hh, i0:i0 + 4, :], ps[bank][:, :].rearrange("p (i c) -> p i c", i=4),
                                 ['ps%d' % bank], ['Win'])
                for W3, W_ in ((Win3R, WinR), (Win3I, WinI)):
                    k.ts('pool', W3[:].rearrange("p h i c -> p (h i c)"), W_[:].rearrange("p h i c -> p (h i c)"),
                         cst[:, MI + 16:MI + 17], None, ALU.mult, None, ['Win', 'cst'], ['Win'])
                k.memset('pool', Wout3[:], 0.0, ['Wout'])
                k.cp('pool', Wout3[:, :, :, :, 32:64], Wout[:, :, :, :, 96:128], ['Wout'], ['Wout'])
                k.memset('pool', Kbd[:], 0.0, ['Kbd'])
                for hh in range(2):
                    for dd in range(8):
                        col = (hh * 8 + dd) * 32
                        f0 = hh == 0 and dd == 0
                        l0 = hh == 1 and dd == 7
                        for a4 in range(3):
                            cs = slice(hh * 128 + 32 * a4, hh * 128 + 32 * a4 + 32)
                            k.mm(ps[3][32 * a4:32 * a4 + 32, col:col + 32], Yf[:, 0, dd, cs], CT[:, 0, hh, 32 * a4:32 * a4 + 32],
                                 f0, False, ['Yf%d' % dd] + R_, ['ps3'])
                            k.mm(ps[3][32 * a4:32 * a4 + 32, col:col + 32], Yf[:, 1, dd, cs], CT[:, 2, hh, 32 * a4:32 * a4 + 32],
                                 False, l0, ['Yf%d' % dd] + R_, ['ps3'])
                        cs = slice(hh * 128 + 64, hh * 128 + 128)
                        k.mm(ps[4][64:128, col:col + 32], Yf[:, 0, dd, cs], CT[:, 0, hh, 96:128], f0, False, ['Yf%d' % dd] + R_, ['ps4'])
                        k.mm(ps[4][64:128, col:col + 32], Yf[:, 1, dd, cs], CT[:, 2, hh, 96:128], False, l0, ['Yf%d' % dd] + R_, ['ps4'])
                for a4 in range(4):
                    src = ps[3] if a4 < 3 else ps[4]
                    k.cp('act', Kbd[32 * a4:32 * a4 + 32, :, :, 32 * a4:32 * a4 + 32],
                         src[32 * a4:32 * a4 + 32, :].rearrange("p (h d c) -> p h d c", h=2, d=8), ['ps3', 'ps4'], ['Kbd'])
                k.barrier()
            self.chk('s5_prep')
            with contextlib.ExitStack() as pm:
                wu = k.sb(pm, [128, 8, 256], BF16, 'wu')
                bu = k.sb(pm, [128, 2], F32, 'bu')
                Z = k.sb(pm, [128, 8, 2, NC_], F32, 'Z')
                Z2 = k.sb(pm, [128, 8, 2, NC_], F32, 'Z2')
                fin = k.sb(pm, [128, 8, 2, 4], F32, 'fin')
                hm = k.sb(pm, [128, 8, 2, 2], F32, 'hm')
                hm2 = k.sb(pm, [128, 8, 2, 2], F32, 'hm2')
                wsrc = d['w_in'][l].rearrange("(c p) n -> p c n", p=128)
                k.dma('pool', wu[:, :, :], wsrc[:, :, 2696:2952], writes=['wu'])
                with self.nc.allow_non_contiguous_dma(reason="small"):
                    k.dma('sp', bu[:, :], d['b_in'][l, 2696:2952].rearrange("(h p) -> p h", p=128), writes=['bu'])
                for hh in range(2):
                    for b, (t0, tn) in enumerate(TB):
                        bank = b % 2
                        for kk in range(8):
                            k.mm(ps[bank][:, :tn], wu[:, kk, hh * 128:(hh + 1) * 128], self.hT[:, kk, t0:t0 + tn], kk == 0, kk == 7,
                                 ['wu', 'hT.%d' % b], ['ps%d' % bank])
                        k.act(uT[:, hh, :, t0 // 8:(t0 + tn) // 8].rearrange("p i c -> p c i"),
                              ps[bank][:, :tn].rearrange("p (c i) -> p c i", i=8), AF.Identity, ['ps%d' % bank, 'bu'], ['uT'],
                              bias=bu[:, hh:hh + 1])
                nb = 0
                for a in range(8):
                    hh, a4 = a // 4, a % 4
                    for ri, W_ in enumerate((WinR, WinI)):
                        bank = 2 + nb % 4
                        nb += 1
                        for i in range(8):
                            if a4 < 3:
                                k.mm(ps[bank][:, 0:NC_], W_[32 * a4:32 * a4 + 32, hh, i, :], uT[32 * a4:32 * a4 + 32, hh, i, :], i == 0, i == 7,
                                     ['Win', 'uT'], ['ps%d' % bank])
                            else:
                                W3 = Win3R if ri == 0 else Win3I
                                k.mm(ps[bank][:, 0:NC_], W3[:, hh, i, :], uT[:, hh, i, :], i == 0, i == 7, ['Win', 'uT'], ['ps%d' % bank])
                        k.cp('act' if nb % 2 == 0 else 'dve', Z[:, a, ri, :], ps[bank][:, 0:NC_], ['ps%d' % bank], ['Z'])
                S22 = [128, 8, 2]
                zc = lambda ri: Z[:, :, ri, 256:NC_:4]
                k.tt('dve', hm[:, :, 0, :], h0[:, :, 0, :], bc(MUr[:, 0, :], S22), ALU.mult, ['h0', 'MU'], ['hm'])
                k.tt('dve', hm[:, :, 1, :], h0[:, :, 1, :], bc(MUi[:, 0, :], S22), ALU.mult, ['h0', 'MU'], ['hm'])
                k.tt('dve', hm2[:, :, 0, :], h0[:, :, 1, :], bc(MUr[:, 0, :], S22), ALU.mult, ['h0', 'MU'], ['hm'])
                k.tt('dve', hm2[:, :, 1, :], h0[:, :, 0, :], bc(MUi[:, 0, :], S22), ALU.mult, ['h0', 'MU'], ['hm'])
                k.tt('dve', zc(0), zc(0), hm[:, :, 0, :], ALU.add, ['Z', 'hm'], ['Z'])
                k.tt('dve', zc(0), zc(0), hm[:, :, 1, :], ALU.subtract, ['Z', 'hm'], ['Z'])
                k.tt('dve', zc(1), zc(1), hm2[:, :, 0, :], ALU.add, ['Z', 'hm'], ['Z'])
                k.tt('dve', zc(1), zc(1), hm2[:, :, 1, :], ALU.add, ['Z', 'hm'], ['Z'])
                X_, Y_ = Z, Z2
                xk, yk = 'Z', 'Z2'
                allk = lambda nm: ['%s.%d.%d' % (nm, a, ri) for a in range(8) for ri in range(2)]
                k.cp('pool', Z2[:, :, :, 256:NC_], Z[:, :, :, 256:NC_], ['Z'], allk('Z') + allk('Z2'))
                for kk in range(8):
                    dd = 1 << kk
                    k.cp('pool', Y_[:, :, :, 0:dd], X_[:, :, :, 0:dd], allk(xk), allk(yk))
                    mu = lambda a: (MUr[:, kk, a:a + 1], MUi[:, kk, a:a + 1], nMUi[:, kk, a:a + 1])
                    K_ = lambda nm, a, ri: '%s.%d.%d' % (nm, a, ri)
                    for a in range(8):
                        k.stt(Y_[:, a, 0, dd:256], X_[:, a, 0, 0:256 - dd], mu(a)[0], X_[:, a, 0, dd:256], ALU.mult, ALU.add,
                              [K_(xk, a, 0), 'MU'], [K_(yk, a, 0)])
                    for a in range(8):
                        k.stt(Y_[:, a, 1, dd:256], X_[:, a, 1, 0:256 - dd], mu(a)[0], X_[:, a, 1, dd:256], ALU.mult, ALU.add,
                              [K_(xk, a, 1), 'MU'], [K_(yk, a, 1)])
                    for a in range(8):
                        k.stt(Y_[:, a, 0, dd:256], X_[:, a, 1, 0:256 - dd], mu(a)[2], Y_[:, a, 0, dd:256], ALU.mult, ALU.add,
                              [K_(xk, a, 1), 'MU', K_(yk, a, 0)], [K_(yk, a, 0)])
                    for a in range(8):
                        k.stt(Y_[:, a, 1, dd:256], X_[:, a, 0, 0:256 - dd], mu(a)[1], Y_[:, a, 1, dd:256], ALU.mult, ALU.add,
                              [K_(xk, a, 0), 'MU', K_(yk, a, 1)], [K_(yk, a, 1)])
                    if kk < 2:
                        def sv(T_, a, ri, lo, hi):
                            return T_[:, a, ri, 256:NC_].rearrange("p (j c) -> p j c", j=2)[:, :, lo:hi]
                        for a in range(8):
                            k.stt(sv(Y_, a, 0, dd, 4), sv(X_, a, 0, 0, 4 - dd), mu(a)[0], sv(X_, a, 0, dd, 4), ALU.mult, ALU.add,
                                  [K_(xk, a, 0), 'MU'], [K_(yk, a, 0)])
                        for a in range(8):
                            k.stt(sv(Y_, a, 1, dd, 4), sv(X_, a, 1, 0, 4 - dd), mu(a)[0], sv(X_, a, 1, dd, 4), ALU.mult, ALU.add,
                                  [K_(xk, a, 1), 'MU'], [K_(yk, a, 1)])
                        for a in range(8):
                            k.stt(sv(Y_, a, 0, dd, 4), sv(X_, a, 1, 0, 4 - dd), mu(a)[2], sv(Y_, a, 0, dd, 4), ALU.mult, ALU.add,
                                  [K_(xk, a, 1), 'MU', K_(yk, a, 0)], [K_(yk, a, 0)])
                        for a in range(8):
                            k.stt(sv(Y_, a, 1, dd, 4), sv(X_, a, 0, 0, 4 - dd), mu(a)[1], sv(Y_, a, 1, dd, 4), ALU.mult, ALU.add,
                                  [K_(xk, a, 0), 'MU', K_(yk, a, 1)], [K_(yk, a, 1)])
                        k.cp('pool', Y_[:, :, :, 256:NC_].rearrange("p a r (j c) -> p a r j c", j=2)[:, :, :, :, 0:dd],
                             X_[:, :, :, 256:NC_].rearrange("p a r (j c) -> p a r j c", j=2)[:, :, :, :, 0:dd], allk(xk), allk(yk))
                    X_, Y_ = Y_, X_
                    xk, yk = yk, xk
                k.cp('pool', hm[:, 0, 0, 0:1], hm[:, 0, 0, 0:1], allk('Z') + allk('Z2'), ['Z'])
                assert X_ is Z
                k.memset('pool', Sprev[:, :, :, 0:1], 0.0, ['Sprev'])
                k.cp('pool', Sprev[:, :, :, 1:256], Z[:, :, :, 0:255], ['Z'], ['Sprev'])
                for j in range(2):
                    c0 = 256 + 4 * j
                    k.cp('pool', Sprev[:, :, :, c0], h0[:, :, :, j], ['h0'], ['Sprev'])
                    k.cp('pool', Sprev[:, :, :, c0 + 1:c0 + 4], Z[:, :, :, c0:c0 + 3], ['Z'], ['Sprev'])
                for q, c in enumerate((255, 259, 263)):
                    k.cp('dve', fin[:, :, :, q], Z[:, :, :, c], ['Z'], ['fin'])
                with self.nc.allow_non_contiguous_dma(reason="ssm state out"):
                    for ri, (pn, sn) in enumerate((('p_ssm_re', 's_ssm_re'), ('p_ssm_im', 's_ssm_im'))):
                        k.dma('sp', d[pn][l].rearrange("(a g) p -> (g p) a", g=2), fin[:, :, ri, 0], reads=['fin'])
                        for j in range(2):
                            k.dma('sp', d[sn][l, j].rearrange("(a g) p -> (g p) a", g=2), fin[:, :, ri, 1 + j], reads=['fin'])
                k.barrier()
            self.chk('s5_scan')
            with contextlib.ExitStack() as po:
                ysT = k.sb(po, [128, 2, NT], F32, 'ysT')
                gb = k.sb(po, [128, 2, NT], BF16, 'gb')
                ycb = k.sb(po, [128, 2, NT], BF16, 'ycb')
                tg = k.sb(po, [128, 2, 512], F32, 'tg')
                sgl = k.sb(po, [128, 2, 512], F32, 'sgl')
                sq = k.sb(po, [128, 2, 512], BF16, 'sqc')
                rs = k.sb(po, [128, 2, 512], F32, 'rsc')
                wgl = k.sb(po, [128, 2, 256], BF16, 'wgl')
                sv_ = k.sb(po, [128, 8], F32, 'sv')
                k.dma('pool', wgl[:, :, :], d['w_glu'][l].rearrange("(c p) n -> p c n", p=128), writes=['wgl'])
                with self.nc.allow_non_contiguous_dma(reason="small"):
                    for q, nm in enumerate(('ssm_d', 'b_glu', 'gn_c_g')):
                        k.dma('sp', sv_[:, 2 * q:2 * q + 2], d[nm][l].rearrange("(h p) -> p h", p=128), writes=['sv'])
                nb = 0
                for hh in range(2):
                    for j in range(8):
                        bank = nb % 4
                        nb += 1
                        for i in range(j + 1):
                            k.mm(ps[bank][:, 0:NC_], Kbd[:, hh, j - i, :], uT[:, hh, i, :], i == 0, False, ['Kbd', 'uT'], ['ps%d' % bank])
                        for a4 in range(4):
                            for ri in range(2):
                                if a4 < 3:
                                    k.mm(ps[bank][32 * a4:32 * a4 + 32, 0:NC_], Wout[:, hh, ri, j, 32 * a4:32 * a4 + 32], Sprev[:, 4 * hh + a4, ri, :],
                                         False, False, ['Wout', 'Sprev'], ['ps%d' % bank])
                                else:
                                    k.mm(ps[bank][64:128, 0:NC_], Wout3[:, hh, ri, j, :], Sprev[:, 4 * hh + a4, ri, :],
                                         False, ri == 1, ['Wout', 'Sprev'], ['ps%d' % bank])
                        k.stt(ysT[:, hh, :].rearrange("p (c j) -> p c j", j=8)[:, :, j], uT[:, hh, j, :], sv_[:, hh:hh + 1], ps[bank][:, 0:NC_],
                              ALU.mult, ALU.add, ['uT', 'sv', 'ps%d' % bank], ['ysT'])
                self.chk('s5_y')
                n2 = 0
                for hh in range(2):
                    for b, (t0, tn) in enumerate(TB):
                        i2 = n2 % 2
                        n2 += 1
                        y_ = ysT[:, hh, t0:t0 + tn]
                        k.tt('pool', tg[:, i2, :tn], y_, y_, ALU.mult, ['ysT'], ['tg%d' % i2])
                        k.ts('dve', tg[:, i2, :tn], tg[:, i2, :tn], 0.044715, 1.0, ALU.mult, ALU.add, ['tg%d' % i2], ['tg%d' % i2])
                        k.tt('dve', tg[:, i2, :tn], tg[:, i2, :tn], y_, ALU.mult, ['tg%d' % i2, 'ysT'], ['tg%d' % i2])
                        k.act(tg[:, i2, :tn], tg[:, i2, :tn], AF.Sigmoid, ['tg%d' % i2], ['tg%d' % i2], scale=1.5957691216057308)
                        k.tt('dve', gb[:, hh, t0:t0 + tn], tg[:, i2, :tn], y_, ALU.mult, ['tg%d' % i2, 'ysT'], ['gb'])
                for n in range(2):
                    for b, (t0, tn) in enumerate(TB):
                        bank = 4 + n2 % 2
                        i2 = n2 % 2
                        n2 += 1
                        for c in range(2):
                            k.mm(ps[bank][:, :tn], wgl[:, c, n * 128:(n + 1) * 128], gb[:, c, t0:t0 + tn], c == 0, c == 1, ['wgl', 'gb'], ['ps%d' % bank])
                        k.act(sgl[:, i2, :tn], ps[bank][:, :tn], AF.Sigmoid, ['ps%d' % bank, 'sv'], ['sgl%d' % i2], bias=sv_[:, 2 + n:3 + n])
                        k.tt('dve', ysT[:, n, t0:t0 + tn], gb[:, n, t0:t0 + tn], sgl[:, i2, :tn], ALU.mult, ['gb', 'sgl%d' % i2, 'ysT'], ['ysT'])
                for b, (t0, tn) in enumerate(TB):
                    bank = 6 + b % 2
                    for n in range(2):
                        k.act(sq[:, n, :tn], ysT[:, n, t0:t0 + tn], AF.Square, ['ysT'], ['sqc%d' % n])
                        k.mm(ps[bank][:, :tn], self.ones_bf[:, :], sq[:, n, :tn], n == 0, n == 1, ['sqc%d' % n, 'ones'], ['ps%d' % bank])
                    r = rs[:, b % 2, :tn]
                    rk = 'rsc%d' % (b % 2)
                    k.act(r, ps[bank][:, :tn], AF.Ln, ['ps%d' % bank], [rk], scale=1.0 / 256, bias=EPS)
                    k.act(r, r, AF.Exp, [rk], [rk], scale=-0.5)
                    for n in range(2):
                        k.stt(ycb[:, n, t0:t0 + tn], ysT[:, n, t0:t0 + tn], sv_[:, 4 + n:5 + n], r, ALU.mult, ALU.mult, ['ysT', 'sv', rk], ['ycb'])
                self.chk('s5_glu')
                self.add_proj('w_out', l, 768, 2, ycb, 'ycb', po)
                k.barrier()

    def finish(self, raw=False):
        k = self.k
        d = k.dram
        with contextlib.ExitStack() as ph:
            yst = k.sb(ph, [128, 3, D], F32, 'yst')
            if raw:
                src = self.xT
                skey = lambda b: ['xT.%d' % b]
            else:
                src = None
            if not raw:
                xn = k.sb(ph, [128, 8, 512], F32, 'xn')
            blocks = [(128 * i, 128, 0) for i in range(16)] + [(NT - 128, 128, 64)]
            for bi, (t0, tn, r0) in enumerate(blocks):
                b = bkey(t0 + r0)
                sl = bi % 3
                if not raw and ((t0 + r0) % 512 == 0):
                    self._norm_block(b, xn, ph)
                for half in range(2):
                    bank = 2 * sl + half
                    for c in range(4):
                        kk = half * 4 + c
                        if raw:
                            inp = self.xT[:, kk, t0:t0 + tn]
                            rk = ['xT.3', 'xT.4'] if r0 else ['xT.%d' % b]
                        else:
                            if r0:
                                inp = xn[:, kk, 0:128]
                            else:
                                o = t0 - TB[b][0]
                                inp = xn[:, kk, o:o + tn]
                            rk = ['xn']
                        k.tr(self.ps[bank][:tn, c * 128:(c + 1) * 128], inp, self.cid(), rk + ['cst'], ['ps%d' % bank])
                    k.cp('act' if half == 0 else 'dve', yst[:tn, sl, half * 512:(half + 1) * 512], self.ps[bank][:tn, :],
                         ['ps%d' % bank], ['yst%d' % sl])
                if r0 == 0:
                    k.dma('sp', d['y_p'][t0:t0 + tn, :], yst[:tn, sl, :], reads=['yst%d' % sl])
                elif raw:
                    k.dma('sp', d['y_s'][:, :], yst[64:128, sl, :], reads=['yst%d' % sl])
                else:
                    k.dma('sp', d['y_s'][:, :], yst[0:64, sl, :], reads=['yst%d' % sl])
            k.barrier()

    def _norm_block(self, b, xn, ph):
        k = self.k
        t0, tn = TB[b]
        if not hasattr(self, '_nb'):
            self._nb = (k.sb(ph, [128, 2, 512], BF16, 'sqf'), k.sb(ph, [128, 512], F32, 'rsf'))
        sq, rs = self._nb
        bank = 6
        for kk in range(8):
            sl = kk % 2
            k.act(sq[:, sl, :tn], self.xT[:, kk, t0:t0 + tn], AF.Square, ['xT.%d' % b], ['sqf%d' % sl])
            k.mm(self.ps[bank][:, :tn], self.ones_bf[:, :], sq[:, sl, :tn], kk == 0, kk == 7, ['sqf%d' % sl, 'ones'], ['ps6'])
        k.act(rs[:, :tn], self.ps[bank][:, :tn], AF.Ln, ['ps6'], ['rsf'], scale=1.0 / D, bias=EPS)
        k.act(rs[:, :tn], rs[:, :tn], AF.Exp, ['rsf'], ['rsf'], scale=-0.5)
        for kk in range(8):
            k.stt(xn[:, kk, :tn], self.xT[:, kk, t0:t0 + tn], self.gains[:, 6, kk:kk + 1], rs[:, :tn], ALU.mult, ALU.mult,
                  ['xT.%d' % b, 'rsf', 'gains'], ['xn'])

    def build(self):
        k = self.k
        self.setup()
        self.layers()
        k.muted = False
        self.finish(raw=self.flag('raw'))
        k.barrier()
        k.es.close()

    def layers(self):
        k = self.k
        for l in range(L):
            if self.phases is not None and l > 0 and not self.on('l1'):
                break
            if self.on('mixnorm') and not self.on('mlstm'):
                with contextlib.ExitStack() as ph:
                    self.rmsnorm_fm(0 + l, ph)
                    k.barrier()
            if self.on('mlstm'):
                self.phase_mlstm(l)
            if self.on('sb'):
                self.phase_sb(l)
            if self.on('s5'):
                self.phase_s5(l)
            if self.on('cross'):
                self.phase_cross(l)
            if self.on('ffn'):
                self.phase_ffn(l)


def build_program(phases=None, dbg=None):
    nc = bass.Bass("TRN2", target_bir_lowering=False)
    p = Prog(nc, phases, dbg)
    p.build()
    return nc, p


WEIGHT_NAMES = ['ln_mix_g', 'w_in', 'b_in', 'gn_a_g', 'gn_b_g', 'gn_c_g', 'ssm_a_re', 'ssm_a_im', 'ssm_log_dt',
                'ssm_b_re', 'ssm_b_im', 'ssm_c_re', 'ssm_c_im', 'ssm_d', 'w_glu', 'b_glu', 'w_out', 'ln_x_g', 'ln_mem_g',
                'w_xq', 'w_xk', 'w_xv', 'w_xo', 'ln_ffn_g', 'w_ffn_a', 'w_ffn_b', 'ffn_conv_w', 'ffn_conv_b', 'w_ffn_down',
                'ln_f_g']


def make_in_maps(inp, cores):
    f = lambda a: np.ascontiguousarray(np.asarray(a, dtype=np.float32))
    cst = make_consts()
    shared = {n: f(inp[n]) for n in WEIGHT_NAMES}
    maps = []
    for c in cores:
        s2 = slice(2 * c, 2 * c + 2)
        m = dict(shared)
        m['cst'] = cst
        m['xp'] = f(inp['x_prompt'][c])
        m['xs'] = f(np.asarray(inp['x_sample'])[s2].reshape(2 * TSQ, D))
        m['csbk'] = f(np.asarray(inp['cache_sb_k'])[:, s2].reshape(L, 2, 1024, 384))
        m['csbv'] = f(np.asarray(inp['cache_sb_v'])[:, s2].reshape(L, 2, 1024, 384))
        m['smc'] = f(np.asarray(inp['state_mlstm_c'])[:, s2])
        m['smn'] = f(np.asarray(inp['state_mlstm_n'])[:, s2])
        m['smm'] = f(np.asarray(inp['state_mlstm_m'])[:, s2])
        m['ssr'] = f(np.asarray(inp['state_ssm_re'])[:, s2])
        m['ssi'] = f(np.asarray(inp['state_ssm_im'])[:, s2])
        m['sfc'] = f(np.asarray(inp['state_ffn_conv'])[:, s2])
        m['cmk'] = f(np.asarray(inp['cache_mem_k'])[:, s2].reshape(L, 2, 256, D))
        m['cmv'] = f(np.asarray(inp['cache_mem_v'])[:, s2].reshape(L, 2, 256, D))
        m['memp'] = f(inp['mem_prompt'][c])
        maps.append(m)
    return maps


def assemble(results):
    n = len(results)
    g = lambda name: [np.asarray(r[name]) for r in results]
    st1 = lambda name: np.stack(g(name), axis=1)
    cat1 = lambda name: np.concatenate(g(name), axis=1)
    y_prompt = np.stack(g('y_p'), 0)
    y_sample = np.concatenate([a.reshape(2, TSQ, D) for a in g('y_s')], 0)
    p_sb_k = st1('p_sb_k').reshape(L, n, TP, 6, 64)
    p_sb_v = st1('p_sb_v').reshape(L, n, TP, 6, 64)
    p_mem_k = st1('p_mem_k').reshape(L, n, 256, 4, 256)
    p_mem_v = st1('p_mem_v').reshape(L, n, 256, 4, 256)
    s_sb_k = np.concatenate([a.reshape(L, 2, TSQ, 6, 64) for a in g('s_sb_k')], 1)
    s_sb_v = np.concatenate([a.reshape(L, 2, TSQ, 6, 64) for a in g('s_sb_v')], 1)
    outs = (y_prompt, y_sample, p_sb_k, p_sb_v, st1('p_ml_c'), st1('p_ml_n'), st1('p_ml_m'),
            st1('p_ssm_re'), st1('p_ssm_im'), st1('p_ffn_conv'), p_mem_k, p_mem_v,
            s_sb_k, s_sb_v, cat1('s_ml_c'), cat1('s_ml_n'), cat1('s_ml_m'), cat1('s_ssm_re'), cat1('s_ssm_im'),
            cat1('s_ffn_conv'))
    return tuple(np.ascontiguousarray(o.astype(np.float32)) for o in outs)


def kernel(**inputs):
    nc, _ = build_program()
    maps = make_in_maps(inputs, list(range(NCORES)))
    res = run_bass_kernel_spmd(nc, maps, core_ids=list(range(NCORES)))
    return assemble(res.results)
```

```python
import contextlib
import numpy as np
import concourse.bass as bass
import concourse.mybir as mybir
from concourse.bass_utils import run_bass_kernel_spmd

F32 = mybir.dt.float32
BF16 = mybir.dt.bfloat16
AF = mybir.ActivationFunctionType
ALU = mybir.AluOpType
AX = mybir.AxisListType

NCORES = 8
L = 2
D = 1024
TP = 2048
TSQ = 32
NT = TP + 2 * TSQ
EPS = 1e-6
N_IN = 2952
DFF = 2816
NF = 22
TB = [(0, 512), (512, 512), (1024, 512), (1536, 512), (2048, 64)]
TMB = [(128 * i, 128) for i in range(16)] + [(2048, 32), (2080, 32)]
NDS = 40
NEG = -30000.0

C_ID, C_NTI, C_SBNEG, C_SBM, C_MLNEG, C_BLK, C_MISC = 0, 128, 256, 384, 512, 640, 768
NCST = 832
NCBF = 768


def make_consts():
    c = np.zeros((128, NCST), np.float32)
    p = np.arange(128)
    c[:, C_ID:C_ID + 128] = np.eye(128)
    c[:, C_NTI:C_NTI + 128] = -(p[:, None] >= p[None, :]).astype(np.float32)
    c[:, C_SBNEG:C_SBNEG + 128] = NEG * (p[:, None] >= p[None, :])
    c[:, C_SBM:C_SBM + 128] = (p[:, None] < p[None, :]).astype(np.float32)
    c[:, C_MLNEG:C_MLNEG + 128] = NEG * (p[:, None] > p[None, :])
    c[:, C_BLK:C_BLK + 128] = (p[:, None] // 64 == p[None, :] // 64)
    m = C_MISC
    c[:, m + 0] = (p < 64)
    c[:, m + 1] = (p >= 64)
    c[:, m + 2] = (p < 4)
    c[:, m + 3] = (p >= 4) & (p < 8)
    for h in range(4):
        c[:, m + 4 + h] = (p == h) | (p == 4 + h)
    c[:, m + 8] = ((p // 16) % 2 == 0)
    c[:, m + 10] = -1.0 * ((p >= 4) & (p < 8))
    c[:, m + 16] = (p >= 96)
    for h in range(4):
        c[:, m + 12 + h] = (p == h)
    c[:, m + 9] = ((p // 16) % 2 == 1)
    return c


class KB:
    def __init__(self, nc):
        self.nc = nc
        self.es = contextlib.ExitStack()
        self.eng = {'pe': nc.tensor, 'act': nc.scalar, 'dve': nc.vector, 'pool': nc.gpsimd, 'sp': nc.sync}
        self.sem = {}
        self.cnt = {}
        for e in ('pe', 'act', 'dve', 'pool'):
            self.sem[e] = self.es.enter_context(nc.semaphore('s_' + e))
            self.cnt[e] = 0
        self.dsem = [self.es.enter_context(nc.semaphore('d%d' % i)) for i in range(NDS)]
        self.dcnt = [0] * NDS
        self.dnext = 0
        self.waited = {e: {} for e in self.eng}
        self.lastw = {}
        self.readers = {}
        self.nalloc = 0
        self.dram = {}
        self.bank_rr = 0
        self.muted = False

    def sb(self, stack, shape, dt, name=None):
        self.nalloc += 1
        return stack.enter_context(self.nc.sbuf_tensor('%s_%d' % (name or 't', self.nalloc), list(shape), dt))

    def din(self, name, shape, dt=F32):
        t = self.nc.dram_tensor(name, list(shape), dt, kind="ExternalInput").ap()
        self.dram[name] = t
        return t

    def dout(self, name, shape, dt=F32):
        t = self.nc.dram_tensor(name, list(shape), dt, kind="ExternalOutput").ap()
        self.dram[name] = t
        return t

    def _h(self, s):
        return self.sem[s[1]] if s[0] == 'e' else self.dsem[s[1]]

    def _deps(self, reads, writes):
        need = {}

        def add(sv):
            if sv is None:
                return
            s, v = sv
            if need.get(s, 0) < v:
                need[s] = v
        for r in reads:
            for s, v in self.lastw.get(r, {}).items():
                add((s, v))
        for w in writes:
            for s, v in self.lastw.get(w, {}).items():
                add((s, v))
            for s, v in self.readers.get(w, {}).items():
                add((s, v))
        return need

    def _wait(self, e, need):
        for s, v in need.items():
            if e == 'pe' and s == ('e', 'pe'):
                continue
            if self.waited[e].get(s, 0) >= v:
                continue
            self.eng[e].wait_ge(self._h(s), v)
            self.waited[e][s] = v

    def _commit(self, sv, reads, writes):
        for w in writes:
            self.lastw.setdefault(w, {})[sv[0]] = sv[1]
            self.readers[w] = {}
        for r in reads:
            if r in writes:
                continue
            d = self.readers.setdefault(r, {})
            d[sv[0]] = max(d.get(sv[0], 0), sv[1])

    def op(self, e, fn, reads=(), writes=()):
        if self.muted:
            return
        self._wait(e, self._deps(reads, writes))
        ins = fn(self.eng[e])
        self.cnt[e] += 1
        ins.then_inc(self.sem[e], 1)
        self._commit((('e', e), self.cnt[e]), reads, writes)

    def dma(self, q, out, in_, reads=(), writes=(), **kw):
        if self.muted:
            return
        i = self.dnext
        self.dnext = (i + 1) % NDS
        need = self._deps(reads, writes)
        if self.dcnt[i] > 0:
            need[('d', i)] = max(need.get(('d', i), 0), self.dcnt[i])
        self._wait(q, need)
        ins = self.eng[q].dma_start(out=out, in_=in_, **kw)
        self.dcnt[i] += 16
        ins.then_inc(self.dsem[i], 16)
        self._commit((('d', i), self.dcnt[i]), reads, writes)

    def barrier(self, engines=('pe', 'act', 'dve', 'pool', 'sp')):
        if self.muted:
            return
        need = {}
        for e in ('pe', 'act', 'dve', 'pool'):
            if self.cnt[e] > 0:
                need[('e', e)] = self.cnt[e]
        for i in range(NDS):
            if self.dcnt[i] > 0:
                need[('d', i)] = self.dcnt[i]
        for e in engines:
            n2 = dict(need)
            self._wait(e, n2)

    def mm(self, out, lhsT, rhs, start, stop, reads, writes):
        self.op('pe', lambda e: e.matmul(out, lhsT=lhsT, rhs=rhs, start=start, stop=stop), reads, writes)

    def tr(self, out, in_, ident, reads, writes):
        self.op('pe', lambda e: e.transpose(out=out, in_=in_, identity=ident), reads, writes)

    def act(self, out, in_, func, reads, writes, **kw):
        self.op('act', lambda e: e.activation(out=out, in_=in_, func=func, **kw), reads, writes)

    def tt(self, eng, out, in0, in1, op, reads, writes):
        self.op(eng, lambda e: e.tensor_tensor(out=out, in0=in0, in1=in1, op=op), reads, writes)

    def ts(self, eng, out, in0, s1, s2, op0, op1, reads, writes):
        if s2 is None:
            self.op(eng, lambda e: e.tensor_scalar(out=out, in0=in0, scalar1=s1, scalar2=None, op0=op0), reads, writes)
        else:
            self.op(eng, lambda e: e.tensor_scalar(out=out, in0=in0, scalar1=s1, scalar2=s2, op0=op0, op1=op1), reads, writes)

    def stt(self, out, in0, scalar, in1, op0, op1, reads, writes):
        self.op('dve', lambda e: e.scalar_tensor_tensor(out=out, in0=in0, scalar=scalar, in1=in1, op0=op0, op1=op1), reads, writes)

    def cp(self, eng, out, in_, reads, writes):
        if eng == 'act':
            self.act(out, in_, AF.Copy, reads, writes)
        else:
            self.op(eng, lambda e: e.tensor_copy(out=out, in_=in_), reads, writes)

    def memset(self, eng, ap, val, writes):
        self.op(eng, lambda e: e.memset(ap, val), (), writes)


def bkey(t0):
    return min(t0 // 512, 4)


class Stop(Exception):
    pass


class Prog:
    def chk(self, tag):
        if self.phases is not None and ('stop:' + tag) in self.phases:
            self.k.barrier()
            self.k.muted = True

    def __init__(self, nc, phases=None, dbg=None):
        self.nc = nc
        self.k = KB(nc)
        self.phases = phases
        self.n_dummy = 0
        self.burst_every = 24
        self.dbg = dbg or {}
        self.declare_io()

    def on(self, name):
        return self.phases is None or name in self.phases

    def flag(self, name):
        return self.phases is not None and name in self.phases

    def declare_io(self):
        k = self.k
        i = k.din
        i('xp', [TP, D]); i('xs', [2 * TSQ, D])
        i('csbk', [L, 2, 1024, 384]); i('csbv', [L, 2, 1024, 384])
        i('smc', [L, 2, 4, 96, 96]); i('smn', [L, 2, 4, 96]); i('smm', [L, 2, 4])
        i('ssr', [L, 2, 16, 64]); i('ssi', [L, 2, 16, 64])
        i('sfc', [L, 2, 2, DFF])
        i('cmk', [L, 2, 256, D]); i('cmv', [L, 2, 256, D])
        i('memp', [256, D])
        i('ln_mix_g', [L, D]); i('w_in', [L, D, N_IN]); i('b_in', [L, N_IN])
        i('gn_a_g', [L, 384]); i('gn_b_g', [L, 384]); i('gn_c_g', [L, 256])
        for n in ('ssm_a_re', 'ssm_a_im', 'ssm_log_dt'):
            i(n, [L, 16, 64])
        i('ssm_b_re', [L, 16, 64, 16]); i('ssm_b_im', [L, 16, 64, 16])
        i('ssm_c_re', [L, 16, 16, 64]); i('ssm_c_im', [L, 16, 16, 64])
        i('ssm_d', [L, 256]); i('w_glu', [L, 256, 256]); i('b_glu', [L, 256])
        i('w_out', [L, D, D]); i('ln_x_g', [L, D]); i('ln_mem_g', [L, D])
        for n in ('w_xq', 'w_xk', 'w_xv', 'w_xo'):
            i(n, [L, D, D])
        i('ln_ffn_g', [L, D]); i('w_ffn_a', [L, D, DFF]); i('w_ffn_b', [L, D, DFF])
        i('ffn_conv_w', [L, 3, DFF]); i('ffn_conv_b', [L, DFF]); i('w_ffn_down', [L, DFF, D])
        i('ln_f_g', [D]); i('cst', [128, NCST])
        o = k.dout
        o('y_p', [TP, D]); o('y_s', [2 * TSQ, D])
        o('p_sb_k', [L, TP, 384]); o('p_sb_v', [L, TP, 384])
        o('p_ml_c', [L, 4, 96, 96]); o('p_ml_n', [L, 4, 96]); o('p_ml_m', [L, 4])
        o('p_ssm_re', [L, 16, 64]); o('p_ssm_im', [L, 16, 64])
        o('p_ffn_conv', [L, 2, DFF]); o('p_mem_k', [L, 256, D]); o('p_mem_v', [L, 256, D])
        o('s_sb_k', [L, 2 * TSQ, 384]); o('s_sb_v', [L, 2 * TSQ, 384])
        o('s_ml_c', [L, 2, 4, 96, 96]); o('s_ml_n', [L, 2, 4, 96]); o('s_ml_m', [L, 2, 4])
        o('s_ssm_re', [L, 2, 16, 64]); o('s_ssm_im', [L, 2, 16, 64])
        o('s_ffn_conv', [L, 2, 2, DFF])
        for name, shape in self.dbg.items():
            o(name, shape)

    def setup(self):
        k, nc = self.k, self.nc
        es = k.es
        d = k.dram
        self.xT = k.sb(es, [128, 8, NT], F32, 'xT')
        self.hT = k.sb(es, [128, 8, NT], BF16, 'hT')
        self.cst = k.sb(es, [128, NCST], F32, 'cst')
        self.cbf = k.sb(es, [128, NCBF], BF16, 'cbf')
        self.ones_bf = k.sb(es, [128, 128], BF16, 'ones')
        self.nones_bf = k.sb(es, [128, 128], BF16, 'nones')
        self.ones32 = k.sb(es, [128, 128], F32, 'ones32')
        self.gains = k.sb(es, [128, 7, 8], F32, 'gains')
        self.ps = [es.enter_context(nc.psum_tensor('ps%d' % i, [128, 512], F32)) for i in range(8)]
        k.dma('sp', self.cst[:], d['cst'][:, :], writes=['cst'])
        k.cp('dve', self.cbf[:], self.cst[:, 0:NCBF], ['cst'], ['cbf'])
        k.memset('dve', self.ones_bf[:], 1.0, ['ones'])
        k.memset('dve', self.nones_bf[:], -1.0, ['nones'])
        k.memset('dve', self.ones32[:], 1.0, ['ones32'])
        with nc.allow_non_contiguous_dma(reason="small gain vectors"):
            for j, (nm, l) in enumerate([('ln_mix_g', 0), ('ln_mix_g', 1), ('ln_x_g', 0), ('ln_x_g', 1),
                                         ('ln_ffn_g', 0), ('ln_ffn_g', 1)]):
                k.dma('sp', self.gains[:, j, :], d[nm][l].rearrange("(k p) -> p k", p=128), writes=['gains'])
            k.dma('sp', self.gains[:, 6, :], d['ln_f_g'].rearrange("(k p) -> p k", p=128), writes=['gains'])
        self.load_xT()

    def cid(self):
        return self.cst[:, C_ID:C_ID + 128]

    def load_xT(self):
        k = self.k
        d = k.dram
        with contextlib.ExitStack() as ph:
            stg = k.sb(ph, [128, 4, D], F32, 'xstg')
            for bi, (t0, tn) in enumerate(TMB):
                sl = bi % 4
                src = d['xp'][t0:t0 + tn, :] if t0 < TP else d['xs'][t0 - TP:t0 - TP + tn, :]
                k.dma('sp', stg[:tn, sl, :], src, writes=['xstg%d' % sl])
                for half in range(2):
                    bank = 2 * sl + half
                    for c in range(4):
                        kk = half * 4 + c
                        k.tr(self.ps[bank][:, c * 128:c * 128 + tn], stg[:tn, sl, kk * 128:(kk + 1) * 128],
                             self.cst[:tn, C_ID:C_ID + tn], ['xstg%d' % sl, 'cst'], ['ps%d' % bank])
                    src_ps = self.ps[bank][:, :].rearrange("p (c t) -> p c t", c=4)[:, :, :tn]
                    k.cp('act' if half == 0 else 'dve', self.xT[:, half * 4:half * 4 + 4, t0:t0 + tn], src_ps,
                         ['ps%d' % bank], ['xT.%d' % bkey(t0)])
            k.barrier()

    def rmsnorm_fm(self, gidx, ph, out_fn=None, out_keys=None):
        k = self.k
        sq = k.sb(ph, [128, 2, 512], BF16, 'sq')
        rs = k.sb(ph, [128, 2, 512], F32, 'rs')
        n = 0
        for b, (t0, tn) in enumerate(TB):
            bank = b % 2
            for kk in range(8):
                sl = n % 2
                n += 1
                k.act(sq[:, sl, :tn], self.xT[:, kk, t0:t0 + tn], AF.Square, ['xT.%d' % b], ['sq%d' % sl])
                k.mm(self.ps[bank][:, :tn], self.ones_bf[:, :], sq[:, sl, :tn], kk == 0, kk == 7,
                     ['sq%d' % sl, 'ones'], ['ps%d' % bank])
            r = rs[:, b % 2, :tn]
            rk = 'rs%d' % (b % 2)
            k.act(r, self.ps[bank][:, :tn], AF.Ln, ['ps%d' % bank], [rk], scale=1.0 / D, bias=EPS)
            k.act(r, r, AF.Exp, [rk], [rk], scale=-0.5)
            for kk in range(8):
                if out_fn is None:
                    dst, wk = self.hT[:, kk, t0:t0 + tn], ['hT.%d' % b]
                else:
                    dst, wk = out_fn(kk, t0, tn), out_keys(b)
                k.stt(dst, self.xT[:, kk, t0:t0 + tn], self.gains[:, gidx, kk:kk + 1], r, ALU.mult, ALU.mult,
                      ['xT.%d' % b, rk, 'gains'], wk)

    def add_proj(self, wname, l, row0, nch, yT, ykey, ph, banks=(4, 5, 6, 7), wt=None):
        k = self.k
        if wt is None:
            wt = k.sb(ph, [128, 2, nch, 256], BF16, 'wo')
        src = k.dram[wname][l][row0:row0 + 128 * nch, :].rearrange("(c p) n -> p c n", p=128)
        j = 0
        for n2 in range(4):
            sl = n2 % 2
            k.dma('pool', wt[:, sl, :, :], src[:, :, n2 * 256:(n2 + 1) * 256], writes=['wo%d' % sl])
            for nn in range(2):
                n = 2 * n2 + nn
                for b, (t0, tn) in enumerate(TB):
                    bank = banks[j % len(banks)]
                    j += 1
                    for c in range(nch):
                        k.mm(self.ps[bank][:, :tn], wt[:, sl, c, nn * 128:(nn + 1) * 128], yT[:, c, t0:t0 + tn], c == 0, c == nch - 1,
                             ['wo%d' % sl, ykey], ['ps%d' % bank])
                    k.tt('dve', self.xT[:, n, t0:t0 + tn], self.xT[:, n, t0:t0 + tn], self.ps[bank][:, :tn], ALU.add,
                         ['ps%d' % bank, 'xT.%d' % b], ['xT.%d' % b])

    def phase_sb(self, l):
        k = self.k
        d = k.dram
        ps = self.ps
        cb = self.cbf
        with contextlib.ExitStack() as ph:
            qT = k.sb(ph, [128, 3, NT], BF16, 'qT')
            kT = k.sb(ph, [128, 3, NT + 128], BF16, 'kT')
            k.memset('pool', kT[:, :, NT:NT + 128], 0.0, ['kT'])
            vB = k.sb(ph, [128, 16, 384], BF16, 'vB')
            vS = k.sb(ph, [32, 2, 384], BF16, 'vS')
            ybT = k.sb(ph, [128, 3, NT], BF16, 'ybT')
            kTp = k.sb(ph, [128, 2, 3, 1024], BF16, 'kTp')
            vP = k.sb(ph, [128, 2, 8, 384], BF16, 'vP')
            bfm = k.sb(ph, [128, 6], F32, 'bfm')
            gb = k.sb(ph, [128, 3], F32, 'gb')
            with contextlib.ExitStack() as ph2:
                wsb = k.sb(ph2, [128, 8, 1152], BF16, 'wsb')
                brow = k.sb(ph2, [1, 768], BF16, 'brow')
                kvst = k.sb(ph2, [128, 2, 768], F32, 'kvst')
                wsrc = d['w_in'][l].rearrange("(c p) n -> p c n", p=128)
                for c0 in range(0, 8, 2):
                    k.dma('pool', wsb[:, c0:c0 + 2, :], wsrc[:, c0:c0 + 2, 1544:2696], writes=['wsb%d' % (c0 // 2)])
                k.dma('pool', brow[:, :], d['b_in'][l:l + 1, 1928:2696], writes=['brow'])
                with self.nc.allow_non_contiguous_dma(reason="small vectors"):
                    k.dma('sp', bfm[:, :], d['b_in'][l, 1544:2312].rearrange("(c p) -> p c", p=128), writes=['bfm'])
                    k.dma('sp', gb[:, :], d['gn_b_g'][l].rearrange("(c p) -> p c", p=128), writes=['gb'])
                for j in range(2):
                    k.dma('pool', vP[:, j, :, :], d['csbv'][l, j].rearrange("(n p) f -> p n f", p=128), writes=['vP%d' % j])
                self.chk('sb_load')
                n = 0
                for c in range(6):
                    for b, (t0, tn) in enumerate(TB):
                        bank = n % 2
                        n += 1
                        for kk in range(8):
                            k.mm(ps[bank][:, :tn], wsb[:, kk, c * 128:(c + 1) * 128], self.hT[:, kk, t0:t0 + tn], kk == 0, kk == 7,
                                 ['wsb%d' % (kk // 2), 'hT.%d' % b], ['ps%d' % bank])
                        if c < 3:
                            k.ts('dve', qT[:, c, t0:t0 + tn], ps[bank][:, :tn], bfm[:, c:c + 1], 0.125, ALU.add, ALU.mult,
                                 ['ps%d' % bank, 'bfm'], ['qT'])
                        else:
                            k.act(kT[:, c - 3, t0:t0 + tn], ps[bank][:, :tn], AF.Identity, ['ps%d' % bank, 'bfm'], ['kT'],
                                  bias=bfm[:, c:c + 1])
                self.chk('sb_fm')
                for bi, (t0, tn) in enumerate(TMB):
                    sl = bi % 2
                    b = bkey(t0)
                    for part in range(2):
                        bank = 2 + 2 * sl + part
                        for kk in range(8):
                            k.mm(ps[bank][:tn, 0:384], self.hT[:, kk, t0:t0 + tn], wsb[:, kk, 384 + 384 * part:768 + 384 * part],
                                 kk == 0, False, ['wsb%d' % (kk // 2), 'hT.%d' % b], ['ps%d' % bank])
                        k.mm(ps[bank][:tn, 0:384], self.ones_bf[0:1, :tn], brow[0:1, 384 * part:384 * part + 384], False, True,
                             ['ones', 'brow'], ['ps%d' % bank])
                        k.cp('act' if part == 0 else 'dve', kvst[:tn, sl, 384 * part:384 * part + 384], ps[bank][:tn, 0:384],
                             ['ps%d' % bank], ['kvst%d' % sl])
                    if t0 < TP:
                        k.cp('pool', vB[:, bi, :], kvst[:, sl, 384:768], ['kvst%d' % sl], ['vB'])
                        k.dma('sp', d['p_sb_k'][l, t0:t0 + tn, :], kvst[:tn, sl, 0:384], reads=['kvst%d' % sl])
                        k.dma('sp', d['p_sb_v'][l, t0:t0 + tn, :], kvst[:tn, sl, 384:768], reads=['kvst%d' % sl])
                    else:
                        j = (t0 - TP) // TSQ
                        k.cp('pool', vS[:, j, :], kvst[:tn, sl, 384:768], ['kvst%d' % sl], ['vB'])
                        k.dma('sp', d['s_sb_k'][l, t0 - TP:t0 - TP + tn, :], kvst[:tn, sl, 0:384], reads=['kvst%d' % sl])
                        k.dma('sp', d['s_sb_v'][l, t0 - TP:t0 - TP + tn, :], kvst[:tn, sl, 384:768], reads=['kvst%d' % sl])
                k.barrier()
            self.chk('sb_tm')
            with contextlib.ExitStack() as ph3:
                e32 = k.sb(ph3, [128, 3, 512], F32, 'e32')
                Lp = k.sb(ph3, [128, 3, 512], BF16, 'Lp')
                At = k.sb(ph3, [128, 2, 512], BF16, 'At')
                sqo = k.sb(ph3, [128, 2, 512], BF16, 'sqo')
                rso = k.sb(ph3, [128, 2, 512], F32, 'rso')
                self._tile_n = 0
                self._grp_n = 0
                k.memset('pool', sqo[:], 0.0, ['sqo0', 'sqo1'])
                qTs = k.sb(ph3, [128, 3, 2, 2 * TSQ], BF16, 'qTs')
                for par in range(2):
                    k.ts('dve', qTs[:, :, par, :], qT[:, :, TP:NT], self.cst[:, C_MISC + par:C_MISC + par + 1], None, ALU.mult, None, ['qT', 'cst'], ['qTs'])
                with contextlib.ExitStack() as phk:
                    kst = k.sb(phk, [128, 4, 384], F32, 'kst')
                    n = 0
                    for j in range(2):
                        for kb in range(8):
                            sl = n % 4
                            n += 1
                            k.dma('sp', kst[:, sl, :], d['csbk'][l, j, kb * 128:(kb + 1) * 128, :], writes=['kst%d' % sl])
                            bank = 4 + sl
                            for c in range(3):
                                k.tr(ps[bank][:, c * 128:(c + 1) * 128], kst[:, sl, c * 128:(c + 1) * 128], self.cid(),
                                     ['kst%d' % sl, 'cst'], ['ps%d' % bank])
                            k.cp('act', kTp[:, j, :, kb * 128:(kb + 1) * 128],
                                 ps[bank][:, 0:384].rearrange("p (c t) -> p c t", c=3), ['ps%d' % bank], ['kTp%d' % j])
                    k.barrier()

                Ls = k.sb(ph3, [128, 2, 4, 512], BF16, 'Ls')
                tiles = []

                def group(segs, W, kblocks, fin):
                    g = self._grp_n % 2
                    self._grp_n += 1
                    obank = 4 + g
                    okey = 'ps%d' % obank
                    touched = set()
                    nkb = len(kblocks)
                    for bi, (kbid, nk, clo, diag) in enumerate(kblocks):
                        first, last = bi == 0, bi == nkb - 1
                        t = self._tile_n % 3
                        t2 = self._tile_n % 2
                        sbank = abank = self._tile_n % 4
                        self._tile_n += 1
                        sk = ak = 'ps%d' % sbank
                        ek, lk, atk = 'e32%d' % t, 'Lp%d' % t, 'At%d' % t2

                        def stage1(first=first, last=last, bi=bi, kbid=kbid, nk=nk, clo=clo, diag=diag, t=t, sbank=sbank, sk=sk, ek=ek, lk=lk):
                            if first:
                                k.memset('pool', Ls[:, g, :, :W], 0.0, ['Ls%d_0' % g, 'Ls%d_1' % g, 'Ls%d_2' % g, 'Ls%d_3' % g])
                            for si, (c0, ncol, q_ap, pb, kfn, vfn) in enumerate(segs):
                                lo = clo if len(segs) == 1 else 0
                                k.mm(ps[sbank][:, c0 + lo:c0 + ncol], kfn(kbid), q_ap[:, lo:ncol], si == 0, False,
                                     ['qT', 'qTs', 'kT', 'kTp0', 'kTp1'], [sk])
                            k.act(e32[:nk, t, clo:W], ps[sbank][:nk, clo:W], AF.Exp, [sk], [ek])

                        def stage1b(first=first, last=last, bi=bi, kbid=kbid, nk=nk, clo=clo, diag=diag, t=t, sbank=sbank, sk=sk, ek=ek, lk=lk):
                            k.act(Lp[:nk, t, clo:W], e32[:nk, t, clo:W], AF.Ln, [ek], [lk], bias=1.0)
                            if diag is not None:
                                dc, dn = diag
                                for (c0, ncol, q_ap, pb, kfn, vfn) in segs:
                                    k.tt('pool', Lp[:dn, t, c0 + dc:c0 + dc + dn], Lp[:dn, t, c0 + dc:c0 + dc + dn],
                                         cb[:dn, C_SBM:C_SBM + dn], ALU.mult, [lk, 'cbf'], [lk])
                            if not last:
                                k.tt('pool', Ls[:nk, g, (bi + 1) % 4, clo:W], Ls[:nk, g, bi % 4, clo:W], Lp[:nk, t, clo:W], ALU.add,
                                     ['Ls%d_%d' % (g, bi % 4), lk], ['Ls%d_%d' % (g, (bi + 1) % 4)])

                        def stage2(first=first, last=last, bi=bi, kbid=kbid, nk=nk, clo=clo, diag=diag, t=t, t2=t2, abank=abank, ak=ak, lk=lk, atk=atk):
                            k.mm(ps[abank][:, clo:W], cb[:nk, C_NTI:C_NTI + 128], Lp[:nk, t, clo:W], False, first and diag is None,
                                 [lk, 'cbf'], [ak])
                            if not first:
                                k.mm(ps[abank][:, clo:W], self.nones_bf[:, :], Ls[:, g, bi % 4, clo:W], False, diag is None,
                                     ['Ls%d_%d' % (g, bi % 4), 'nones'], [ak])
                            if diag is not None:
                                dc, dn = diag
                                for si, (c0, ncol, q_ap, pb, kfn, vfn) in enumerate(segs):
                                    k.mm(ps[abank][:, c0 + dc:c0 + dc + dn], cb[:dn, C_ID:C_ID + 128], cb[:dn, C_SBNEG:C_SBNEG + dn],
                                         False, si == len(segs) - 1, ['cbf'], [ak])
                            k.act(At[:nk, t2, clo:W], ps[abank][:nk, clo:W], AF.Exp, [ak], [atk])

                        def stage2b(first=first, last=last, bi=bi, kbid=kbid, nk=nk, clo=clo, diag=diag, t=t, t2=t2, abank=abank, ak=ak, lk=lk, atk=atk):
                            for si, (c0, ncol, q_ap, pb, kfn, vfn) in enumerate(segs):
                                lo = clo if len(segs) == 1 else 0
                                st = pb not in touched
                                touched.add(pb)
                                k.mm(ps[obank][pb:pb + 64, c0 + lo:c0 + ncol], vfn(kbid), At[:nk, t2, c0 + lo:c0 + ncol], st, last,
                                     [atk, 'vB', 'vP0', 'vP1'], [okey])

                        epiA = epiB = None
                        if last:
                            pbs = sorted(set(s_[3] for s_ in segs))
                            sbk = 6 + g

                            def epiA(pbs=pbs, sbk=sbk):
                                for pb in pbs:
                                    k.act(sqo[pb:pb + 64, g, :W], ps[obank][pb:pb + 64, :W], AF.Square, [okey], ['sqo%d' % g])
                                k.mm(ps[sbk][:, :W], cb[:, C_BLK:C_BLK + 128], sqo[:, g, :W], True, True, ['sqo%d' % g, 'cbf'], ['ps%d' % sbk])

                            def epiB(pbs=pbs, sbk=sbk):
                                for pb in pbs:
                                    k.act(rso[pb:pb + 64, g, :W], ps[sbk][pb:pb + 64, :W], AF.Ln, ['ps%d' % sbk], ['rso%d' % g],
                                          scale=1.0 / 64, bias=EPS)
                                    k.act(rso[pb:pb + 64, g, :W], rso[pb:pb + 64, g, :W], AF.Exp, ['rso%d' % g], ['rso%d' % g], scale=-0.5)
                                fin(obank, okey, rso, g)
                        tiles.append((stage1, stage2, epiA, epiB, stage1b, stage2b))

                def run_pipeline():
                    pendA = pendB = None
                    n = len(tiles)
                    def epi_step(i):
                        nonlocal pendA, pendB, pendA_b
                        if pendB is not None:
                            pendB()
                            pendB = None
                        if pendA is not None:
                            pendA()
                            pendB = pendA_b
                            pendA = None
                        if i >= 0 and tiles[i][2] is not None:
                            pendA, pendA_b = tiles[i][2], tiles[i][3]
                    pendA_b = None
                    for j in range(min(3, n)):
                        tiles[j][0]()
                    for j in range(min(2, n)):
                        tiles[j][4]()
                    for i in range(n + 1):
                        if i + 3 < n:
                            tiles[i + 3][0]()
                        if i + 2 < n:
                            tiles[i + 2][4]()
                        if i < n:
                            tiles[i][1]()
                        if i >= 1:
                            tiles[i - 1][5]()
                            epi_step(i - 1)
                    epi_step(-1)
                    epi_step(-1)
                    del tiles[:]

                for h in range(6 if self.on('sb_prompt') else 0):
                    c, pb = h // 2, 64 * (h % 2)
                    for qg in range(4):
                        q0 = 512 * qg
                        seg = (0, 512, qT[pb:pb + 64, c, q0:q0 + 512], pb,
                               lambda kb, c=c, pb=pb: kT[pb:pb + 64, c, kb * 128:(kb + 1) * 128],
                               lambda kb, h=h: vB[:, kb, 64 * h:64 * h + 64])
                        kbl = []
                        for kb in range(4 * qg + 3, -1, -1):
                            r = kb - 4 * qg
                            if r >= 0:
                                kbl.append((kb, 128, 128 * r, (128 * r, 128)))
                            else:
                                kbl.append((kb, 128, 0, None))

                        def fin(obank, okey, rso, g, c=c, pb=pb, q0=q0):
                            k.stt(ybT[pb:pb + 64, c, q0:q0 + 512], ps[obank][pb:pb + 64, :512], gb[pb:pb + 64, c:c + 1],
                                  rso[pb:pb + 64, g, :512], ALU.mult, ALU.mult, [okey, 'rso%d' % g, 'gb'], ['ybT'])
                        group([seg], 512, kbl, fin)
                for j in range(2):
                    q0 = TP + TSQ * j
                    segs = []
                    for h in range(6):
                        c, pb = h // 2, 64 * (h % 2)

                        def kfn(kb, c=c, pb=pb, j=j, q0=q0):
                            if kb == 8:
                                return kT[:, c, q0:q0 + 128]
                            return kTp[:, j, c, kb * 128:(kb + 1) * 128]

                        def vfn(kb, h=h, j=j):
                            if kb == 8:
                                return vS[:, j, 64 * h:64 * h + 64]
                            return vP[:, j, kb, 64 * h:64 * h + 64]
                        segs.append((32 * h, 32, qTs[:, c, h % 2, TSQ * j:TSQ * j + TSQ], pb, kfn, vfn))
                    kbl = [(8, 32, 0, (0, 32))] + [(kb, 128, 0, None) for kb in range(7, -1, -1)]

                    def fin(obank, okey, rso, g, q0=q0):
                        for h in range(6):
                            c, pb = h // 2, 64 * (h % 2)
                            k.stt(ybT[pb:pb + 64, c, q0:q0 + TSQ], ps[obank][pb:pb + 64, 32 * h:32 * h + 32], gb[pb:pb + 64, c:c + 1],
                                  rso[pb:pb + 64, g, 32 * h:32 * h + 32], ALU.mult, ALU.mult, [okey, 'rso%d' % g, 'gb'], ['ybT'])
                    group(segs, 192, kbl, fin)
                run_pipeline()
                k.barrier()
            if 'ybT' in self.dbg:
                k.dbg_dump = True
                for c in range(3):
                    k.dma('pool', d['ybT'][c], ybT[:, c, :], reads=['ybT'])
                k.barrier()
            self.add_proj('w_out', l, 384, 3, ybT, 'ybT', ph)
            k.barrier()

    def phase_cross(self, l):
        k = self.k
        d = k.dram
        ps = self.ps
        with contextlib.ExitStack() as ph:
            self.rmsnorm_fm(2 + l, ph)
            mkT = k.sb(ph, [128, 3, 8, 256], BF16, 'mkT')
            mv = k.sb(ph, [128, 3, 2, D], BF16, 'mv')
            gm = k.sb(ph, [128, 8], F32, 'gm')
            with self.nc.allow_non_contiguous_dma(reason="small vectors"):
                k.dma('sp', gm[:, :], d['ln_mem_g'][l].rearrange("(c p) -> p c", p=128), writes=['gm'])
            for j in range(2):
                k.dma('pool', mv[:, 1 + j, :, :], d['cmv'][l, j].rearrange("(n p) f -> p n f", p=128), writes=['mv'])
            with contextlib.ExitStack() as ph2:
                mst2 = [k.sb(ph2, [128, 2, D], F32, 'mst%d' % q) for q in range(2)]
                mnT = k.sb(ph2, [128, 8, 256], BF16, 'mnT')
                ss = k.sb(ph2, [128, 4], F32, 'ss')
                junk = k.sb(ph2, [128, D], BF16, 'junk')
                ost = k.sb(ph2, [128, 2, 512], F32, 'ost')
                wk = k.sb(ph2, [128, 8, D], BF16, 'wk')
                wv = k.sb(ph2, [128, 8, D], BF16, 'wv')
                for (wt_, nm) in ((wk, 'w_xk'), (wv, 'w_xv')):
                    src = d[nm][l].rearrange("(c p) n -> p c n", p=128)
                    for c0 in range(0, 8, 2):
                        k.dma('pool', wt_[:, c0:c0 + 2, :], src[:, c0:c0 + 2, :], writes=['%s%d' % (nm, c0 // 2)])
                for j in range(2):
                    mst, mk_ = mst2[j], 'mst%d' % j
                    k.dma('sp', mst[:, :, :], d['cmk'][l, j].rearrange("(n p) f -> p n f", p=128), writes=[mk_])
                    for mb in range(2):
                        for half in range(2):
                            bank = 2 * mb + half
                            for c in range(4):
                                kk = half * 4 + c
                                k.tr(ps[bank][:, c * 128:(c + 1) * 128], mst[:, mb, kk * 128:(kk + 1) * 128], self.cid(),
                                     [mk_, 'cst'], ['ps%d' % bank])
                            k.cp('act' if half == 0 else 'dve', mkT[:, 1 + j, half * 4:half * 4 + 4, mb * 128:(mb + 1) * 128],
                                 ps[bank][:, :].rearrange("p (c t) -> p c t", c=4), ['ps%d' % bank], ['mkT'])
                self.chk('x_a')
                mst = mst2[0]
                k.dma('sp', mst[:, :, :], d['memp'].rearrange("(n p) f -> p n f", p=128), writes=['mst', 'mst0'])
                for mb in range(2):
                    k.act(junk[:, :], mst[:, mb, :], AF.Square, ['mst'], ['junk', 'ss'], accum_out=ss[:, mb:mb + 1])
                k.act(ss[:, 2:4], ss[:, 0:2], AF.Ln, ['ss'], ['ss2'], scale=1.0 / D, bias=EPS)
                k.act(ss[:, 2:4], ss[:, 2:4], AF.Exp, ['ss2'], ['ss2'], scale=-0.5)
                for mb in range(2):
                    k.ts('dve', mst[:, mb, :], mst[:, mb, :], ss[:, 2 + mb:3 + mb], None, ALU.mult, None, ['mst', 'ss2'], ['mst'])
                for mb in range(2):
                    for half in range(2):
                        bank = 4 + 2 * mb + half
                        for c in range(4):
                            kk = half * 4 + c
                            k.tr(ps[bank][:, c * 128:(c + 1) * 128], mst[:, mb, kk * 128:(kk + 1) * 128], self.cid(),
                                 ['mst', 'cst'], ['ps%d' % bank])
                        for c in range(4):
                            kk = half * 4 + c
                            k.ts('dve', mnT[:, kk, mb * 128:(mb + 1) * 128], ps[bank][:, c * 128:(c + 1) * 128], gm[:, kk:kk + 1], None,
                                 ALU.mult, None, ['ps%d' % bank, 'gm'], ['mnT'])
                self.chk('x_b')
                for n in range(8):
                    bank = n % 2
                    for kk in range(8):
                        k.mm(ps[bank][:, 0:256], wk[:, kk, n * 128:(n + 1) * 128], mnT[:, kk, :], kk == 0, kk == 7,
                             ['w_xk%d' % (kk // 2), 'mnT'], ['ps%d' % bank])
                    k.cp('act', mkT[:, 0, n, :], ps[bank][:, 0:256], ['ps%d' % bank], ['mkT'])
                self.chk('x_c')
                n = 0
                for (wt_, nm, onm) in ((wk, 'w_xk', 'p_mem_k'), (wv, 'w_xv', 'p_mem_v')):
                    for mb in range(2):
                        for nh in range(2):
                            bank = 2 + n % 2
                            sl = n % 2
                            n += 1
                            for kk in range(8):
                                k.mm(ps[bank][:, :], mnT[:, kk, mb * 128:(mb + 1) * 128], wt_[:, kk, nh * 512:(nh + 1) * 512], kk == 0, kk == 7,
                                     ['%s%d' % (nm, kk // 2), 'mnT'], ['ps%d' % bank])
                            k.cp('act', ost[:, sl, :], ps[bank][:, :], ['ps%d' % bank], ['ost%d' % sl])
                            if nm == 'w_xv' and not self.flag('x_nomv'):
                                k.cp('pool', mv[:, 0, mb, nh * 512:(nh + 1) * 512], ost[:, sl, :], ['ost%d' % sl], ['mv'])
                            if not self.flag('x_nodma'):
                                k.dma('sp', d[onm][l, mb * 128:(mb + 1) * 128, nh * 512:(nh + 1) * 512], ost[:, sl, :], reads=['ost%d' % sl])
                            self.chk('x_c%d' % n)
                k.barrier()
            self.chk('x_d')
            oT = k.sb(ph, [128, 8, NT], BF16, 'oT')
            with contextlib.ExitStack() as ph3:
                wq = k.sb(ph3, [128, 2, 8, 256], BF16, 'wq')
                qh = k.sb(ph3, [128, 2, 2, NT], BF16, 'qh')
                pT = k.sb(ph3, [128, 2, 2, 512], BF16, 'pT')
                rden = k.sb(ph3, [128, 2, 512], F32, 'rden')
                qsrc = d['w_xq'][l].rearrange("(c p) n -> p c n", p=128)
                blocks = [(t0, tn, 0) for (t0, tn) in TB[:4]] + [(TP, TSQ, 1), (TP + TSQ, TSQ, 2)]
                it = 0
                def qgroups(h):
                    sl = h % 2
                    k.dma('pool', wq[:, sl, :, :], qsrc[:, :, 256 * h:256 * h + 256], writes=['wq%d' % sl])
                    gl = []
                    for dc in range(2):
                        for b, (t0, tn) in enumerate(TB):
                            def g_(dc=dc, b=b, t0=t0, tn=tn, sl=sl):
                                bank = b % 2
                                for kk in range(8):
                                    k.mm(ps[bank][:, :tn], wq[:, sl, kk, dc * 128:(dc + 1) * 128], self.hT[:, kk, t0:t0 + tn], kk == 0, kk == 7,
                                         ['wq%d' % sl, 'hT.%d' % b], ['ps%d' % bank])
                                k.cp('act' if b % 2 == 0 else 'dve', qh[:, sl, dc, t0:t0 + tn], ps[bank][:, :tn], ['ps%d' % bank], ['qh%d' % sl])
                            gl.append(g_)
                    return gl
                for g_ in qgroups(0):
                    g_()
                for h in range(4):
                    sl = h % 2
                    pend = qgroups(h + 1) if h < 3 else []
                    for (t0, tn, mset) in blocks:
                        i2 = it % 2
                        it += 1
                        for mc in range(2):
                            bank = 2 + mc
                            for dc in range(2):
                                k.mm(ps[bank][:, :tn], mkT[:, mset, 2 * h + dc, mc * 128:(mc + 1) * 128], qh[:, sl, dc, t0:t0 + tn], dc == 0, dc == 1,
                                     ['mkT', 'qh%d' % sl], ['ps%d' % bank])
                            k.act(pT[:, i2, mc, :tn], ps[bank][:, :tn], AF.Exp, ['ps%d' % bank], ['pT%d' % i2], scale=1.0 / 16)
                        for _ in range(2):
                            if pend:
                                pend.pop(0)()
                        for mc in range(2):
                            k.mm(ps[4][:, :tn], self.ones_bf[:, :], pT[:, i2, mc, :tn], mc == 0, mc == 1, ['pT%d' % i2, 'ones'], ['ps4'])
                        k.act(rden[:, i2, :tn], ps[4][:, :tn], AF.Ln, ['ps4'], ['rden%d' % i2])
                        k.act(rden[:, i2, :tn], rden[:, i2, :tn], AF.Exp, ['rden%d' % i2], ['rden%d' % i2], scale=-1.0)
                        for dc in range(2):
                            bank = 5 + dc
                            for mc in range(2):
                                k.mm(ps[bank][:, :tn], mv[:, mset, mc, 256 * h + 128 * dc:256 * h + 128 * dc + 128], pT[:, i2, mc, :tn], mc == 0, mc == 1,
                                     ['mv', 'pT%d' % i2], ['ps%d' % bank])
                            k.tt('dve', oT[:, 2 * h + dc, t0:t0 + tn], ps[bank][:, :tn], rden[:, i2, :tn], ALU.mult,
                                 ['ps%d' % bank, 'rden%d' % i2], ['oT'])
                    for g_ in pend:
                        g_()
                k.barrier()
            self.chk('x_e')
            self.add_proj('w_xo', l, 0, 8, oT, 'oT', ph)
            k.barrier()

    def phase_ffn(self, l):
        k = self.k
        d = k.dram
        ps = self.ps
        NG = 11
        with contextlib.ExitStack() as ph:
            self.rmsnorm_fm(4 + l, ph)
            cw = k.sb(ph, [128, 3, NF], F32, 'cw')
            cbi = k.sb(ph, [128, NF], F32, 'cbi')
            pv = k.sb(ph, [128, 2, 2, NF], F32, 'pv')
            cvst = k.sb(ph, [128, 2, 3, 128], F32, 'cvst')
            with self.nc.allow_non_contiguous_dma(reason="small conv params / states"):
                for j in range(3):
                    k.dma('sp', cw[:, j, :], d['ffn_conv_w'][l, j].rearrange("(c p) -> p c", p=128), writes=['cw'])
                k.dma('sp', cbi[:, :], d['ffn_conv_b'][l].rearrange("(c p) -> p c", p=128), writes=['cw'])
                for j in range(2):
                    for t in range(2):
                        k.dma('sp', pv[:, j, t, :], d['sfc'][l, j, t].rearrange("(c p) -> p c", p=128), writes=['pv'])
            yT = k.sb(ph, [128, NG, NT], BF16, 'yT')
            wa = k.sb(ph, [128, 2, 8, 128], BF16, 'wa')
            wb = k.sb(ph, [128, 2, 8, 128], BF16, 'wb')
            aS = k.sb(ph, [128, 2, 2 + TP], F32, 'aS')
            aSs = k.sb(ph, [128, 2, 2, 2 + TSQ], F32, 'aSs')
            cc = k.sb(ph, [128, 2, 512], F32, 'cc')
            sg = k.sb(ph, [128, 2, 512], F32, 'sg')
            wt_down = k.sb(ph, [128, 2, NG, 256], BF16, 'wo')
            asrc = d['w_ffn_a'][l].rearrange("(c p) n -> p c n", p=128)
            bsrc = d['w_ffn_b'][l].rearrange("(c p) n -> p c n", p=128)
            k.memset('pool', aS[:, :, 0:2], 0.0, ['aS0', 'aS1'])
            n = 0
            for g in range(2):
                if True:
                    for fl in range(NG):
                        fc = g * NG + fl
                        sl = fl % 2
                        k.dma('pool', wa[:, sl, :, :], asrc[:, :, fc * 128:(fc + 1) * 128], writes=['wa%d' % sl])
                        k.dma('pool', wb[:, sl, :, :], bsrc[:, :, fc * 128:(fc + 1) * 128], writes=['wb%d' % sl])
                        ak = 'aS%d' % sl
                        for j in range(2):
                            k.cp('pool', aSs[:, sl, j, 0:2], pv[:, j, :, fc], ['pv'], [ak])
                        for b, (t0, tn) in enumerate(TB):
                            bankA = b % 2
                            bank = 2 + b % 2
                            i2 = n % 2
                            n += 1
                            ck = 'cc%d' % i2
                            for kk in range(8):
                                k.mm(ps[bankA][:, :tn], wa[:, sl, kk, :], self.hT[:, kk, t0:t0 + tn], kk == 0, kk == 7,
                                     ['wa%d' % sl, 'hT.%d' % b], ['ps%d' % bankA])
                            if b < 4:
                                cur, m1, m2 = aS[:, sl, 2 + t0:2 + t0 + tn], aS[:, sl, 1 + t0:1 + t0 + tn], aS[:, sl, t0:t0 + tn]
                                co = cc[:, i2, :tn]
                                so = sg[:, i2, :tn]
                                pa_ = ps[bankA][:, :tn]
                                pb_ = ps[bank][:, :tn]
                                yo = yT[:, fl, t0:t0 + tn]
                            else:
                                cur, m1, m2 = aSs[:, sl, :, 2:2 + TSQ], aSs[:, sl, :, 1:1 + TSQ], aSs[:, sl, :, 0:TSQ]
                                co = cc[:, i2, 0:2 * TSQ].rearrange("p (j t) -> p j t", j=2)
                                so = sg[:, i2, 0:2 * TSQ].rearrange("p (j t) -> p j t", j=2)
                                pa_ = ps[bankA][:, 0:2 * TSQ].rearrange("p (j t) -> p j t", j=2)
                                pb_ = ps[bank][:, 0:2 * TSQ].rearrange("p (j t) -> p j t", j=2)
                                yo = yT[:, fl, t0:t0 + tn].rearrange("p (j t) -> p j t", j=2)
                            k.cp('act', cur, pa_, ['ps%d' % bankA], [ak])
                            k.act(co, pa_, AF.Identity, ['ps%d' % bankA, 'cw'], [ck], scale=cw[:, 2, fc:fc + 1], bias=cbi[:, fc:fc + 1])
                            for kk in range(8):
                                k.mm(ps[bank][:, :tn], wb[:, sl, kk, :], self.hT[:, kk, t0:t0 + tn], kk == 0, kk == 7,
                                     ['wb%d' % sl, 'hT.%d' % b], ['ps%d' % bank])
                            k.stt(co, m1, cw[:, 1, fc:fc + 1], co, ALU.mult, ALU.add, [ak, 'cw', ck], [ck])
                            k.stt(co, m2, cw[:, 0, fc:fc + 1], co, ALU.mult, ALU.add, [ak, 'cw', ck], [ck])
                            k.act(so, co, AF.Silu, [ck], ['sg%d' % i2])
                            k.tt('dve', yo, so, pb_, ALU.mult, ['sg%d' % i2, 'ps%d' % bank], ['yT'])
                        for si, tend in enumerate((TP, TP + TSQ, NT)):
                            bank = 4 + si
                            for kk in range(8):
                                k.mm(ps[bank][:, 0:128], self.hT[:, kk, tend - 128:tend], wa[:, sl, kk, :], kk == 0, kk == 7,
                                     ['wa%d' % sl, 'hT.3', 'hT.4'], ['ps%d' % bank])
                        for si in range(3):
                            bank = 4 + si
                            k.cp('dve', cvst[96:128, sl, si, :], ps[bank][96:128, 0:128], ['ps%d' % bank], ['cvst%d.%d' % (sl, si)])
                            dst = d['p_ffn_conv'][l, :, fc * 128:(fc + 1) * 128] if si == 0 else d['s_ffn_conv'][l, si - 1, :, fc * 128:(fc + 1) * 128]
                            k.dma('sp', dst, cvst[126:128, sl, si, :], reads=['cvst%d.%d' % (sl, si)])
                self.add_proj('w_ffn_down', l, 128 * NG * g, NG, yT, 'yT', ph, wt=wt_down)
            k.barrier()

    def phase_mlstm(self, l):
        k = self.k
        d = k.dram
        ps = self.ps
        cst = self.cst
        MI = C_MISC
        SC = 96 ** -0.5
        with contextlib.ExitStack() as ph:
            yaT = k.sb(ph, [128, 3, NT], BF16, 'yaT')
            wqk = k.sb(ph, [128, 8, 768], BF16, 'wqk')
            wkvo = k.sb(ph, [128, 8, 1152], BF16, 'wkvo')
            wg = k.sb(ph, [128, 8, 64], BF16, 'wg')
            brow = k.sb(ph, [1, 1152], BF16, 'browa')
            bqk = k.sb(ph, [96, 8], F32, 'bqk')
            bif = k.sb(ph, [8, 4], F32, 'bif')
            m0t = k.sb(ph, [8, 4], F32, 'm0t')
            gna = k.sb(ph, [128, 384], F32, 'gna')
            NB = k.sb(ph, [8, NT], F32, 'NB')
            G = k.sb(ph, [8, NT], F32, 'G')
            M = k.sb(ph, [8, NT], F32, 'M')
            Cst = k.sb(ph, [96, 4, 128], F32, 'Cst')
            Cbf = k.sb(ph, [96, 4, 97], BF16, 'Cbf')
            c0l = k.sb(ph, [96, 4, 128], F32, 'c0l')
            wsrc = d['w_in'][l].rearrange("(c p) n -> p c n", p=128)
            for c0 in range(0, 8, 2):
                k.dma('pool', wqk[:, c0:c0 + 2, :], wsrc[:, c0:c0 + 2, 0:768], writes=['wqk%d' % (c0 // 2)])
                k.dma('pool', wkvo[:, c0:c0 + 2, :], wsrc[:, c0:c0 + 2, 384:1536], writes=['wkvo%d' % (c0 // 2)])
            k.memset('pool', wg[:], 0.0, ['wg'])
            k.memset('dve', m0t[:], 0.0, ['m0t'])
            with self.nc.allow_non_contiguous_dma(reason="small vectors"):
                for q in range(2):
                    k.dma('pool', wg[:, :, 4 * q:4 * q + 4], wsrc[:, :, 1536:1540], writes=['wg'])
                    k.dma('pool', wg[:, :, 32 + 4 * q:36 + 4 * q], wsrc[:, :, 1540:1544], writes=['wg'])
                    k.dma('sp', bif[4 * q:4 * q + 4, 0:1], d['b_in'][l, 1536:1540].rearrange("(p o) -> p o", o=1), writes=['bif'])
                    k.dma('sp', bif[4 * q:4 * q + 4, 1:2], d['b_in'][l, 1540:1544].rearrange("(p o) -> p o", o=1), writes=['bif'])
                    for j in range(2):
                        k.dma('sp', m0t[4 * q:4 * q + 4, 1 + j:2 + j], d['smm'][l, j].rearrange("(p o) -> p o", o=1), writes=['m0t'])
                k.dma('sp', bqk[:, :], d['b_in'][l, 0:768].rearrange("(c p) -> p c", p=96), writes=['bqk'])
                k.dma('sp', gna[:, :], d['gn_a_g'][l:l + 1, :].to_broadcast([128, 384]), writes=['gna'])
            k.dma('pool', brow[:, :], d['b_in'][l:l + 1, 384:1536], writes=['browa'])
            k.ts('dve', bif[:, 2:3], bif[:, 1:2], -1.0, None, ALU.mult, None, ['bif'], ['bif'])
            if self.on('mixnorm'):
                with contextlib.ExitStack() as phn:
                    self.rmsnorm_fm(0 + l, phn)
                    k.barrier()
            for b, (t0, tn) in enumerate(TB):
                for kk in range(8):
                    k.mm(ps[0][0:32, :tn], wg[:, kk, 0:32], self.hT[:, kk, t0:t0 + tn], kk == 0, kk == 7, ['wg', 'hT.%d' % b], ['ps0'])
                for kk in range(8):
                    k.mm(ps[1][0:32, :tn], wg[:, kk, 32:64], self.hT[:, kk, t0:t0 + tn], kk == 0, kk == 7, ['wg', 'hT.%d' % b], ['ps1'])
                k.act(G[:, t0:t0 + tn], ps[0][0:8, :tn], AF.Identity, ['ps0', 'bif'], ['G'], bias=bif[:, 0:1])
                k.act(M[:, t0:t0 + tn], ps[1][0:8, :tn], AF.Exp, ['ps1', 'bif'], ['M'], bias=bif[:, 2:3], scale=-1.0)
                k.act(M[:, t0:t0 + tn], M[:, t0:t0 + tn], AF.Ln, ['M'], ['M'], bias=1.0)
            segs = [(0, TP, 0), (TP, TSQ, 1), (TP + TSQ, TSQ, 2)]
            for (t0, tn, si) in segs:
                k.op('dve', lambda e, t0=t0, tn=tn: e.tensor_tensor_scan(
                    out=NB[:, t0:t0 + tn], data0=self.ones32[0:8, 0:1].to_broadcast([8, tn]), data1=M[:, t0:t0 + tn],
                    initial=0.0, op0=ALU.mult, op1=ALU.add), ['M', 'ones32'], ['NB'])
            k.tt('dve', G[:, :], G[:, :], NB[:, :], ALU.add, ['G', 'NB'], ['G'])
            for (t0, tn, si) in segs:
                k.op('dve', lambda e, t0=t0, tn=tn, si=si: e.tensor_tensor_scan(
                    out=M[:, t0:t0 + tn], data0=G[:, t0:t0 + tn], data1=G[:, t0:t0 + tn],
                    initial=m0t[:, si:si + 1], op0=ALU.max, op1=ALU.max), ['G', 'm0t', 'NB'], ['M'])
            self.chk('ml_gates')
            with contextlib.ExitStack() as ph2:
                qc = k.sb(ph2, [96, 4, 128], BF16, 'qc')
                kc = k.sb(ph2, [96, 4, 128], BF16, 'kc')
                ktmp = k.sb(ph2, [96, 4, 128], F32, 'ktmp')
                kTM = k.sb(ph2, [128, 384], BF16, 'kTM')
                vc = k.sb(ph2, [128, 4, 97], BF16, 'vc')
                vw = k.sb(ph2, [128, 4, 97], BF16, 'vw')
                og = k.sb(ph2, [128, 384], F32, 'og')
                gm1 = k.sb(ph2, [8, 128], F32, 'gm1')
                LH = k.sb(ph2, [8, 4, 128], F32, 'LH')
                RH = k.sb(ph2, [8, 128], F32, 'RH')
                gsm = k.sb(ph2, [8, 3, 128], F32, 'gsm')
                sml = k.sb(ph2, [8, 8], F32, 'sml')
                gTM = k.sb(ph2, [128, 24], F32, 'gTM')
                wT = k.sb(ph2, [128, 512], F32, 'wT')
                Sw = k.sb(ph2, [128, 512], BF16, 'Sw')
                tmp = k.sb(ph2, [128, 4, 97], F32, 'tmp')
                nd = k.sb(ph2, [128, 4, 97], F32, 'nd')
                hg = k.sb(ph2, [128, 384], F32, 'hg')
                sq = k.sb(ph2, [128, 384], F32, 'sqa')
                st = k.sb(ph2, [128, 16], F32, 'st')
                decb = k.sb(ph2, [96, 4], F32, 'decb')
                cout = k.sb(ph2, [128, 4, 96], F32, 'cout')
                k.memset('pool', vc[:, :, 96:97], 1.0, ['vc'])
                k.memset('pool', c0l[:], 0.0, ['c0l'])
                chunks = [(128 * i, 128, 0) for i in range(16)] + [(TP, TSQ, 1), (TP + TSQ, TSQ, 2)]
                for ci, (t0, n, si) in enumerate(chunks):
                    b = bkey(t0)
                    hk = 'hT.%d' % b
                    first = (ci == 0) or si > 0
                    last = (ci == 15) or si > 0
                    if first:
                        if si == 0:
                            k.memset('pool', Cst[:], 0.0, ['Cst'])
                        else:
                            j = si - 1
                            k.memset('pool', Cst[:], 0.0, ['Cst'])
                            k.dma('sp', c0l[:, :, 0:96], d['smc'][l, j].rearrange("h v k -> v h k"), writes=['c0l'])
                            with self.nc.allow_non_contiguous_dma(reason="n0 state"):
                                k.dma('sp', Cst[:, :, 96], d['smn'][l, j].rearrange("h k -> k h"), writes=['Cst'])
                            for h in range(4):
                                k.tr(ps[7][:, h * 96:(h + 1) * 96], c0l[:, h, :], cst[0:96, C_ID:C_ID + 96], ['c0l', 'cst'], ['ps7'])
                            k.cp('dve', Cst[:, :, 0:96], ps[7][0:96, 0:384].rearrange("p (h v) -> p h v", h=4), ['ps7'], ['Cst'])
                        k.cp('act', Cbf[:, :, :], Cst[:, :, 0:97], ['Cst'], ['Cbf'])
                    for hc in range(8):
                        bank = hc // 4
                        for kk in range(8):
                            k.mm(ps[bank][0:96, (hc % 4) * 128:(hc % 4) * 128 + n], wqk[:, kk, 96 * hc:96 * hc + 96], self.hT[:, kk, t0:t0 + n],
                                 kk == 0, kk == 7, ['wqk%d' % (kk // 2), hk], ['ps%d' % bank])
                    p0 = ps[0][0:96, :].rearrange("p (h t) -> p h t", h=4)[:, :, :n]
                    p1 = ps[1][0:96, :].rearrange("p (h t) -> p h t", h=4)[:, :, :n]
                    k.tt('dve', qc[:, :, :n], p0, bqk[:, 0:4].unsqueeze(2).to_broadcast([96, 4, n]), ALU.add, ['ps0', 'bqk'], ['qc'])
                    k.tt('dve', ktmp[:, :, :n], p1, bqk[:, 4:8].unsqueeze(2).to_broadcast([96, 4, n]), ALU.add, ['ps1', 'bqk'], ['ktmp'])
                    k.act(kc[:, :, :n], ktmp[:, :, :n], AF.Copy, ['ktmp'], ['kc'], scale=SC)
                    for part in range(3):
                        bank = 2 + part
                        for kk in range(8):
                            k.mm(ps[bank][:n, 0:384], self.hT[:, kk, t0:t0 + n], wkvo[:, kk, 384 * part:384 * part + 384], kk == 0, False,
                                 ['wkvo%d' % (kk // 2), hk], ['ps%d' % bank])
                        k.mm(ps[bank][:n, 0:384], self.ones_bf[0:1, :n], brow[0:1, 384 * part:384 * part + 384], False, True,
                             ['ones', 'browa'], ['ps%d' % bank])
                    k.act(kTM[:n, :], ps[2][:n, 0:384], AF.Copy, ['ps2'], ['kTM'], scale=SC)
                    k.cp('dve', vc[:n, :, 0:96], ps[3][:n, 0:384].rearrange("p (h v) -> p h v", h=4), ['ps3'], ['vc'])
                    k.act(og[:n, :], ps[4][:n, 0:384], AF.Exp, ['ps4'], ['og'], scale=-1.0)
                    k.act(og[:n, :], og[:n, :], AF.Ln, ['og'], ['og'], bias=1.0)
                    k.act(og[:n, :], og[:n, :], AF.Exp, ['og'], ['og'], scale=-1.0)
                    k.ts('dve', gm1[:, :n], G[:, t0:t0 + n], cst[0:8, MI + 2:MI + 3], cst[0:8, MI + 3:MI + 4], ALU.mult, ALU.add, ['G', 'cst'], ['gm1'])
                    for h in range(4):
                        k.ts('dve', LH[:, h, :n], gm1[:, :n], cst[0:8, MI + 4 + h:MI + 5 + h], None, ALU.mult, None, ['gm1', 'cst'], ['LH'])
                    k.ts('dve', RH[:, :n], M[:, t0:t0 + n], cst[0:8, MI + 10:MI + 11], cst[0:8, MI + 2:MI + 3], ALU.mult, ALU.add, ['M', 'cst'], ['RH'])
                    mprev = m0t[:, si:si + 1] if first else M[:, t0 - 1:t0]
                    k.act(gsm[:, 0, :n], M[:, t0:t0 + n], AF.Exp, ['M', 'm0t'], ['gsm'], scale=-1.0, bias=mprev)
                    k.ts('dve', sml[:, 0:1], M[:, t0 + n - 1:t0 + n], -1.0, None, ALU.mult, None, ['M'], ['sml'])
                    k.act(gsm[:, 1, :n], G[:, t0:t0 + n], AF.Exp, ['G', 'sml'], ['gsm'], bias=sml[:, 0:1])
                    k.tt('dve', gsm[:, 2, :n], NB[:, t0:t0 + n], M[:, t0:t0 + n], ALU.subtract, ['NB', 'M'], ['gsm2'])
                    k.act(gsm[:, 2, :n], gsm[:, 2, :n], AF.Exp, ['gsm2'], ['gsm2'])
                    for h in range(4):
                        k.mm(ps[5][:n, h * 128:h * 128 + n], LH[:, h, :n], RH[:, :n], h == 0, False, ['LH', 'RH'], ['ps5'])
                    for h in range(4):
                        k.mm(ps[5][:n, h * 128:h * 128 + n], cst[:n, C_ID:C_ID + n], cst[:n, C_MLNEG:C_MLNEG + n], False, h == 3, ['cst'], ['ps5'])
                    p5 = ps[5][:n, :].rearrange("p (h t) -> p h t", h=4)[:, :, :n]
                    wT3 = wT[:n, :].rearrange("p (h t) -> p h t", h=4)[:, :, :n]
                    Sw3 = Sw[:n, :].rearrange("p (h t) -> p h t", h=4)[:, :, :n]
                    k.act(wT3, p5, AF.Exp, ['ps5'], ['wT'])
                    for h in range(4):
                        k.mm(ps[6][:n, h * 128:h * 128 + n], kc[:, h, :n], qc[:, h, :n], h == 0, h == 3, ['kc', 'qc'], ['ps6'])
                    p6 = ps[6][:n, :].rearrange("p (h t) -> p h t", h=4)[:, :, :n]
                    k.tt('dve', Sw3, p6, wT3, ALU.mult, ['ps6', 'wT'], ['Sw'])
                    for h in range(4):
                        k.mm(ps[7][:n, h * 97:(h + 1) * 97], Sw[:n, h * 128:h * 128 + n], vc[:n, h, :], h == 0, h == 3, ['Sw', 'vc'], ['ps7'])
                    for h in range(4):
                        k.mm(ps[0][:n, h * 97:(h + 1) * 97], qc[:, h, :n], Cbf[:, h, :], h == 0, h == 3, ['qc', 'Cbf'], ['ps0'])
                    for q in range(3):
                        k.mm(ps[1][:n, q * 8:(q + 1) * 8], gsm[:, q, :n], cst[0:8, C_ID:C_ID + 8], q == 0, q == 2, ['gsm', 'gsm2', 'cst'], ['ps1'])
                    k.cp('act', gTM[:n, :], ps[1][:n, 0:24], ['ps1'], ['gTM'])
                    p7 = ps[7][:n, 0:388].rearrange("p (h v) -> p h v", h=4)
                    p0b = ps[0][:n, 0:388].rearrange("p (h v) -> p h v", h=4)
                    k.tt('dve', tmp[:n], p0b, gTM[:n, 0:4].unsqueeze(2).to_broadcast([n, 4, 97]), ALU.mult, ['ps0', 'gTM'], ['tmp'])
                    k.tt('dve', nd[:n], p7, tmp[:n], ALU.add, ['ps7', 'tmp'], ['nd'])
                    k.act(st[:n, 0:4], nd[:n, :, 96], AF.Abs, ['nd'], ['st'])
                    k.tt('dve', st[:n, 0:4], st[:n, 0:4], gTM[:n, 16:20], ALU.max, ['st', 'gTM'], ['st'])
                    k.op('dve', lambda e, n=n: e.reciprocal(out=st[:n, 4:8], in_=st[:n, 0:4]), ['st'], ['st'])
                    hg3 = hg[:n, :].rearrange("p (h v) -> p h v", h=4)
                    k.tt('dve', hg3, nd[:n, :, 0:96], st[:n, 4:8].unsqueeze(2).to_broadcast([n, 4, 96]), ALU.mult, ['nd', 'st'], ['hg'])
                    k.tt('dve', hg[:n, :], hg[:n, :], og[:n, :], ALU.mult, ['hg', 'og'], ['hg'])
                    k.tt('dve', sq[:n, :], hg[:n, :], hg[:n, :], ALU.mult, ['hg'], ['sqa'])
                    k.op('dve', lambda e, n=n: e.tensor_reduce(out=st[:n, 8:12], in_=sq[:n, :].rearrange("p (h v) -> p h v", h=4),
                                                               axis=AX.X, op=ALU.add), ['sqa'], ['st2'])
                    k.act(st[:n, 12:16], st[:n, 8:12], AF.Ln, ['st2'], ['st3'], scale=1.0 / 96, bias=EPS)
                    k.act(st[:n, 12:16], st[:n, 12:16], AF.Exp, ['st3'], ['st3'], scale=-0.5)
                    k.tt('dve', hg3, hg3, st[:n, 12:16].unsqueeze(2).to_broadcast([n, 4, 96]), ALU.mult, ['hg', 'st3'], ['hg'])
                    k.tt('dve', hg[:n, :], hg[:n, :], gna[:n, :], ALU.mult, ['hg', 'gna'], ['hg'])
                    for c in range(3):
                        k.tr(ps[3][:, c * 128:c * 128 + n], hg[:n, c * 128:(c + 1) * 128], cst[:n, C_ID:C_ID + n], ['hg', 'cst'], ['ps3'])
                    k.cp('act', yaT[:, :, t0:t0 + n], ps[3][:, 0:384].rearrange("p (c t) -> p c t", c=3)[:, :, :n], ['ps3'], ['yaT'])
                    k.tt('dve', vw[:n], vc[:n], gTM[:n, 8:12].unsqueeze(2).to_broadcast([n, 4, 97]), ALU.mult, ['vc', 'gTM'], ['vw'])
                    for h in range(4):
                        k.mm(ps[2][0:96, h * 97:(h + 1) * 97], kTM[:n, 96 * h:96 * h + 96], vw[:n, h, :], h == 0, h == 3, ['kTM', 'vw'], ['ps2'])
                    k.ts('dve', sml[:, 4:8], cst[0:8, MI + 12:MI + 16], gsm[:, 0, n - 1:n], None, ALU.mult, None, ['gsm', 'cst'], ['sml2'])
                    k.mm(ps[4][0:96, 0:4], self.ones32[0:8, 0:96], sml[:, 4:8], True, True, ['sml2', 'ones32'], ['ps4'])
                    k.cp('act', decb[:, :], ps[4][0:96, 0:4], ['ps4'], ['decb'])
                    k.tt('dve', Cst[:, :, 0:97], Cst[:, :, 0:97], decb[:, :].unsqueeze(2).to_broadcast([96, 4, 97]), ALU.mult, ['Cst', 'decb'], ['Cst'])
                    k.tt('dve', Cst[:, :, 0:97], Cst[:, :, 0:97], ps[2][0:96, 0:388].rearrange("p (h v) -> p h v", h=4), ALU.add,
                         ['Cst', 'ps2'], ['Cst'])
                    if not last:
                        k.cp('act', Cbf[:, :, :], Cst[:, :, 0:97], ['Cst'], ['Cbf'])
                    else:
                        for h in range(4):
                            k.tr(ps[6][:, h * 96:(h + 1) * 96], Cst[:, h, :], cst[0:96, C_ID:C_ID + 96], ['Cst', 'cst'], ['ps6'])
                        k.cp('act', cout[:, :, :], ps[6][:, 0:384].rearrange("p (h k) -> p h k", h=4), ['ps6'], ['cout'])
                        k.tt('dve', sml[:, 1:2], M[:, t0 + n - 1:t0 + n], NB[:, t0 + n - 1:t0 + n], ALU.subtract, ['M', 'NB'], ['sml3'])
                        if si == 0:
                            dc_, dn_, dm_ = d['p_ml_c'][l], d['p_ml_n'][l], d['p_ml_m'][l]
                        else:
                            dc_, dn_, dm_ = d['s_ml_c'][l, si - 1], d['s_ml_n'][l, si - 1], d['s_ml_m'][l, si - 1]
                        k.dma('sp', dc_.rearrange("h v k -> v h k"), cout[0:96, :, :], reads=['cout'])
                        k.dma('sp', dn_.rearrange("(o h) k -> o h k", o=1), cout[96:97, :, :], reads=['cout'])
                        with self.nc.allow_non_contiguous_dma(reason="m state"):
                            k.dma('sp', dm_.rearrange("(p o) -> p o", o=1), sml[0:4, 1:2], reads=['sml3'])
                k.barrier()
            self.chk('ml_core')
            self.add_proj('w_out', l, 0, 3, yaT, 'yaT', ph)
            k.barrier()

    def phase_s5(self, l):
        k = self.k
        d = k.dram
        ps = self.ps
        cst = self.cst
        MI = C_MISC
        NC_ = NT // 8
        TWO_PI = 2.0 * np.pi

        def bc(ap, shape):
            return ap.unsqueeze(2).to_broadcast(shape)

        with contextlib.ExitStack() as ph:
            WinR = k.sb(ph, [128, 2, 8, 128], BF16, 'WinR')
            WinI = k.sb(ph, [128, 2, 8, 128], BF16, 'WinI')
            Wout = k.sb(ph, [128, 2, 2, 8, 128], BF16, 'Wout')
            Kbd = k.sb(ph, [128, 2, 8, 128], BF16, 'Kbd')
            Win3R = k.sb(ph, [128, 2, 8, 128], BF16, 'Win3R')
            Win3I = k.sb(ph, [128, 2, 8, 128], BF16, 'Win3I')
            Wout3 = k.sb(ph, [128, 2, 2, 8, 64], BF16, 'Wout3')
            uT = k.sb(ph, [128, 2, 8, NC_], BF16, 'uT')
            Sprev = k.sb(ph, [128, 8, 2, NC_], BF16, 'Sprev')
            MUr = k.sb(ph, [128, 8, 8], F32, 'MUr')
            MUi = k.sb(ph, [128, 8, 8], F32, 'MUi')
            nMUi = k.sb(ph, [128, 8, 8], F32, 'nMUi')
            h0 = k.sb(ph, [128, 8, 2, 2], F32, 'h0')
            with contextlib.ExitStack() as pp:
                _n = [0]

                def T(shape, dt=F32):
                    _n[0] += 1
                    return k.sb(pp, shape, dt, 'p%d' % _n[0])
                are, aim, ldt = T([128, 8]), T([128, 8]), T([128, 8])
                with self.nc.allow_non_contiguous_dma(reason="small ssm params"):
                    for t_, nm in ((are, 'ssm_a_re'), (aim, 'ssm_a_im'), (ldt, 'ssm_log_dt')):
                        k.dma('sp', t_[:, :], d[nm][l].rearrange("(a g) p -> (g p) a", g=2), writes=['prm'])
                    for ri, nm in enumerate(('ssr', 'ssi')):
                        for j in range(2):
                            k.dma('sp', h0[:, :, ri, j], d[nm][l, j].rearrange("(a g) p -> (g p) a", g=2), writes=['h0'])
                R_ = ['prm']
                dt = T([128, 8]); ang = T([128, 8]); mag = T([128, 8])
                k.act(dt[:], ldt[:], AF.Exp, R_, R_)
                k.tt('dve', ang[:], aim[:], dt[:], ALU.mult, R_, R_)
                k.tt('dve', mag[:], are[:], dt[:], ALU.mult, R_, R_)
                k.act(mag[:], mag[:], AF.Exp, R_, R_)
                sc = T([128, 2, 8])
                xx = T([128, 8]); ti = T([128, 8], mybir.dt.int32); tf = T([128, 8]); gg = T([128, 8])
                for q in range(2):
                    if q == 0:
                        k.cp('dve', xx[:], ang[:], R_, R_)
                    else:
                        k.ts('dve', xx[:], ang[:], float(np.pi / 2), None, ALU.add, None, R_, R_)
                    k.ts('dve', tf[:], xx[:], float(1.0 / TWO_PI), None, ALU.mult, None, R_, R_)
                    k.cp('dve', ti[:], tf[:], R_, R_)
                    k.cp('dve', tf[:], ti[:], R_, R_)
                    k.stt(xx[:], tf[:], -TWO_PI, xx[:], ALU.mult, ALU.add, R_, R_)
                    k.ts('dve', gg[:], xx[:], float(np.pi), None, ALU.is_gt, None, R_, R_)
                    k.stt(xx[:], gg[:], -TWO_PI, xx[:], ALU.mult, ALU.add, R_, R_)
                    k.ts('dve', gg[:], xx[:], float(-np.pi), None, ALU.is_lt, None, R_, R_)
                    k.stt(xx[:], gg[:], TWO_PI, xx[:], ALU.mult, ALU.add, R_, R_)
                    k.act(sc[:, q, :], xx[:], AF.Sin, R_, R_)
                LAMr = T([128, 9, 8]); LAMi = T([128, 9, 8]); nLAMi = T([128, 9, 8])
                k.memset('dve', LAMr[:, 0, :], 1.0, R_)
                k.memset('dve', LAMi[:, 0, :], 0.0, R_)
                k.tt('dve', LAMr[:, 1, :], mag[:], sc[:, 1, :], ALU.mult, R_, R_)
                k.tt('dve', LAMi[:, 1, :], mag[:], sc[:, 0, :], ALU.mult, R_, R_)
                t1 = T([128, 8]); t2 = T([128, 8])

                def cmul(or_, oi_, ar, ai, br, bi):
                    k.tt('dve', t1[:], ar, br, ALU.mult, R_, R_)
                    k.tt('dve', t2[:], ai, bi, ALU.mult, R_, R_)
                    k.tt('dve', or_, t1[:], t2[:], ALU.subtract, R_, R_)
                    k.tt('dve', t1[:], ar, bi, ALU.mult, R_, R_)
                    k.tt('dve', t2[:], ai, br, ALU.mult, R_, R_)
                    k.tt('dve', oi_, t1[:], t2[:], ALU.add, R_, R_)
                for kk in range(1, 8):
                    cmul(LAMr[:, kk + 1, :], LAMi[:, kk + 1, :], LAMr[:, kk, :], LAMi[:, kk, :], LAMr[:, 1, :], LAMi[:, 1, :])
                k.ts('dve', nLAMi[:], LAMi[:], -1.0, None, ALU.mult, None, R_, R_)
                k.cp('dve', MUr[:, 0, :], LAMr[:, 8, :], R_, ['MU'])
                k.cp('dve', MUi[:, 0, :], LAMi[:, 8, :], R_, ['MU'])
                RM = ['prm', 'MU']
                for kk in range(7):
                    k.tt('dve', t1[:], MUr[:, kk, :], MUr[:, kk, :], ALU.mult, RM, R_)
                    k.tt('dve', t2[:], MUi[:, kk, :], MUi[:, kk, :], ALU.mult, RM, R_)
                    k.tt('dve', MUr[:, kk + 1, :], t1[:], t2[:], ALU.subtract, R_, ['MU'])
                    k.tt('dve', t1[:], MUr[:, kk, :], MUi[:, kk, :], ALU.mult, RM, R_)
                    k.ts('dve', MUi[:, kk + 1, :], t1[:], 2.0, None, ALU.mult, None, R_, ['MU'])
                k.ts('dve', nMUi[:], MUi[:], -1.0, None, ALU.mult, None, RM, ['MU'])
                den = T([128, 8]); nr = T([128, 8]); zr = T([128, 8]); zi = T([128, 8])
                k.tt('dve', den[:], are[:], are[:], ALU.mult, R_, R_)
                k.tt('dve', t1[:], aim[:], aim[:], ALU.mult, R_, R_)
                k.tt('dve', den[:], den[:], t1[:], ALU.add, R_, R_)
                k.op('dve', lambda e: e.reciprocal(out=den[:], in_=den[:]), R_, R_)
                k.ts('dve', nr[:], LAMr[:, 1, :], -1.0, None, ALU.add, None, R_, R_)
                k.tt('dve', t1[:], nr[:], are[:], ALU.mult, R_, R_)
                k.tt('dve', t2[:], LAMi[:, 1, :], aim[:], ALU.mult, R_, R_)
                k.tt('dve', zr[:], t1[:], t2[:], ALU.add, R_, R_)
                k.tt('dve', zr[:], zr[:], den[:], ALU.mult, R_, R_)
                k.tt('dve', t1[:], LAMi[:, 1, :], are[:], ALU.mult, R_, R_)
                k.tt('dve', t2[:], nr[:], aim[:], ALU.mult, R_, R_)
                k.tt('dve', zi[:], t1[:], t2[:], ALU.subtract, R_, R_)
                k.tt('dve', zi[:], zi[:], den[:], ALU.mult, R_, R_)
                bre = T([128, 8, 16]); bim = T([128, 8, 16]); bbr = T([128, 8, 16]); bbi = T([128, 8, 16])
                u1 = T([128, 8, 16]); u2 = T([128, 8, 16])
                with self.nc.allow_non_contiguous_dma(reason="ssm B"):
                    k.dma('sp', bre[:], d['ssm_b_re'][l].rearrange("(a g) p c -> (g p) a c", g=2), writes=['prm'])
                    k.dma('sp', bim[:], d['ssm_b_im'][l].rearrange("(a g) p c -> (g p) a c", g=2), writes=['prm'])
                S16 = [128, 8, 16]

                def cmul16(or_, oi_, ar, ai, br, bi):
                    k.tt('dve', u1[:], ar, bc(br, S16), ALU.mult, R_, R_)
                    k.tt('dve', u2[:], ai, bc(bi, S16), ALU.mult, R_, R_)
                    k.tt('dve', or_, u1[:], u2[:], ALU.subtract, R_, R_)
                    k.tt('dve', u1[:], ar, bc(bi, S16), ALU.mult, R_, R_)
                    k.tt('dve', u2[:], ai, bc(br, S16), ALU.mult, R_, R_)
                    k.tt('dve', oi_, u1[:], u2[:], ALU.add, R_, R_)
                cmul16(bbr[:], bbi[:], bre[:], bim[:], zr[:], zi[:])
                cnat = T([128, 2, 2, 64])
                for ri, nm in enumerate(('ssm_c_re', 'ssm_c_im')):
                    k.dma('sp', cnat[:, ri, :, :], d[nm][l].rearrange("(hh g) c p -> (g c) hh p", hh=2), writes=['prm'])
                X = T([128, 2, 2, 2, 64])
                for g2 in range(2):
                    k.ts('dve', X[:, :, :, g2, :], cnat[:, :, :, :], cst[:, MI + 8 + g2:MI + 9 + g2], None, ALU.mult, None, R_ + ['cst'], R_)
                CT = T([128, 3, 2, 128])
                for ri in range(2):
                    for hh in range(2):
                        k.tr(ps[0][:, (2 * ri + hh) * 128:(2 * ri + hh + 1) * 128],
                             X[:, ri, hh, :, :].rearrange("p g q -> p (g q)"), self.cid(), R_ + ['cst'], ['ps0'])
                k.cp('act', CT[:, 0:2, :, :], ps[0][:, :].rearrange("p (r h c) -> p r h c", r=2, h=2), ['ps0'], R_)
                k.ts('dve', CT[:, 2, :, :], CT[:, 1, :, :], -1.0, None, ALU.mult, None, R_, R_)
                S32 = [128, 8, 32]
                vv = [(T([128, 8, 32]), T([128, 8, 32]), T([128, 8, 32]), T([128, 8, 32])) for _ in range(2)]

                def c8(ri):
                    return CT[:, ri, :, :].rearrange("p h (a x) -> p (h a) x", a=4)
                r4 = lambda t_: t_[:].rearrange("p (h a) x -> p h a x", h=2)
                for j in range(8):
                    lr, li, nli = LAMr[:, j + 1, :], LAMi[:, j + 1, :], nLAMi[:, j + 1, :]
                    v1, v2, v3, v4 = vv[j % 2]
                    kz = ['wv%d.%d' % (j % 2, q) for q in range(4)]
                    k.tt('dve', v1[:], c8(0), bc(lr, S32), ALU.mult, R_, [kz[0]])
                    k.tt('pool', v2[:], c8(1), bc(li, S32), ALU.mult, R_, [kz[1]])
                    k.tt('dve', v3[:], c8(0), bc(nli, S32), ALU.mult, R_, [kz[2]])
                    k.tt('pool', v4[:], c8(1), bc(lr, S32), ALU.mult, R_, [kz[3]])
                    k.tt('dve', Wout[:, :, 0, j, :].rearrange("p h (a x) -> p h a x", a=4), r4(v1), r4(v2), ALU.subtract, [kz[0], kz[1]], ['Wout'])
                    k.tt('dve', Wout[:, :, 1, j, :].rearrange("p h (a x) -> p h a x", a=4), r4(v3), r4(v4), ALU.subtract, [kz[2], kz[3]], ['Wout'])
                Yf = T([128, 2, 8, 256])
                yy = [tuple(T([128, 8, 16]) for _ in range(6)) for _ in range(2)]
                for dd in range(8):
                    a1, a2, a3, a4, pr, pi_ = yy[dd % 2]
                    kz = ['yv%d.%d' % (dd % 2, q) for q in range(6)]
                    lr, li = LAMr[:, dd, :], LAMi[:, dd, :]
                    k.tt('dve', a1[:], bbr[:], bc(lr, S16), ALU.mult, R_, [kz[0]])
                    k.tt('pool', a2[:], bbi[:], bc(li, S16), ALU.mult, R_, [kz[1]])
                    k.tt('dve', a3[:], bbr[:], bc(li, S16), ALU.mult, R_, [kz[2]])
                    k.tt('pool', a4[:], bbi[:], bc(lr, S16), ALU.mult, R_, [kz[3]])
                    k.tt('dve', pr[:], a1[:], a2[:], ALU.subtract, [kz[0], kz[1]], [kz[4]])
                    k.tt('pool', pi_[:], a3[:], a4[:], ALU.add, [kz[2], kz[3]], [kz[5]])
                    for ri, src in enumerate((pr, pi_)):
                        for g2 in range(2):
                            k.ts('dve' if g2 == 0 else 'pool', Yf[:, ri, dd, :].rearrange("p (a g c) -> p a g c", g=2, c=16)[:, :, g2, :], src[:],
                                 cst[:, MI + g2:MI + g2 + 1], None, ALU.mult, None, [kz[4 + ri], 'cst'], ['Yf%d' % dd])
                nb = 0
                for ri, W_ in enumerate((WinR, WinI)):
                    for hh in range(2):
                        for i0 in range(0, 8, 4):
                            bank = 1 + nb % 2
                            nb += 1
                            for ii in range(4):
                                i = i0 + ii
                                k.tr(ps[bank][:, ii * 128:(ii + 1) * 128], Yf[:, ri, 7 - i, hh * 128:(hh + 1) * 128], self.cid(),
                                     ['Yf%d' % (7 - i), 'cst'], ['ps%d' % bank])
                            k.cp('act' if bank == 1 else 'dve', W_[:, hh, i0:i0 + 4, :], ps[bank][:, :].rearrange("p (i c) -> p i c", i=4),
                                 ['ps%d' % bank], ['Win'])
                for W3, W_ in ((Win3R, WinR), (Win3I, WinI)):
                    k.ts('pool', W3[:].rearrange("p h i c -> p (h i c)"), W_[:].rearrange("p h i c -> p (h i c)"),
                         cst[:, MI + 16:MI + 17], None, ALU.mult, None, ['Win', 'cst'], ['Win'])
                k.memset('pool', Wout3[:], 0.0, ['Wout'])
                k.cp('pool', Wout3[:, :, :, :, 32:64], Wout[:, :, :, :, 96:128], ['Wout'], ['Wout'])
                k.memset('pool', Kbd[:], 0.0, ['Kbd'])
                for hh in range(2):
                    for dd in range(8):
                        col = (hh * 8 + dd) * 32
                        f0 = hh == 0 and dd == 0
                        l0 = hh == 1 and dd == 7
                        for a4 in range(3):
                            cs = slice(hh * 128 + 32 * a4, hh * 128 + 32 * a4 + 32)
                            k.mm(ps[3][32 * a4:32 * a4 + 32, col:col + 32], Yf[:, 0, dd, cs], CT[:, 0, hh, 32 * a4:32 * a4 + 32],
                                 f0, False, ['Yf%d' % dd] + R_, ['ps3'])
                            k.mm(ps[3][32 * a4:32 * a4 + 32, col:col + 32], Yf[:, 1, dd, cs], CT[:, 2, hh, 32 * a4:32 * a4 + 32],
                                 False, l0, ['Yf%d' % dd] + R_, ['ps3'])
                        cs = slice(hh * 128 + 64, hh * 128 + 128)
                        k.mm(ps[4][64:128, col:col + 32], Yf[:, 0, dd, cs], CT[:, 0, hh, 96:128], f0, False, ['Yf%d' % dd] + R_, ['ps4'])
                        k.mm(ps[4][64:128, col:col + 32], Yf[:, 1, dd, cs], CT[:, 2, hh, 96:128], False, l0, ['Yf%d' % dd] + R_, ['ps4'])
                for a4 in range(4):
                    src = ps[3] if a4 < 3 else ps[4]
                    k.cp('act', Kbd[32 * a4:32 * a4 + 32, :, :, 32 * a4:32 * a4 + 32],
                         src[32 * a4:32 * a4 + 32, :].rearrange("p (h d c) -> p h d c", h=2, d=8), ['ps3', 'ps4'], ['Kbd'])
                k.barrier()
            self.chk('s5_prep')
            with contextlib.ExitStack() as pm:
                wu = k.sb(pm, [128, 8, 256], BF16, 'wu')
                bu = k.sb(pm, [128, 2], F32, 'bu')
                Z = k.sb(pm, [128, 8, 2, NC_], F32, 'Z')
                Z2 = k.sb(pm, [128, 8, 2, NC_], F32, 'Z2')
                fin = k.sb(pm, [128, 8, 2, 4], F32, 'fin')
                hm = k.sb(pm, [128, 8, 2, 2], F32, 'hm')
                hm2 = k.sb(pm, [128, 8, 2, 2], F32, 'hm2')
                wsrc = d['w_in'][l].rearrange("(c p) n -> p c n", p=128)
                k.dma('pool', wu[:, :, :], wsrc[:, :, 2696:2952], writes=['wu'])
                with self.nc.allow_non_contiguous_dma(reason="small"):
                    k.dma('sp', bu[:, :], d['b_in'][l, 2696:2952].rearrange("(h p) -> p h", p=128), writes=['bu'])
                for hh in range(2):
                    for b, (t0, tn) in enumerate(TB):
                        bank = b % 2
                        for kk in range(8):
                            k.mm(ps[bank][:, :tn], wu[:, kk, hh * 128:(hh + 1) * 128], self.hT[:, kk, t0:t0 + tn], kk == 0, kk == 7,
                                 ['wu', 'hT.%d' % b], ['ps%d' % bank])
                        k.act(uT[:, hh, :, t0 // 8:(t0 + tn) // 8].rearrange("p i c -> p c i"),
                              ps[bank][:, :tn].rearrange("p (c i) -> p c i", i=8), AF.Identity, ['ps%d' % bank, 'bu'], ['uT'],
                              bias=bu[:, hh:hh + 1])
                nb = 0
                for a in range(8):
                    hh, a4 = a // 4, a % 4
                    for ri, W_ in enumerate((WinR, WinI)):
                        bank = 2 + nb % 4
                        nb += 1
                        for i in range(8):
                            if a4 < 3:
                                k.mm(ps[bank][:, 0:NC_], W_[32 * a4:32 * a4 + 32, hh, i, :], uT[32 * a4:32 * a4 + 32, hh, i, :], i == 0, i == 7,
                                     ['Win', 'uT'], ['ps%d' % bank])
                            else:
                                W3 = Win3R if ri == 0 else Win3I
                                k.mm(ps[bank][:, 0:NC_], W3[:, hh, i, :], uT[:, hh, i, :], i == 0, i == 7, ['Win', 'uT'], ['ps%d' % bank])
                        k.cp('act' if nb % 2 == 0 else 'dve', Z[:, a, ri, :], ps[bank][:, 0:NC_], ['ps%d' % bank], ['Z'])
                S22 = [128, 8, 2]
                zc = lambda ri: Z[:, :, ri, 256:NC_:4]
                k.tt('dve', hm[:, :, 0, :], h0[:, :, 0, :], bc(MUr[:, 0, :], S22), ALU.mult, ['h0', 'MU'], ['hm'])
                k.tt('dve', hm[:, :, 1, :], h0[:, :, 1, :], bc(MUi[:, 0, :], S22), ALU.mult, ['h0', 'MU'], ['hm'])
                k.tt('dve', hm2[:, :, 0, :], h0[:, :, 1, :], bc(MUr[:, 0, :], S22), ALU.mult, ['h0', 'MU'], ['hm'])
                k.tt('dve', hm2[:, :, 1, :], h0[:, :, 0, :], bc(MUi[:, 0, :], S22), ALU.mult, ['h0', 'MU'], ['hm'])
                k.tt('dve', zc(0), zc(0), hm[:, :, 0, :], ALU.add, ['Z', 'hm'], ['Z'])
                k.tt('dve', zc(0), zc(0), hm[:, :, 1, :], ALU.subtract, ['Z', 'hm'], ['Z'])
                k.tt('dve', zc(1), zc(1), hm2[:, :, 0, :], ALU.add, ['Z', 'hm'], ['Z'])
                k.tt('dve', zc(1), zc(1), hm2[:, :, 1, :], ALU.add, ['Z', 'hm'], ['Z'])
                X_, Y_ = Z, Z2
                xk, yk = 'Z', 'Z2'
                allk = lambda nm: ['%s.%d.%d' % (nm, a, ri) for a in range(8) for ri in range(2)]
                k.cp('pool', Z2[:, :, :, 256:NC_], Z[:, :, :, 256:NC_], ['Z'], allk('Z') + allk('Z2'))
                for kk in range(8):
                    dd = 1 << kk
                    k.cp('pool', Y_[:, :, :, 0:dd], X_[:, :, :, 0:dd], allk(xk), allk(yk))
                    mu = lambda a: (MUr[:, kk, a:a + 1], MUi[:, kk, a:a + 1], nMUi[:, kk, a:a + 1])
                    K_ = lambda nm, a, ri: '%s.%d.%d' % (nm, a, ri)
                    for a in range(8):
                        k.stt(Y_[:, a, 0, dd:256], X_[:, a, 0, 0:256 - dd], mu(a)[0], X_[:, a, 0, dd:256], ALU.mult, ALU.add,
                              [K_(xk, a, 0), 'MU'], [K_(yk, a, 0)])
                    for a in range(8):
                        k.stt(Y_[:, a, 1, dd:256], X_[:, a, 1, 0:256 - dd], mu(a)[0], X_[:, a, 1, dd:256], ALU.mult, ALU.add,
                              [K_(xk, a, 1), 'MU'], [K_(yk, a, 1)])
                    for a in range(8):
                        k.stt(Y_[:, a, 0, dd:256], X_[:, a, 1, 0:256 - dd], mu(a)[2], Y_[:, a, 0, dd:256], ALU.mult, ALU.add,
                              [K_(xk, a, 1), 'MU', K_(yk, a, 0)], [K_(yk, a, 0)])
                    for a in range(8):
                        k.stt(Y_[:, a, 1, dd:256], X_[:, a, 0, 0:256 - dd], mu(a)[1], Y_[:, a, 1, dd:256], ALU.mult, ALU.add,
                              [K_(xk, a, 0), 'MU', K_(yk, a, 1)], [K_(yk, a, 1)])
                    if kk < 2:
                        def sv(T_, a, ri, lo, hi):
                            return T_[:, a, ri, 256:NC_].rearrange("p (j c) -> p j c", j=2)[:, :, lo:hi]
                        for a in range(8):
                            k.stt(sv(Y_, a, 0, dd, 4), sv(X_, a, 0, 0, 4 - dd), mu(a)[0], sv(X_, a, 0, dd, 4), ALU.mult, ALU.add,
                                  [K_(xk, a, 0), 'MU'], [K_(yk, a, 0)])
                        for a in range(8):
                            k.stt(sv(Y_, a, 1, dd, 4), sv(X_, a, 1, 0, 4 - dd), mu(a)[0], sv(X_, a, 1, dd, 4), ALU.mult, ALU.add,
                                  [K_(xk, a, 1), 'MU'], [K_(yk, a, 1)])
                        for a in range(8):
                            k.stt(sv(Y_, a, 0, dd, 4), sv(X_, a, 1, 0, 4 - dd), mu(a)[2], sv(Y_, a, 0, dd, 4), ALU.mult, ALU.add,
                                  [K_(xk, a, 1), 'MU', K_(yk, a, 0)], [K_(yk, a, 0)])
                        for a in range(8):
                            k.stt(sv(Y_, a, 1, dd, 4), sv(X_, a, 0, 0, 4 - dd), mu(a)[1], sv(Y_, a, 1, dd, 4), ALU.mult, ALU.add,
                                  [K_(xk, a, 0), 'MU', K_(yk, a, 1)], [K_(yk, a, 1)])
                        k.cp('pool', Y_[:, :, :, 256:NC_].rearrange("p a r (j c) -> p a r j c", j=2)[:, :, :, :, 0:dd],
                             X_[:, :, :, 256:NC_].rearrange("p a r (j c) -> p a r j c", j=2)[:, :, :, :, 0:dd], allk(xk), allk(yk))
                    X_, Y_ = Y_, X_
                    xk, yk = yk, xk
                k.cp('pool', hm[:, 0, 0, 0:1], hm[:, 0, 0, 0:1], allk('Z') + allk('Z2'), ['Z'])
                assert X_ is Z
                k.memset('pool', Sprev[:, :, :, 0:1], 0.0, ['Sprev'])
                k.cp('pool', Sprev[:, :, :, 1:256], Z[:, :, :, 0:255], ['Z'], ['Sprev'])
                for j in range(2):
                    c0 = 256 + 4 * j
                    k.cp('pool', Sprev[:, :, :, c0], h0[:, :, :, j], ['h0'], ['Sprev'])
                    k.cp('pool', Sprev[:, :, :, c0 + 1:c0 + 4], Z[:, :, :, c0:c0 + 3], ['Z'], ['Sprev'])
                for q, c in enumerate((255, 259, 263)):
                    k.cp('dve', fin[:, :, :, q], Z[:, :, :, c], ['Z'], ['fin'])
                with self.nc.allow_non_contiguous_dma(reason="ssm state out"):
                    for ri, (pn, sn) in enumerate((('p_ssm_re', 's_ssm_re'), ('p_ssm_im', 's_ssm_im'))):
                        k.dma('sp', d[pn][l].rearrange("(a g) p -> (g p) a", g=2), fin[:, :, ri, 0], reads=['fin'])
                        for j in range(2):
                            k.dma('sp', d[sn][l, j].rearrange("(a g) p -> (g p) a", g=2), fin[:, :, ri, 1 + j], reads=['fin'])
                k.barrier()
            self.chk('s5_scan')
            with contextlib.ExitStack() as po:
                ysT = k.sb(po, [128, 2, NT], F32, 'ysT')
                gb = k.sb(po, [128, 2, NT], BF16, 'gb')
                ycb = k.sb(po, [128, 2, NT], BF16, 'ycb')
                tg = k.sb(po, [128, 2, 512], F32, 'tg')
                sgl = k.sb(po, [128, 2, 512], F32, 'sgl')
                sq = k.sb(po, [128, 2, 512], BF16, 'sqc')
                rs = k.sb(po, [128, 2, 512], F32, 'rsc')
                wgl = k.sb(po, [128, 2, 256], BF16, 'wgl')
                sv_ = k.sb(po, [128, 8], F32, 'sv')
                k.dma('pool', wgl[:, :, :], d['w_glu'][l].rearrange("(c p) n -> p c n", p=128), writes=['wgl'])
                with self.nc.allow_non_contiguous_dma(reason="small"):
                    for q, nm in enumerate(('ssm_d', 'b_glu', 'gn_c_g')):
                        k.dma('sp', sv_[:, 2 * q:2 * q + 2], d[nm][l].rearrange("(h p) -> p h", p=128), writes=['sv'])
                nb = 0
                for hh in range(2):
                    for j in range(8):
                        bank = nb % 4
                        nb += 1
                        for i in range(j + 1):
                            k.mm(ps[bank][:, 0:NC_], Kbd[:, hh, j - i, :], uT[:, hh, i, :], i == 0, False, ['Kbd', 'uT'], ['ps%d' % bank])
                        for a4 in range(4):
                            for ri in range(2):
                                if a4 < 3:
                                    k.mm(ps[bank][32 * a4:32 * a4 + 32, 0:NC_], Wout[:, hh, ri, j, 32 * a4:32 * a4 + 32], Sprev[:, 4 * hh + a4, ri, :],
                                         False, False, ['Wout', 'Sprev'], ['ps%d' % bank])
                                else:
                                    k.mm(ps[bank][64:128, 0:NC_], Wout3[:, hh, ri, j, :], Sprev[:, 4 * hh + a4, ri, :],
                                         False, ri == 1, ['Wout', 'Sprev'], ['ps%d' % bank])
                        k.stt(ysT[:, hh, :].rearrange("p (c j) -> p c j", j=8)[:, :, j], uT[:, hh, j, :], sv_[:, hh:hh + 1], ps[bank][:, 0:NC_],
                              ALU.mult, ALU.add, ['uT', 'sv', 'ps%d' % bank], ['ysT'])
                self.chk('s5_y')
                n2 = 0
                for hh in range(2):
                    for b, (t0, tn) in enumerate(TB):
                        i2 = n2 % 2
                        n2 += 1
                        y_ = ysT[:, hh, t0:t0 + tn]
                        k.tt('pool', tg[:, i2, :tn], y_, y_, ALU.mult, ['ysT'], ['tg%d' % i2])
                        k.ts('dve', tg[:, i2, :tn], tg[:, i2, :tn], 0.044715, 1.0, ALU.mult, ALU.add, ['tg%d' % i2], ['tg%d' % i2])
                        k.tt('dve', tg[:, i2, :tn], tg[:, i2, :tn], y_, ALU.mult, ['tg%d' % i2, 'ysT'], ['tg%d' % i2])
                        k.act(tg[:, i2, :tn], tg[:, i2, :tn], AF.Sigmoid, ['tg%d' % i2], ['tg%d' % i2], scale=1.5957691216057308)
                        k.tt('dve', gb[:, hh, t0:t0 + tn], tg[:, i2, :tn], y_, ALU.mult, ['tg%d' % i2, 'ysT'], ['gb'])
                for n in range(2):
                    for b, (t0, tn) in enumerate(TB):
                        bank = 4 + n2 % 2
                        i2 = n2 % 2
                        n2 += 1
                        for c in range(2):
                            k.mm(ps[bank][:, :tn], wgl[:, c, n * 128:(n + 1) * 128], gb[:, c, t0:t0 + tn], c == 0, c == 1, ['wgl', 'gb'], ['ps%d' % bank])
                        k.act(sgl[:, i2, :tn], ps[bank][:, :tn], AF.Sigmoid, ['ps%d' % bank, 'sv'], ['sgl%d' % i2], bias=sv_[:, 2 + n:3 + n])
                        k.tt('dve', ysT[:, n, t0:t0 + tn], gb[:, n, t0:t0 + tn], sgl[:, i2, :tn], ALU.mult, ['gb', 'sgl%d' % i2, 'ysT'], ['ysT'])
                for b, (t0, tn) in enumerate(TB):
                    bank = 6 + b % 2
                    for n in range(2):
                        k.act(sq[:, n, :tn], ysT[:, n, t0:t0 + tn], AF.Square, ['ysT'], ['sqc%d' % n])
                        k.mm(ps[bank][:, :tn], self.ones_bf[:, :], sq[:, n, :tn], n == 0, n == 1, ['sqc%d' % n, 'ones'], ['ps%d' % bank])
                    r = rs[:, b % 2, :tn]
                    rk = 'rsc%d' % (b % 2)
                    k.act(r, ps[bank][:, :tn], AF.Ln, ['ps%d' % bank], [rk], scale=1.0 / 256, bias=EPS)
                    k.act(r, r, AF.Exp, [rk], [rk], scale=-0.5)
                    for n in range(2):
                        k.stt(ycb[:, n, t0:t0 + tn], ysT[:, n, t0:t0 + tn], sv_[:, 4 + n:5 + n], r, ALU.mult, ALU.mult, ['ysT', 'sv', rk], ['ycb'])
                self.chk('s5_glu')
                self.add_proj('w_out', l, 768, 2, ycb, 'ycb', po)
                k.barrier()

    def finish(self, raw=False):
        k = self.k
        d = k.dram
        with contextlib.ExitStack() as ph:
            yst = k.sb(ph, [128, 3, D], F32, 'yst')
            if raw:
                src = self.xT
                skey = lambda b: ['xT.%d' % b]
            else:
                src = None
            if not raw:
                xn = k.sb(ph, [128, 8, 512], F32, 'xn')
            blocks = [(128 * i, 128, 0) for i in range(16)] + [(NT - 128, 128, 64)]
            for bi, (t0, tn, r0) in enumerate(blocks):
                b = bkey(t0 + r0)
                sl = bi % 3
                if not raw and ((t0 + r0) % 512 == 0):
                    self._norm_block(b, xn, ph)
                for half in range(2):
                    bank = 2 * sl + half
                    for c in range(4):
                        kk = half * 4 + c
                        if raw:
                            inp = self.xT[:, kk, t0:t0 + tn]
                            rk = ['xT.3', 'xT.4'] if r0 else ['xT.%d' % b]
                        else:
                            if r0:
                                inp = xn[:, kk, 0:128]
                            else:
                                o = t0 - TB[b][0]
                                inp = xn[:, kk, o:o + tn]
                            rk = ['xn']
                        k.tr(self.ps[bank][:tn, c * 128:(c + 1) * 128], inp, self.cid(), rk + ['cst'], ['ps%d' % bank])
                    k.cp('act' if half == 0 else 'dve', yst[:tn, sl, half * 512:(half + 1) * 512], self.ps[bank][:tn, :],
                         ['ps%d' % bank], ['yst%d' % sl])
                if r0 == 0:
                    k.dma('sp', d['y_p'][t0:t0 + tn, :], yst[:tn, sl, :], reads=['yst%d' % sl])
                elif raw:
                    k.dma('sp', d['y_s'][:, :], yst[64:128, sl, :], reads=['yst%d' % sl])
                else:
                    k.dma('sp', d['y_s'][:, :], yst[0:64, sl, :], reads=['yst%d' % sl])
            k.barrier()

    def _norm_block(self, b, xn, ph):
        k = self.k
        t0, tn = TB[b]
        if not hasattr(self, '_nb'):
            self._nb = (k.sb(ph, [128, 2, 512], BF16, 'sqf'), k.sb(ph, [128, 512], F32, 'rsf'))
        sq, rs = self._nb
        bank = 6
        for kk in range(8):
            sl = kk % 2
            k.act(sq[:, sl, :tn], self.xT[:, kk, t0:t0 + tn], AF.Square, ['xT.%d' % b], ['sqf%d' % sl])
            k.mm(self.ps[bank][:, :tn], self.ones_bf[:, :], sq[:, sl, :tn], kk == 0, kk == 7, ['sqf%d' % sl, 'ones'], ['ps6'])
        k.act(rs[:, :tn], self.ps[bank][:, :tn], AF.Ln, ['ps6'], ['rsf'], scale=1.0 / D, bias=EPS)
        k.act(rs[:, :tn], rs[:, :tn], AF.Exp, ['rsf'], ['rsf'], scale=-0.5)
        for kk in range(8):
            k.stt(xn[:, kk, :tn], self.xT[:, kk, t0:t0 + tn], self.gains[:, 6, kk:kk + 1], rs[:, :tn], ALU.mult, ALU.mult,
                  ['xT.%d' % b, 'rsf', 'gains'], ['xn'])

    def build(self):
        k = self.k
        self.setup()
        self.layers()
        k.muted = False
        self.finish(raw=self.flag('raw'))
        k.barrier()
        k.es.close()

    def layers(self):
        k = self.k
        for l in range(L):
            if self.phases is not None and l > 0 and not self.on('l1'):
                break
            if self.on('mixnorm') and not self.on('mlstm'):
                with contextlib.ExitStack() as ph:
                    self.rmsnorm_fm(0 + l, ph)
                    k.barrier()
            if self.on('mlstm'):
                self.phase_mlstm(l)
            if self.on('sb'):
                self.phase_sb(l)
            if self.on('s5'):
                self.phase_s5(l)
            if self.on('cross'):
                self.phase_cross(l)
            if self.on('ffn'):
                self.phase_ffn(l)


def build_program(phases=None, dbg=None):
    nc = bass.Bass("TRN2", target_bir_lowering=False)
    p = Prog(nc, phases, dbg)
    p.build()
    return nc, p


WEIGHT_NAMES = ['ln_mix_g', 'w_in', 'b_in', 'gn_a_g', 'gn_b_g', 'gn_c_g', 'ssm_a_re', 'ssm_a_im', 'ssm_log_dt',
                'ssm_b_re', 'ssm_b_im', 'ssm_c_re', 'ssm_c_im', 'ssm_d', 'w_glu', 'b_glu', 'w_out', 'ln_x_g', 'ln_mem_g',
                'w_xq', 'w_xk', 'w_xv', 'w_xo', 'ln_ffn_g', 'w_ffn_a', 'w_ffn_b', 'ffn_conv_w', 'ffn_conv_b', 'w_ffn_down',
                'ln_f_g']


def make_in_maps(inp, cores):
    f = lambda a: np.ascontiguousarray(np.asarray(a, dtype=np.float32))
    cst = make_consts()
    shared = {n: f(inp[n]) for n in WEIGHT_NAMES}
    maps = []
    for c in cores:
        s2 = slice(2 * c, 2 * c + 2)
        m = dict(shared)
        m['cst'] = cst
        m['xp'] = f(inp['x_prompt'][c])
        m['xs'] = f(np.asarray(inp['x_sample'])[s2].reshape(2 * TSQ, D))
        m['csbk'] = f(np.asarray(inp['cache_sb_k'])[:, s2].reshape(L, 2, 1024, 384))
        m['csbv'] = f(np.asarray(inp['cache_sb_v'])[:, s2].reshape(L, 2, 1024, 384))
        m['smc'] = f(np.asarray(inp['state_mlstm_c'])[:, s2])
        m['smn'] = f(np.asarray(inp['state_mlstm_n'])[:, s2])
        m['smm'] = f(np.asarray(inp['state_mlstm_m'])[:, s2])
        m['ssr'] = f(np.asarray(inp['state_ssm_re'])[:, s2])
        m['ssi'] = f(np.asarray(inp['state_ssm_im'])[:, s2])
        m['sfc'] = f(np.asarray(inp['state_ffn_conv'])[:, s2])
        m['cmk'] = f(np.asarray(inp['cache_mem_k'])[:, s2].reshape(L, 2, 256, D))
        m['cmv'] = f(np.asarray(inp['cache_mem_v'])[:, s2].reshape(L, 2, 256, D))
        m['memp'] = f(inp['mem_prompt'][c])
        maps.append(m)
    return maps


def assemble(results):
    n = len(results)
    g = lambda name: [np.asarray(r[name]) for r in results]
    st1 = lambda name: np.stack(g(name), axis=1)
    cat1 = lambda name: np.concatenate(g(name), axis=1)
    y_prompt = np.stack(g('y_p'), 0)
    y_sample = np.concatenate([a.reshape(2, TSQ, D) for a in g('y_s')], 0)
    p_sb_k = st1('p_sb_k').reshape(L, n, TP, 6, 64)
    p_sb_v = st1('p_sb_v').reshape(L, n, TP, 6, 64)
    p_mem_k = st1('p_mem_k').reshape(L, n, 256, 4, 256)
    p_mem_v = st1('p_mem_v').reshape(L, n, 256, 4, 256)
    s_sb_k = np.concatenate([a.reshape(L, 2, TSQ, 6, 64) for a in g('s_sb_k')], 1)
    s_sb_v = np.concatenate([a.reshape(L, 2, TSQ, 6, 64) for a in g('s_sb_v')], 1)
    outs = (y_prompt, y_sample, p_sb_k, p_sb_v, st1('p_ml_c'), st1('p_ml_n'), st1('p_ml_m'),
            st1('p_ssm_re'), st1('p_ssm_im'), st1('p_ffn_conv'), p_mem_k, p_mem_v,
            s_sb_k, s_sb_v, cat1('s_ml_c'), cat1('s_ml_n'), cat1('s_ml_m'), cat1('s_ssm_re'), cat1('s_ssm_im'),
            cat1('s_ffn_conv'))
    return tuple(np.ascontiguousarray(o.astype(np.float32)) for o in outs)


def kernel(**inputs):
    nc, _ = build_program()
    maps = make_in_maps(inputs, list(range(NCORES)))
    res = run_bass_kernel_spmd(nc, maps, core_ids=list(range(NCORES)))
    return assemble(res.results)
```
